# Optimizing a Trainium2 kernel written in Bass

```python
import jax
import jax.numpy as jnp
from jax import lax
import numpy as np

D_MODEL = 1024
BATCH = 1
SEQ = 16384
DEPTH = 4

CHUNK = 64
N_META = 16
META_PAD = CHUNK - N_META
D_CONV = 512
CONV_WIDTH = 31
HG_HEADS = 4
HG_DK = 128
HG_DV = 128
D_HG_K = HG_HEADS * HG_DK
D_HG = HG_HEADS * HG_DV
F_FLOOR = 1e-30
ATT_Q_HEADS = 8
ATT_KV_HEADS = 2
ATT_HEAD_DIM = 64
ATT_GROUP = ATT_Q_HEADS // ATT_KV_HEADS
D_ATT = ATT_Q_HEADS * ATT_HEAD_DIM
D_KV = ATT_KV_HEADS * ATT_HEAD_DIM
WINDOW = 128
WINDOW_CHUNKS = WINDOW // CHUNK
N_BRANCH = 3
EPS = 1e-6
IN_SIZES = (2 * D_CONV, D_CONV, D_HG_K, D_HG_K, D_HG, D_HG, D_ATT, D_KV, D_KV, D_ATT, N_BRANCH * D_MODEL)
D_IN = 2 * D_CONV + D_CONV + 2 * D_HG_K + 2 * D_HG + 2 * D_ATT + 2 * D_KV + N_BRANCH * D_MODEL

kernel_name = "hybrid_conv_hgrn2_swa_meta_trunk"


def _rmsnorm(x, g):
    xf = x.astype(jnp.float32)
    return xf * lax.rsqrt(jnp.mean(xf * xf, axis=-1, keepdims=True) + EPS) * g.astype(jnp.float32)


def _conv_branch(a_in, a_gate, valid, conv_w, conv_b, ln_g, ln_b, w_o):
    a = a_in.astype(jnp.float32)
    u = a[..., :D_CONV] * jax.nn.sigmoid(a[..., D_CONV:])
    u = jnp.where(valid[None, :, None], u, 0.0)
    y = lax.conv_general_dilated(
        u, conv_w.astype(jnp.float32)[:, None, :], window_strides=(1,),
        padding=[(CONV_WIDTH - 1, 0)], dimension_numbers=("NWC", "WIO", "NWC"),
        feature_group_count=D_CONV)
    y = y + conv_b.astype(jnp.float32)
    mu = jnp.mean(y, axis=-1, keepdims=True)
    var = jnp.mean(jnp.square(y - mu), axis=-1, keepdims=True)
    y = (y - mu) * lax.rsqrt(var + EPS) * ln_g.astype(jnp.float32) + ln_b.astype(jnp.float32)
    y = jax.nn.silu(y) * jax.nn.silu(a_gate.astype(jnp.float32))
    return y.astype(w_o.dtype) @ w_o


def _hgrn2_branch(q, fz, i, gate, valid, lb, gn_g, w_o):
    B, L, _ = q.shape
    n_chunks = L // CHUNK
    qh = jax.nn.silu(q.astype(jnp.float32)).reshape(B, L, HG_HEADS, HG_DK)
    z = fz.astype(jnp.float32).reshape(B, L, HG_HEADS, HG_DK)
    lbh = lb.astype(jnp.float32).reshape(HG_HEADS, HG_DK)
    f = lbh + (1.0 - lbh) * jax.nn.sigmoid(z)
    logf = jnp.log(jnp.maximum(f, F_FLOOR))
    kh = (1.0 - lbh) * jax.nn.sigmoid(-z)
    vmask = valid[None, :, None, None]
    logf = jnp.where(vmask, logf, 0.0)
    kh = jnp.where(vmask, kh, 0.0)
    vh = i.astype(jnp.float32).reshape(B, L, HG_HEADS, HG_DV)

    def to_chunks(t):
        return t.reshape(B, n_chunks, CHUNK, HG_HEADS, -1).transpose(1, 0, 3, 2, 4)

    tri = jnp.tril(jnp.ones((CHUNK, CHUNK), dtype=bool))[:, :, None]

    def step(S, inp):
        qc, kc, vc, gc = inp
        b = jnp.cumsum(gc, axis=2)
        b_last = b[:, :, -1:, :]
        rel = b[:, :, :, None, :] - b[:, :, None, :, :]
        decay = jnp.exp(jnp.where(tri, rel, -jnp.inf))
        scores = jnp.einsum("bhtd,bhtsd,bhsd->bhts", qc, decay, kc)
        o = (jnp.einsum("bhts,bhsv->bhtv", scores, vc)
             + jnp.einsum("bhtd,bhdv->bhtv", qc * jnp.exp(b), S))
        S = (jnp.exp(b_last)[:, :, 0, :, None] * S
             + jnp.einsum("bhsd,bhsv->bhdv", kc * jnp.exp(b_last - b), vc))
        return S, o

    S0 = jnp.zeros((B, HG_HEADS, HG_DK, HG_DV), jnp.float32)
    _, o = lax.scan(step, S0, (to_chunks(qh), to_chunks(kh), to_chunks(vh), to_chunks(logf)))
    o = o.transpose(1, 0, 3, 2, 4).reshape(B, L, HG_HEADS, HG_DV)
    o = _rmsnorm(o, gn_g).reshape(B, L, D_HG) * jax.nn.silu(gate.astype(jnp.float32))
    return o.astype(w_o.dtype) @ w_o


def _swa_key_mask(n_chunks):
    c = jnp.arange(n_chunks)[:, None]
    j = jnp.arange(CHUNK)[None, :]
    meta_ok = (c > WINDOW_CHUNKS) & (j >= META_PAD)
    band = [((c - WINDOW_CHUNKS + r) * CHUNK + j) >= META_PAD for r in range(WINDOW_CHUNKS + 1)]
    return jnp.concatenate([meta_ok] + band, axis=1)


def _band(t, n_chunks):
    tp = jnp.pad(t, ((0, 0), (WINDOW_CHUNKS, 0), (0, 0), (0, 0), (0, 0)))
    meta = jnp.broadcast_to(t[:, :1], t.shape)
    return jnp.concatenate([meta] + [tp[:, r:r + n_chunks] for r in range(WINDOW_CHUNKS + 1)], axis=2)


def _swa_branch(q, k, v, gate, qn_g, kn_g, sinks, w_o, key_mask):
    B, L, _ = q.shape
    n_chunks = L // CHUNK
    qh = _rmsnorm(q.reshape(B, L, ATT_Q_HEADS, ATT_HEAD_DIM), qn_g).reshape(
        B, n_chunks, CHUNK, ATT_KV_HEADS, ATT_GROUP, ATT_HEAD_DIM)
    kh = _rmsnorm(k.reshape(B, L, ATT_KV_HEADS, ATT_HEAD_DIM), kn_g).reshape(
        B, n_chunks, CHUNK, ATT_KV_HEADS, ATT_HEAD_DIM)
    vh = v.astype(jnp.float32).reshape(B, n_chunks, CHUNK, ATT_KV_HEADS, ATT_HEAD_DIM)
    kb = _band(kh, n_chunks)
    vb = _band(vh, n_chunks)
    s = jnp.einsum("bnqhgd,bnkhd->bnhgqk", qh, kb) * (ATT_HEAD_DIM ** -0.5)
    s = jnp.where(key_mask[None, :, None, None, None, :], s, -jnp.inf)
    sink = sinks.astype(jnp.float32).reshape(ATT_KV_HEADS, ATT_GROUP)[None, None, :, :, None, None]
    m = jnp.maximum(jnp.max(s, axis=-1, keepdims=True), sink)
    p = jnp.exp(s - m)
    denom = jnp.sum(p, axis=-1, keepdims=True) + jnp.exp(sink - m)
    o = jnp.einsum("bnhgqk,bnkhd->bnqhgd", p / denom, vb).reshape(B, L, D_ATT)
    o = o * jax.nn.silu(gate.astype(jnp.float32))
    return o.astype(w_o.dtype) @ w_o


def setup_inputs(seed: int = 0) -> dict:
    key = jax.random.key(seed)
    ks = jax.random.split(key, 20)
    f32 = jnp.float32
    nrm = lambda k, shape, scale: jax.random.normal(k, shape, f32) * scale
    return {
        "x": nrm(ks[0], (BATCH, SEQ, D_MODEL), 1.0),
        "meta_tokens": nrm(ks[1], (N_META, D_MODEL), 1.0),
        "norm_g": 1.0 + nrm(ks[2], (DEPTH, D_MODEL), 0.05),
        "w_in": nrm(ks[3], (DEPTH, D_MODEL, D_IN), D_MODEL ** -0.5),
        "conv_w": nrm(ks[4], (DEPTH, CONV_WIDTH, D_CONV), CONV_WIDTH ** -0.5),
        "conv_b": nrm(ks[5], (DEPTH, D_CONV), 0.02),
        "conv_ln_g": 1.0 + nrm(ks[6], (DEPTH, D_CONV), 0.05),
        "conv_ln_b": nrm(ks[7], (DEPTH, D_CONV), 0.02),
        "w_conv_out": nrm(ks[8], (DEPTH, D_CONV, D_MODEL), D_CONV ** -0.5),
        "hg_lower_bounds": nrm(ks[9], (DEPTH, D_HG_K), 0.1),
        "hg_norm_g": 1.0 + nrm(ks[10], (DEPTH, HG_DV), 0.05),
        "w_hg_out": nrm(ks[11], (DEPTH, D_HG, D_MODEL), D_HG ** -0.5),
        "q_norm_g": 1.0 + nrm(ks[12], (DEPTH, ATT_HEAD_DIM), 0.05),
        "k_norm_g": 1.0 + nrm(ks[13], (DEPTH, ATT_HEAD_DIM), 0.05),
        "attn_sinks": nrm(ks[14], (DEPTH, ATT_Q_HEADS), 0.5),
        "w_att_out": nrm(ks[15], (DEPTH, D_ATT, D_MODEL), D_ATT ** -0.5),
        "w_out": nrm(ks[16], (DEPTH, D_MODEL, D_MODEL), D_MODEL ** -0.5),
    }


def reference(x, meta_tokens, norm_g, w_in, conv_w, conv_b, conv_ln_g, conv_ln_b, w_conv_out,
              hg_lower_bounds, hg_norm_g, w_hg_out, q_norm_g, k_norm_g, attn_sinks, w_att_out, w_out):
    B = x.shape[0]
    dt = x.dtype
    h = jnp.concatenate([
        jnp.zeros((B, META_PAD, D_MODEL), dt),
        jnp.broadcast_to(meta_tokens.astype(dt)[None], (B, N_META, D_MODEL)),
        x], axis=1)
    L = h.shape[1]
    n_chunks = L // CHUNK
    valid = jnp.arange(L) >= META_PAD
    key_mask = _swa_key_mask(n_chunks)
    lb_sm = jax.nn.softmax(hg_lower_bounds.astype(jnp.float32), axis=0)
    lb_all = jnp.clip(jnp.cumsum(lb_sm, axis=0) - lb_sm[0:1], 0.0, 1.0)
    split_idx = [int(v) for v in np.cumsum(IN_SIZES)[:-1]]

    for l in range(DEPTH):
        hn = _rmsnorm(h, norm_g[l]).astype(dt)
        u = hn @ w_in[l]
        (a_in, a_gate, b_q, b_f, b_i, b_gate, c_q, c_k, c_v, c_gate, g_logits) = jnp.split(u, split_idx, axis=-1)
        z_a = _conv_branch(a_in, a_gate, valid, conv_w[l], conv_b[l], conv_ln_g[l], conv_ln_b[l], w_conv_out[l])
        z_b = _hgrn2_branch(b_q, b_f, b_i, b_gate, valid, lb_all[l], hg_norm_g[l], w_hg_out[l])
        z_c = _swa_branch(c_q, c_k, c_v, c_gate, q_norm_g[l], k_norm_g[l], attn_sinks[l], w_att_out[l], key_mask)
        g = jax.nn.sigmoid(g_logits.astype(jnp.float32))
        mixed = (g[..., :D_MODEL] * z_a.astype(jnp.float32)
                 + g[..., D_MODEL:2 * D_MODEL] * z_b.astype(jnp.float32)
                 + g[..., 2 * D_MODEL:] * z_c.astype(jnp.float32))
        h = h + mixed.astype(dt) @ w_out[l]

    return h[:, CHUNK:]
```

```python
import contextlib
import numpy as np
import concourse.bass as bass
import concourse.mybir as mybir
from concourse.bass_utils import run_bass_kernel_spmd

F32 = mybir.dt.float32
BF16 = mybir.dt.bfloat16
AF = mybir.ActivationFunctionType
ALU = mybir.AluOpType
AX = mybir.AxisListType

NCORES = 8
D = 1024
KC = 8
SEQ = 16384
OWN = SEQ // NCORES
NT = 1 + OWN // 128
TOK = NT * 128
TG = 2
TMAX = TG * 128
DEPTH = 4
EPS = 1e-6
FL = 1e-30
NCH = 21
CW = 4096
DBG = False
ENGS = ("pe", "act", "dve", "pool", "sp")

P_NG, P_LBRAW, P_LSEL, P_KG, P_QG, P_GNG, P_CB, P_LNG, P_LNB, P_CW = 0, 8, 24, 28, 29, 30, 31, 35, 39, 43
NPRM = 43 + 124
C_ID, C_ONES, C_BONES, C_TRI, C_VROW, C_VCOL = 0, 128, 256, 384, 448, 576
NCST = 577


class _Op:
    __slots__ = ("eng", "fn", "reads", "writes", "dma", "idx", "waits", "flag", "sem", "val")


class Prog:
    def __init__(self, nc):
        self.nc = nc
        self.ops = []
        self.stack = contextlib.ExitStack()
        self.n_dma_sems = {"sp": 16, "pool": 12}

    def sb(self, name, shape, dtype):
        return self.stack.enter_context(self.nc.sbuf_tensor(name, list(shape), dtype))

    def ps(self, name, shape, dtype=F32):
        return self.stack.enter_context(self.nc.psum_tensor(name, list(shape), dtype))

    def add(self, eng, fn, reads=(), writes=(), dma=False):
        op = _Op()
        op.eng, op.fn, op.reads, op.writes, op.dma = eng, fn, tuple(reads), tuple(writes), dma
        op.idx = len(self.ops)
        op.flag = dma
        self.ops.append(op)
        return op

    def dma(self, eng, out, in_, reads=(), writes=()):
        return self.add(eng, lambda e: e.dma_start(out=out, in_=in_), reads, writes, dma=True)

    def finish(self):
        nc, ops = self.nc, self.ops
        last_w, rdrs, deps_all = {}, {}, []
        for op in ops:
            deps = set()
            raw = set()
            for k in op.reads:
                w = last_w.get(k)
                if w is not None:
                    deps.add(w)
                    raw.add(w)
            for k in op.writes:
                w = last_w.get(k)
                if w is not None:
                    deps.add(w)
                r = rdrs.get(k)
                if r:
                    deps.update(r)
            deps.discard(op.idx)
            need = []
            for d in deps:
                dop = ops[d]
                if dop.eng == op.eng and not dop.dma and not op.dma:
                    if op.eng == "pe" or d not in raw:
                        continue
                need.append(d)
                dop.flag = True
            deps_all.append(need)
            for k in op.reads:
                rdrs.setdefault(k, []).append(op.idx)
            for k in op.writes:
                last_w[k] = op.idx
                rdrs[k] = []
        sem_eng = {e: self.stack.enter_context(nc.semaphore("s_" + e)) for e in ENGS if e != "sp"}
        dma_sems = {q: [self.stack.enter_context(nc.semaphore("d_%s%d" % (q, i))) for i in range(n)]
                    for q, n in self.n_dma_sems.items()}
        cnt = {e: 0 for e in ENGS}
        dma_rr = {q: 0 for q in dma_sems}
        dma_val, dma_prev = {}, {}
        for op in ops:
            if op.dma:
                q = op.eng
                i = dma_rr[q] % len(dma_sems[q])
                dma_rr[q] += 1
                v = dma_val.get((q, i), 0) + 16
                dma_val[(q, i)] = v
                op.sem, op.val = dma_sems[q][i], v
                p = dma_prev.get((q, i))
                if p is not None:
                    deps_all[op.idx].append(p)
                dma_prev[(q, i)] = op.idx
            elif op.flag:
                cnt[op.eng] += 1
                op.sem, op.val = sem_eng[op.eng], cnt[op.eng]
            else:
                op.sem = op.val = None
        seen = {e: {} for e in ENGS}
        for op in ops:
            best = {}
            for d in deps_all[op.idx]:
                dop = ops[d]
                key = id(dop.sem)
                if key not in best or best[key][1] < dop.val:
                    best[key] = (dop.sem, dop.val)
            sn = seen[op.eng]
            waits = []
            for key, (sem, val) in best.items():
                if sn.get(key, 0) >= val:
                    continue
                sn[key] = val
                waits.append((sem, val))
            op.waits = waits
        per = {e: [o for o in ops if o.eng == e] for e in ENGS}
        final = [(dma_sems[q][i], v) for (q, i), v in dma_val.items()]
        self.stats = {e: len(per[e]) for e in ENGS}
        self.stats["waits"] = sum(len(o.waits) for o in ops)

        def emit(e, name):
            for op in per[name]:
                for sem, val in op.waits:
                    e.wait_ge(sem, val)
                ins = op.fn(e)
                if op.sem is not None:
                    ins.then_inc(op.sem, 16 if op.dma else 1)
            if name == "sp":
                for sem, val in final:
                    e.wait_ge(sem, val)

        with nc.Block() as block:
            @block.tensor
            def _(e):
                emit(e, "pe")

            @block.scalar
            def _(e):
                emit(e, "act")

            @block.vector
            def _(e):
                emit(e, "dve")

            @block.gpsimd
            def _(e):
                emit(e, "pool")

            @block.sync
            def _(e):
                emit(e, "sp")
        self.stack.close()


def build(phase):
    nc = bass.Bass("TRN2", target_bir_lowering=False)
    P = Prog(nc)

    def din(name, shape, dt=F32):
        return nc.dram_tensor(name, list(shape), dt, kind="ExternalInput").ap()

    def dout(name, shape, dt=F32):
        return nc.dram_tensor(name, list(shape), dt, kind="ExternalOutput").ap()

    h_in = din("hT", [128, KC, TOK])
    nw = NCH if phase == "B" else 5
    w_in_d = din("wch", [nw, 128, CW])
    prm_d = din("prm", [128, NPRM])
    cst_d = din("cst", [128, NCST])
    cmask_d = din("cmask", [128, 16])
    wscr = nc.dram_tensor("wscr", [nw, 128, CW], BF16).ap()
    if phase == "B":
        sinks_d = din("sinks", [32, 8])
        sall_d = din("Sall", [NCORES, 128, 512])
        fall_d = din("Fall", [NCORES, 128, 4])
        uex_d = din("uex", [128, 4, 30])
        kex_d = din("kex", [128, 2, 128])
        vex_d = din("vex", [128, 128])
        h_out = dout("hT_out", [128, KC, TOK])
    else:
        s_out = dout("Sloc", [128, 512])
        f_out = dout("Floc", [128, 4])
        u_out = dout("utail", [128, 4, 30])
        k_out = dout("ktail", [128, 2, 128])
        v_out = dout("vtail", [128, 128])
    if phase == "A":
        cmap = {7: 0, 8: 1, 0: 2, 1: 3, 14: 4}
    else:
        cmap = {i: i for i in range(NCH)}

    T = TMAX
    hT = P.sb("hTs", [128, KC, TOK], F32)
    hnT = P.sb("hnT", [128, KC, T], BF16)
    sqb = P.sb("sqb", [128, KC, T], BF16)
    wr = [P.sb("wr%d" % i, [128, CW], BF16) for i in range(3)]
    prm = P.sb("prm_s", [128, NPRM], F32)
    cstf = P.sb("cstf", [128, NCST], F32)
    cstb = P.sb("cstb", [128, NCST], BF16)
    cmask = P.sb("cmask_s", [128, 16], F32)
    A = [P.sb("A%d" % i, [128, 4 * T], F32) for i in range(5)]
    st = [P.sb("st%d" % i, [128, T], F32) for i in range(4)]
    lbt = P.sb("lbt", [128, 64], F32)
    kA = P.sb("kA", [128, 4, T], BF16)
    kAtok = P.sb("kAtok", [128, TG, 512], BF16)
    vtok = P.sb("vtok", [128, TG, 512], BF16)
    S = P.sb("S", [128, 512], F32)
    Stmp = P.sb("Stmp", [128, 512], F32)
    E3 = P.sb("E3", [128, 4, 3, 2 * TG], F32)
    D3 = P.sb("D3", [128, 4, 3, 2 * TG], F32)
    Bend = P.sb("Bend", [128, 4, 2 * TG + 1], F32)
    Bmid = P.sb("Bmid", [128, 4, 2 * TG], F32)
    Ltot = P.sb("Ltot", [128, 4], F32)
    bfs = P.sb("bfs", [128, 2, T], BF16)
    if phase == "B":
        diag = P.sb("diag", [128, 124, 128], BF16)
        uT = P.sb("uT", [128, 4, 30 + T], BF16)
        yT = A[0]
        brY = [P.sb("brY%d" % i, [128, 4, T], BF16) for i in range(3)]
        qA = P.sb("qA", [128, 4, T], BF16)
        scm = P.sb("scm", [128, TG, 4, 64], BF16)
        Sbf = P.sb("Sbf", [128, 512], BF16)
        qn = P.sb("qn", [128, 4, T], BF16)
        KT = P.sb("KT", [128, 2, 128 + T], BF16)
        Vaug = P.sb("Vaug", [128, TG + 1, 2, 128], BF16)
        KM = P.sb("KM", [128, 2, 32], BF16)
        KMz = P.sb("KMz", [128, 2, 32], BF16)
        VM = P.sb("VM", [32, 8, 128], BF16)
        VM0 = P.sb("VM0", [32, 8, 128], BF16)
        VMs = P.sb("VMs", [32, 8, 128], BF16)
        sinks = P.sb("sinks_s", [32, 8], F32)
        pA = P.sb("pA", [128, 4, 128], BF16)
        pB = P.sb("pB", [128, 4, 128], BF16)
        pM = P.sb("pM", [32, 4, 128], BF16)
        rden = P.sb("rden", [128, 4, 128], F32)
        ao = P.sb("ao", [128, 2, 128], F32)
        exs = P.sb("exs", [128, 512], F32)
        Fall = P.sb("Fall_s", [128, NCORES, 4], F32)
        mixb = sqb
    else:
        utl = P.sb("utl", [128, 4, 128], F32)
    ps = [P.ps("ps%d" % i, [128, 512], F32) for i in range(7)]
    pst = P.ps("pst", [128, 512], BF16)
    pjc = [0]

    def pj():
        i = pjc[0] % 3
        pjc[0] += 1
        return ps[i], "ps%d" % i

    def ak(i, lo, hi):
        return [("A", i, u) for u in range(lo // 128, (hi + 127) // 128)]

    ident = cstb[:, C_ID:C_ID + 128]
    ones = cstb[:, C_ONES:C_ONES + 128]
    bones = cstb[:, C_BONES:C_BONES + 128]
    trim = cstb[:, C_TRI:C_TRI + 64]
    vrow = cstf[:, C_VROW:C_VROW + 128]
    vcol = cstf[:, C_VCOL:C_VCOL + 1]
    onecol = cstf[:, C_ONES:C_ONES + 1]

    def act(out, in_, func, reads, writes, scale=1.0, bias=0.0):
        P.add("act", lambda e: e.activation(out=out, in_=in_, func=func, bias=bias, scale=scale), reads, writes)

    def tt(eng, out, in0, in1, op, reads, writes):
        P.add(eng, lambda e: e.tensor_tensor(out=out, in0=in0, in1=in1, op=op), reads, writes)

    def ts(eng, out, in0, s1, s2, op0, op1, reads, writes):
        if op1 is None:
            P.add(eng, lambda e: e.tensor_scalar(out=out, in0=in0, scalar1=s1, scalar2=None, op0=op0), reads, writes)
        else:
            P.add(eng, lambda e: e.tensor_scalar(out=out, in0=in0, scalar1=s1, scalar2=s2, op0=op0, op1=op1), reads, writes)

    def stt(eng, out, in0, scalar, in1, op0, op1, reads, writes):
        P.add(eng, lambda e: e.scalar_tensor_tensor(out=out, in0=in0, scalar=scalar, in1=in1, op0=op0, op1=op1), reads, writes)

    def cp(eng, out, in_, reads, writes):
        P.add(eng, lambda e: e.tensor_copy(out=out, in_=in_), reads, writes)

    def mm(out, lhsT, rhs, start, stop, reads, writes):
        P.add("pe", lambda e: e.matmul(out, lhsT=lhsT, rhs=rhs, start=start, stop=stop), reads, writes)

    def tr(out, in_, reads, writes):
        P.add("pe", lambda e: e.transpose(out, in_, ident), reads, writes)

    def ms(eng, ap, val, writes):
        P.add(eng, lambda e: e.memset(ap, val), (), writes)

    P.dma("sp", prm[:], prm_d[:, :], writes=["prm"])
    P.dma("sp", cstf[:], cst_d[:, :], writes=["cstf"])
    P.dma("pool", cstb[:], cst_d[:, :], writes=["cstb"])
    P.dma("sp", cmask[:], cmask_d[:, :], writes=["cmask"])
    for kc in range(KC):
        P.dma("sp", hT[:, kc, :], h_in[:, kc, :], writes=[("hTl", kc)])
    if phase == "A":
        order = [7, 8, 0, 1, 14]
    else:
        order = list(range(NCH))
    for ci in order:
        P.dma("pool", wscr[cmap[ci]], w_in_d[cmap[ci]], writes=[("wscr", ci)])

    sched = []
    ring = {"next_load": 0, "next_use": 0}

    def ring_load():
        i = ring["next_load"]
        if i < len(sched):
            ci = sched[i]
            slot = i % 3
            P.dma("sp", wr[slot][:, :], wscr[cmap[ci]], reads=[("wscr", ci)], writes=[("wr", slot)])
            ring["next_load"] += 1

    def ring_get(ci):
        i = ring["next_use"]
        assert sched[i] == ci, (sched[i], ci, i)
        ring["next_use"] += 1
        slot = i % 3
        return wr[slot], ("wr", slot)

    def ring_done():
        ring_load()

    ng = prm[:, P_NG:P_NG + 8]
    lbraw = prm[:, P_LBRAW:P_LBRAW + 16].rearrange("p (h l) -> p h l", l=4)
    lsel = prm[:, P_LSEL:P_LSEL + 4]
    kg = prm[:, P_KG:P_KG + 1]
    qg = prm[:, P_QG:P_QG + 1]
    gng = prm[:, P_GNG:P_GNG + 1]
    cb = prm[:, P_CB:P_CB + 4]
    lng = prm[:, P_LNG:P_LNG + 4]
    lnb = prm[:, P_LNB:P_LNB + 4]
    cwv = prm[:, P_CW:P_CW + 124].rearrange("p (c k) -> p c k", k=31)
    mx = lbt[:, 0:4]
    ex = lbt[:, 4:20].rearrange("p (h l) -> p h l", l=4)
    sm = lbt[:, 20:24]
    lball = lbt[:, 24:40].rearrange("p (h l) -> p h l", l=4)
    lb = lbt[:, 40:44]
    oml = lbt[:, 44:48]
    flb = lbt[:, 48:52]
    negone = lbt[:, 52:53]
    P.add("dve", lambda e: e.tensor_reduce(out=mx, in_=lbraw, axis=AX.X, op=ALU.max), ["prm"], ["lbt"])
    tt("dve", ex, lbraw, mx.unsqueeze(2).to_broadcast([128, 4, 4]), ALU.subtract, ["prm", "lbt"], ["lbt"])
    act(ex, ex, AF.Exp, ["lbt"], ["lbt"])
    P.add("dve", lambda e: e.tensor_reduce(out=sm, in_=ex, axis=AX.X, op=ALU.add), ["lbt"], ["lbt"])
    P.add("dve", lambda e: e.reciprocal(out=sm, in_=sm), ["lbt"], ["lbt"])
    tt("dve", ex, ex, sm.unsqueeze(2).to_broadcast([128, 4, 4]), ALU.mult, ["lbt"], ["lbt"])
    ms("dve", lball[:, :, 0:1], 0.0, ["lbt"])
    for l in range(1, 4):
        tt("dve", lball[:, :, l:l + 1], lball[:, :, l - 1:l], ex[:, :, l:l + 1], ALU.add, ["lbt"], ["lbt"])
    ts("dve", lball, lball, 0.0, 1.0, ALU.max, ALU.min, ["lbt"], ["lbt"])
    tt("dve", lball, lball, lsel.unsqueeze(1).to_broadcast([128, 4, 4]), ALU.mult, ["lbt", "prm"], ["lbt"])
    P.add("dve", lambda e: e.tensor_reduce(out=lb, in_=lball, axis=AX.X, op=ALU.add), ["lbt"], ["lbt"])
    ts("dve", oml, lb, -1.0, 1.0, ALU.mult, ALU.add, ["lbt"], ["lbt"])
    ts("dve", flb, lb, -1.0, FL, ALU.mult, ALU.add, ["lbt"], ["lbt"])
    ms("dve", negone, -1.0, ["lbt"])

    ms("dve", S[:], 0.0, ["S"])
    ms("dve", Ltot[:], 0.0, ["Ltot"])
    ms("dve", Bend[:], 0.0, [("Bend", q_) for q_ in range(4)])

    if phase == "B":
        P.dma("sp", sinks[:], sinks_d[:, :], writes=["sinks"])
        for cg in range(4):
            for k in range(31):
                ts("pool", diag[:, cg * 31 + k, :], ident, cwv[:, cg, k:k + 1], None, ALU.mult, None,
                   ["cstb", "prm"], ["diag"])
        ms("pool", uT[:], 0.0, ["uT"] + [("uT", q_) for q_ in range(4)])
        ms("pool", KT[:], 0.0, ["KThist", ("KT", 0), ("KT", 1)])
        ms("pool", Vaug[:], 0.0, [("Vaug", q_) for q_ in range(TG + 1)])
        ms("pool", KMz[:], 0.0, ["KMz"])
        ms("dve", VMs[:], 0.0, ["VMs"])
        act(sinks[0:1, :], sinks[0:1, :], AF.Exp, ["sinks"], ["sinks"])
        cp("dve", VMs[0:1, :, 64:128], sinks[0:1, :].unsqueeze(2).to_broadcast([1, 8, 64]), ["sinks", "VMs"], ["VMs"])

    def norm(gi, tok0, T):
        hk = ("hT", gi)
        act(sqb[:, :, :T], hT[:, :, tok0:tok0 + T], AF.Square, [hk] + [("hTl", k_) for k_ in range(KC)], ["sqb"])
        for kc in range(KC):
            mm(ps[3][:, :T], ones, sqb[:, kc, :T], kc == 0, kc == KC - 1, ["cstb", "sqb"], ["ps3"])
        act(st[0][:, :T], ps[3][:, :T], AF.Ln, ["ps3"], ["st0"], scale=1.0 / D, bias=EPS)
        act(st[1][:, :T], st[0][:, :T], AF.Exp, ["st0"], ["st1"], scale=-0.5)
        for kc in range(KC):
            stt("dve", hnT[:, kc, :T], hT[:, kc, tok0:tok0 + T], ng[:, kc:kc + 1], st[1][:, :T], ALU.mult, ALU.mult,
                [hk, "prm", "st1"], ["hnT"])

    def proj(psb, pk, wbuf, wk, c0, T, width=128, ncol=512):
        wv = wbuf[:, :].rearrange("p (k c) -> p k c", c=ncol)
        for kc in range(KC):
            mm(psb[:width, :T], wv[:, kc, c0:c0 + width], hnT[:, kc, :T], kc == 0, kc == KC - 1, [wk, "hnT"], [pk])

    def hg_gates(psz, pzk, hb, T, nchunk, g0):
        sl = slice(hb * T, (hb + 1) * T)
        X1, X2, X3, X4 = A[0][:, sl], A[1][:, sl], A[2][:, sl], A[3][:, sl]
        k1, k2, k3, k4 = ak(0, hb * T, hb * T + T), ak(1, hb * T, hb * T + T), ak(2, hb * T, hb * T + T), ak(3, hb * T, hb * T + T)
        act(X1, psz[:, :T], AF.Sigmoid, [pzk], k1)
        ts("dve", X1, X1, oml[:, hb:hb + 1], flb[:, hb:hb + 1], ALU.mult, ALU.max, k1 + ["lbt"], k1)
        act(X2, X1, AF.Ln, k1 + ["lbt"], k2, bias=lb[:, hb:hb + 1])
        ts("dve", X1, X1, negone, oml[:, hb:hb + 1], ALU.mult, ALU.add, k1 + ["lbt"], k1)
        if g0:
            tt("dve", X2, X2, vrow[:, :T], ALU.mult, k2 + ["cstf"], k2)
            tt("dve", X1, X1, vrow[:, :T], ALU.mult, k1 + ["cstf"], k1)
        P.add("dve", lambda e: e.tensor_tensor_scan(out=X3, data0=X2, data1=X2, initial=0.0, op0=ALU.add, op1=ALU.bypass),
              k2, k3)
        B3 = X3.rearrange("p (c t) -> p c t", t=64)
        cp("dve", Bmid[:, hb, :nchunk].unsqueeze(2), B3[:, :, 31:32], k3, [("Bmid", hb)])
        cp("dve", Bend[:, hb, 1:1 + nchunk].unsqueeze(2), B3[:, :, 63:64], k3, [("Bend", hb)])
        tt("dve", D3[:, hb, 0, :nchunk], Bend[:, hb, 1:1 + nchunk], Bend[:, hb, 0:nchunk], ALU.subtract, [("Bend", hb)], [("D3", hb)])
        tt("dve", D3[:, hb, 1, :nchunk], Bmid[:, hb, :nchunk], Bend[:, hb, 0:nchunk], ALU.subtract, [("Bend", hb), ("Bmid", hb)], [("D3", hb)])
        tt("dve", D3[:, hb, 2, :nchunk], Bend[:, hb, 1:1 + nchunk], Bmid[:, hb, :nchunk], ALU.subtract, [("Bend", hb), ("Bmid", hb)], [("D3", hb)])
        act(E3[:, hb, :, :nchunk], D3[:, hb, :, :nchunk], AF.Exp, [("D3", hb)], [("E3", hb)])
        tt("dve", Ltot[:, hb:hb + 1], Ltot[:, hb:hb + 1], Bend[:, hb, nchunk:nchunk + 1], ALU.add, [("Bend", hb), "Ltot"], ["Ltot"])
        tt("dve", B3, B3, Bmid[:, hb, :nchunk].unsqueeze(2).to_broadcast([128, nchunk, 64]), ALU.subtract, k3 + [("Bmid", hb)], k3)
        act(X4, X3, AF.Exp, k3, k4, scale=-1.0)
        tt("dve", kA[:, hb, :T], X1, X4, ALU.mult, k1 + k4, [("kA", hb)])

    def hg_vtok(wbuf, wk, nt, T):
        wv = wbuf[:, :].rearrange("p (k c) -> p k c", c=512)
        for ti in range(nt):
            psb, pk = pj()
            for kc in range(KC):
                mm(psb[:, :], hnT[:, kc, ti * 128:(ti + 1) * 128], wv[:, kc, :], kc == 0, kc == KC - 1, [wk, "hnT"], [pk])
            act(vtok[:, ti, :], psb[:, :], AF.Copy, [pk], [("vtok", ti)])

    def hg_ktok(nt):
        for ti in range(nt):
            for hb in range(4):
                tr(pst[:, hb * 128:(hb + 1) * 128], kA[:, hb, ti * 128:(ti + 1) * 128], [("kA", hb), "cstb"], ["pst"])
            cp("dve", kAtok[:, ti, :], pst[:, :], ["pst"], [("kAtok", ti)])

    def hg_state(ti, half):
        cj = ti * 2 + half
        rows = slice(half * 64, half * 64 + 64)
        for hb in range(4):
            mm(ps[5][:, hb * 128:(hb + 1) * 128], kAtok[rows, ti, hb * 128:(hb + 1) * 128],
               vtok[rows, ti, hb * 128:(hb + 1) * 128], True, True, [("kAtok", ti), ("vtok", ti)], ["ps5"])
        for hb in range(4):
            hs = slice(hb * 128, (hb + 1) * 128)
            act(Stmp[:, hs], ps[5][:, hs], AF.Copy, ["ps5", ("E3", hb)], ["Stmp"], scale=E3[:, hb, 2, cj:cj + 1])
            stt("dve", S[:, hs], S[:, hs], E3[:, hb, 0, cj:cj + 1], Stmp[:, hs], ALU.mult, ALU.add,
                ["S", "Stmp", ("E3", hb)], ["S"])

    def knorm(psb, pk, out_ap, out_keys, gcol, T):
        act(bfs[:, 0, :T], psb[:, :T], AF.Square, [pk], ["bfs0"])
        mm(ps[3][:, :T], bones, bfs[:, 0, :T], True, True, ["cstb", "bfs0"], ["ps3"])
        act(st[2][:, :T], ps[3][:, :T], AF.Ln, ["ps3"], ["st2"], scale=1.0 / 64, bias=EPS)
        act(st[3][:, :T], st[2][:, :T], AF.Exp, ["st2"], ["st3"], scale=-0.5)
        stt("dve", out_ap, psb[:, :T], gcol, st[3][:, :T], ALU.mult, ALU.mult, [pk, "prm", "st3"], out_keys)

    if phase == "A":
        groups = [list(range(1 + g * TG, 1 + (g + 1) * TG)) for g in range((NT - 1) // TG)]
        for gi in range(len(groups)):
            sched.extend([7, 8])
        sched.extend([0, 1, 14])
        for _ in range(3):
            ring_load()
        for gi, tiles in enumerate(groups):
            nt = len(tiles)
            T = nt * 128
            tok0 = tiles[0] * 128
            nchunk = 2 * nt
            norm(gi, tok0, T)
            wf, wfk = ring_get(7)
            for hb in range(4):
                psb, pk = pj()
                proj(psb, pk, wf, wfk, hb * 128, T)
                hg_gates(psb, pk, hb, T, nchunk, False)
            ring_done()
            wi, wik = ring_get(8)
            hg_vtok(wi, wik, nt, T)
            ring_done()
            hg_ktok(nt)
            for ti in range(nt):
                for half in range(2):
                    hg_state(ti, half)
            if gi == len(groups) - 1:
                lt = (nt - 1) * 128
                wa, wak = ring_get(0)
                wb, wbk = ring_get(1)
                wva = wa[:, :].rearrange("p (k c) -> p k c", c=512)
                wvb = wb[:, :].rearrange("p (k c) -> p k c", c=512)
                for cg in range(4):
                    pa, pak = pj()
                    pb, pbk = pj()
                    for kc in range(KC):
                        mm(pa[:, :128], wva[:, kc, cg * 128:(cg + 1) * 128], hnT[:, kc, lt:lt + 128], kc == 0, kc == KC - 1, [wak, "hnT"], [pak])
                    for kc in range(KC):
                        mm(pb[:, :128], wvb[:, kc, cg * 128:(cg + 1) * 128], hnT[:, kc, lt:lt + 128], kc == 0, kc == KC - 1, [wbk, "hnT"], [pbk])
                    act(st[2][:, :128], pb[:, :128], AF.Sigmoid, [pbk], ["st2"])
                    tt("dve", utl[:, cg, :], pa[:, :128], st[2][:, :128], ALU.mult, [pak, "st2"], ["utl"])
                ring_done()
                ring_done()
                P.dma("sp", u_out[:, :, :], utl[:, :, 98:128], reads=["utl"])
                wc, wck = ring_get(14)
                wvc = wc[:, :].rearrange("p (k c) -> p k c", c=512)
                for kvh in range(2):
                    psb, pk = pj()
                    for kc in range(KC):
                        mm(psb[:, :128], wvc[:, kc, kvh * 128:(kvh + 1) * 128], hnT[:, kc, lt:lt + 128], kc == 0, kc == KC - 1, [wck, "hnT"], [pk])
                    knorm(psb, pk, A[4][:, kvh * 128:(kvh + 1) * 128], ak(4, kvh * 128, kvh * 128 + 128), kg, 128)
                    P.dma("sp", k_out[:, kvh, :], A[4][:, kvh * 128:(kvh + 1) * 128], reads=ak(4, kvh * 128, kvh * 128 + 128))
                psb, pk = pj()
                for kc in range(KC):
                    mm(psb[:, :128], hnT[:, kc, lt:lt + 128], wvc[:, kc, 256:384], kc == 0, kc == KC - 1, [wck, "hnT"], [pk])
                act(A[4][:, 256:384], psb[:, :128], AF.Copy, [pk], ak(4, 256, 384))
                P.dma("sp", v_out[:, :], A[4][:, 256:384], reads=ak(4, 256, 384))
                ring_done()
        if DBG:
            for nm, tile_, keys in [("d_lbt", lbt, ["lbt"]), ("d_Ltot", Ltot, ["Ltot"]), ("d_Bend", Bend, [("Bend", h_) for h_ in range(4)]),
                                    ("d_D3", D3, [("D3", h_) for h_ in range(4)]), ("d_E3", E3, [("E3", h_) for h_ in range(4)]),
                                    ("d_A0", A[0], ak(0, 0, 4 * TMAX)), ("d_A1", A[1], ak(1, 0, 4 * TMAX)), ("d_A2", A[2], ak(2, 0, 4 * TMAX)),
                                    ("d_A3", A[3], ak(3, 0, 4 * TMAX))]:
                shp = list(tile_.shape)
                dd = dout(nm, [shp[0], int(np.prod(shp[1:]))])
                src = tile_[:] if len(shp) == 2 else (tile_[:].rearrange("p a b -> p (a b)") if len(shp) == 3 else tile_[:].rearrange("p a b c -> p (a b c)"))
                P.dma("sp", dd[:, :], src, reads=keys)
        act(Ltot[:], Ltot[:], AF.Exp, ["Ltot"], ["Ltot"])
        P.dma("sp", f_out[:, :], Ltot[:], reads=["Ltot"])
        P.dma("sp", s_out[:, :], S[:], reads=["S"])
        P.finish()
        return nc, P

    groups = [[0]] + [list(range(1 + g * TG, 1 + (g + 1) * TG)) for g in range((NT - 1) // TG)]
    group_chunks = [0, 1, 2, 7, 6, 8, 9, 13, 14, 15] + [3, 4, 5, 10, 11, 12, 16, 17, 18] + [19, 20]
    for _ in groups:
        sched.extend(group_chunks)
    for _ in range(3):
        ring_load()

    def group(gi, tiles, g0, vm_first):
        nt = len(tiles)
        T = nt * 128
        tok0 = tiles[0] * 128
        nchunk = 2 * nt
        hk = ("hT", gi)
        norm(gi, tok0, T)
        wa, wak = ring_get(0)
        wb, wbk = ring_get(1)
        for cg in range(4):
            pa, pak = pj()
            pb, pbk = pj()
            proj(pa, pak, wa, wak, cg * 128, T)
            proj(pb, pbk, wb, wbk, cg * 128, T)
            act(st[2][:, :T], pb[:, :T], AF.Sigmoid, [pbk], ["st2"])
            if g0:
                tt("dve", st[2][:, :T], st[2][:, :T], vrow[:, :T], ALU.mult, ["st2", "cstf"], ["st2"])
            tt("dve", uT[:, cg, 30:30 + T], pa[:, :T], st[2][:, :T], ALU.mult, [pak, "st2"], [("uT", cg)])
        ring_done()
        ring_done()
        for cg in range(4):
            pc, pck = pj()
            for k in range(31):
                mm(pc[:, :T], diag[:, cg * 31 + k, :], uT[:, cg, k:k + T], k == 0, k == 30, ["diag", ("uT", cg), "uT"], [pck])
            yk = ak(0, cg * T, cg * T + T)
            act(yT[:, cg * T:(cg + 1) * T], pc[:, :T], AF.Identity, [pck, "prm"], yk, bias=cb[:, cg:cg + 1])
            act(bfs[:, 0, :T], pc[:, :T], AF.Square, [pck, "prm"], ["bfs0"], bias=cb[:, cg:cg + 1])
            cp("pool", bfs[:, 1, :T], yT[:, cg * T:(cg + 1) * T], yk, ["bfs1"])
            mm(ps[3][:, :T], ones, bfs[:, 1, :T], cg == 0, cg == 3, ["cstb", "bfs1"], ["ps3"])
            mm(ps[4][:, :T], ones, bfs[:, 0, :T], cg == 0, cg == 3, ["cstb", "bfs0"], ["ps4"])
        cp("pool", uT[:, :, 0:30], uT[:, :, T:T + 30], [("uT", c) for c in range(4)], ["uT"])
        act(st[0][:, :T], ps[3][:, :T], AF.Copy, ["ps3"], ["st0"], scale=1.0 / 512)
        tt("dve", st[1][:, :T], st[0][:, :T], st[0][:, :T], ALU.mult, ["st0"], ["st1"])
        stt("dve", st[1][:, :T], ps[4][:, :T], 1.0 / 512, st[1][:, :T], ALU.mult, ALU.subtract, ["ps4", "st1"], ["st1"])
        act(st[1][:, :T], st[1][:, :T], AF.Ln, ["st1"], ["st1"], bias=EPS)
        act(st[1][:, :T], st[1][:, :T], AF.Exp, ["st1"], ["st1"], scale=-0.5)
        wg, wgk = ring_get(2)
        for cg in range(4):
            pg, pgk = pj()
            proj(pg, pgk, wg, wgk, cg * 128, T)
            ysl = yT[:, cg * T:(cg + 1) * T]
            yk = ak(0, cg * T, cg * T + T)
            tt("dve", ysl, ysl, st[0][:, :T], ALU.subtract, yk + ["st0"], yk)
            tt("dve", ysl, ysl, st[1][:, :T], ALU.mult, yk + ["st1"], yk)
            act(ysl, ysl, AF.Silu, yk + ["prm"], yk, scale=lng[:, cg:cg + 1], bias=lnb[:, cg:cg + 1])
            act(st[2][:, :T], pg[:, :T], AF.Silu, [pgk], ["st2"])
            tt("dve", brY[0][:, cg, :T], ysl, st[2][:, :T], ALU.mult, yk + ["st2"], [("brY0", cg)])
        ring_done()
        wf, wfk = ring_get(7)
        for hb in range(4):
            psb, pk = pj()
            proj(psb, pk, wf, wfk, hb * 128, T)
            hg_gates(psb, pk, hb, T, nchunk, g0)
        ring_done()
        wq, wqk = ring_get(6)
        for hb in range(4):
            sl = slice(hb * T, (hb + 1) * T)
            psb, pk = pj()
            proj(psb, pk, wq, wqk, hb * 128, T)
            act(A[4][:, sl], A[2][:, sl], AF.Exp, ak(2, hb * T, hb * T + T), ak(4, hb * T, hb * T + T))
            act(A[0][:, sl], psb[:, :T], AF.Silu, [pk], ak(0, hb * T, hb * T + T))
            tt("dve", qA[:, hb, :T], A[0][:, sl], A[4][:, sl], ALU.mult, ak(0, hb * T, hb * T + T) + ak(4, hb * T, hb * T + T), [("qA", hb)])
        ring_done()
        wi, wik = ring_get(8)
        hg_vtok(wi, wik, nt, T)
        ring_done()
        hg_ktok(nt)
        wgt, wgtk = ring_get(9)
        for hb in range(4):
            psb, pk = pj()
            proj(psb, pk, wgt, wgtk, hb * 128, T)
            act(A[1][:, hb * T:(hb + 1) * T], psb[:, :T], AF.Silu, [pk], ak(1, hb * T, hb * T + T))
        ring_done()
        for ti in range(nt):
            tsl = slice(ti * 128, (ti + 1) * 128)
            for hb in range(4):
                for half in range(2):
                    c0 = ti * 128 + half * 64
                    mm(ps[4][half * 64:half * 64 + 64, hb * 64:(hb + 1) * 64], kA[:, hb, c0:c0 + 64], qA[:, hb, c0:c0 + 64],
                       True, True, [("kA", hb), ("qA", hb)], ["ps4"])
            tt("dve", scm[:, ti, :, :], ps[4][:, 0:256].rearrange("p (h t) -> p h t", t=64),
               trim.unsqueeze(1).to_broadcast([128, 4, 64]), ALU.mult, ["ps4", "cstb"], [("scm", ti)])
            for half in range(2):
                cj = ti * 2 + half
                rows = slice(half * 64, half * 64 + 64)
                c0 = ti * 128 + half * 64
                for hb in range(4):
                    hs = slice(hb * 128, (hb + 1) * 128)
                    ts("dve", Sbf[:, hs], S[:, hs], E3[:, hb, 1, cj:cj + 1], None, ALU.mult, None, ["S", ("E3", hb)], ["Sbf"])
                for hb in range(4):
                    hs = slice(hb * 128, (hb + 1) * 128)
                    osl = ps[6][:, hb * 128 + half * 64: hb * 128 + half * 64 + 64]
                    mm(osl, vtok[rows, ti, hs], scm[rows, ti, hb, :], True, False, [("vtok", ti), ("scm", ti)], ["ps6"])
                    mm(osl, Sbf[:, hs], qA[:, hb, c0:c0 + 64], False, True, ["Sbf", ("qA", hb)], ["ps6"])
                hg_state(ti, half)
            bff = bfs[:, :, :].rearrange("p a t -> p (a t)")[:, 0:512]
            a3k = ak(3, 0, 512)
            act(bff, ps[6][:, :], AF.Square, ["ps6"], ["bfs0", "bfs1"])
            mm(ps[3][:, :], ones, bff, True, True, ["cstb", "bfs0", "bfs1"], ["ps3"])
            act(A[3][:, 0:512], ps[3][:, :], AF.Ln, ["ps3"], a3k, scale=1.0 / 128, bias=EPS)
            act(A[3][:, 0:512], A[3][:, 0:512], AF.Exp, a3k, a3k, scale=-0.5)
            tt("dve", A[3][:, 0:512], ps[6][:, :], A[3][:, 0:512], ALU.mult, ["ps6"] + a3k, a3k)
            for hb in range(4):
                g0_, g1_ = hb * T + ti * 128, hb * T + (ti + 1) * 128
                stt("dve", brY[1][:, hb, tsl], A[3][:, hb * 128:(hb + 1) * 128], gng, A[1][:, g0_:g1_],
                    ALU.mult, ALU.mult, a3k + ["prm"] + ak(1, g0_, g1_), [("brY1", hb)])
        wq2, wq2k = ring_get(13)
        for qb in range(4):
            psb, pk = pj()
            proj(psb, pk, wq2, wq2k, qb * 128, T)
            knorm(psb, pk, qn[:, qb, :T], [("qn", qb)], qg, T)
        ring_done()
        wc, wck = ring_get(14)
        wvc = wc[:, :].rearrange("p (k c) -> p k c", c=512)
        for kvh in range(2):
            psb, pk = pj()
            proj(psb, pk, wc, wck, kvh * 128, T)
            knorm(psb, pk, KT[:, kvh, 128:128 + T], [("KT", kvh)], kg, T)
        for ti in range(nt):
            psb, pk = pj()
            for kc in range(KC):
                mm(psb[:, :128], hnT[:, kc, ti * 128:(ti + 1) * 128], wvc[:, kc, 256:384], kc == 0, kc == KC - 1, [wck, "hnT"], [pk])
            act(Vaug[:, 1 + ti, :, 0:64], psb[:, 0:128].rearrange("p (h d) -> p h d", d=64), AF.Copy, [pk], [("Vaug", 1 + ti)])
            if g0:
                ts("dve", Vaug[:, 1 + ti, :, 0:64], Vaug[:, 1 + ti, :, 0:64], vcol, None, ALU.mult, None, [("Vaug", 1 + ti), "cstf"], [("Vaug", 1 + ti)])
                cp("dve", Vaug[:, 1 + ti, :, 64:128], vcol.unsqueeze(1).to_broadcast([128, 2, 64]), [("Vaug", 1 + ti), "cstf"], [("Vaug", 1 + ti)])
            else:
                ms("dve", Vaug[:, 1 + ti, :, 64:128], 1.0, [("Vaug", 1 + ti)])
        ring_done()
        wgc, wgck = ring_get(15)
        for qb in range(4):
            psb, pk = pj()
            proj(psb, pk, wgc, wgck, qb * 128, T)
            act(A[1][:, qb * T:(qb + 1) * T], psb[:, :T], AF.Silu, [pk], ak(1, qb * T, qb * T + T))
        ring_done()
        for ti in range(nt):
            tsl = slice(ti * 128, (ti + 1) * 128)
            hist = slice(ti * 128, (ti + 1) * 128)
            cur = slice((ti + 1) * 128, (ti + 2) * 128)
            if g0:
                km, kmk, vm, vmk = KMz, "KMz", VMs, "VMs"
            elif vm_first and ti == 0:
                km, kmk, vm, vmk = KM, "KM", VM0, "VM0"
            else:
                km, kmk, vm, vmk = KM, "KM", VM, "VMall"
            for kvh in range(2):
                for g in range(4):
                    hq = kvh * 4 + g
                    qb = hq // 2
                    r = slice((hq % 2) * 64, (hq % 2) * 64 + 64)
                    mm(ps[5][:, g * 128:(g + 1) * 128], KT[r, kvh, hist], qn[r, qb, tsl], True, True, [("KT", kvh), "KThist", ("qn", qb)], ["ps5"])
                    mm(ps[6][:, g * 128:(g + 1) * 128], KT[r, kvh, cur], qn[r, qb, tsl], True, True, [("KT", kvh), ("qn", qb)], ["ps6"])
                    mm(ps[4][0:32, g * 128:(g + 1) * 128], km[r, kvh, :], qn[r, qb, tsl], True, True, [kmk, ("qn", qb)], ["ps4"])
                act(pA[:, :, :], ps[5][:, :].rearrange("p (g t) -> p g t", t=128), AF.Exp, ["ps5"], ["pA"], scale=0.125)
                act(pB[:, :, :], ps[6][:, :].rearrange("p (g t) -> p g t", t=128), AF.Exp, ["ps6"], ["pB"], scale=0.125)
                act(pM[:, :, :], ps[4][0:32, :].rearrange("p (g t) -> p g t", t=128), AF.Exp, ["ps4"], ["pM"], scale=0.125)
                for g in range(4):
                    hq = kvh * 4 + g
                    o0 = ps[3][:, g * 128:g * 128 + 64]
                    o1 = ps[3][:, g * 128 + 64:g * 128 + 128]
                    mm(o0, Vaug[:, ti, kvh, :], pA[:, g, 0:64], True, False, [("Vaug", ti), "pA"], ["ps3"])
                    mm(o0, Vaug[0:64, ti + 1, kvh, :], pB[0:64, g, 0:64], False, False, [("Vaug", ti + 1), "pB"], ["ps3"])
                    vmr = ["VM"] + [("VMd", q_) for q_ in range(8)] if vmk == "VMall" else [vmk]
                    mm(o0, vm[0:32, hq, :], pM[0:32, g, 0:64], False, True, vmr + ["pM"], ["ps3"])
                    mm(o1, Vaug[64:128, ti, kvh, :], pA[64:128, g, 64:128], True, False, [("Vaug", ti), "pA"], ["ps3"])
                    mm(o1, Vaug[:, ti + 1, kvh, :], pB[:, g, 64:128], False, False, [("Vaug", ti + 1), "pB"], ["ps3"])
                    mm(o1, vm[0:32, hq, :], pM[0:32, g, 64:128], False, True, vmr + ["pM"], ["ps3"])
                P.add("dve", lambda e: e.reciprocal(out=rden[64:128, :, :], in_=ps[3][64:128, :].rearrange("p (g t) -> p g t", t=128)),
                      ["ps3"], ["rden"])
                pv = ps[3][0:64, :].rearrange("p (g t) -> p g t", t=128)
                for par in range(2):
                    orow = slice(par * 64, par * 64 + 64)
                    qb0 = kvh * 2
                    tt("dve", ao[orow, :, :], pv[:, par::2, :], rden[64:128, par::2, :], ALU.mult, ["ps3", "rden"], [("ao", par)])
                    gk = ak(1, qb0 * T, (qb0 + 2) * T)
                    tt("dve", brY[2][orow, qb0:qb0 + 2, tsl], ao[orow, :, :],
                       A[1][orow, :].rearrange("p (q t) -> p q t", t=T)[:, qb0:qb0 + 2, tsl], ALU.mult,
                       [("ao", par)] + gk, [("brY2", qb0), ("brY2", qb0 + 1)])
        cp("pool", KT[:, :, 0:128], KT[:, :, T:T + 128], [("KT", 0), ("KT", 1)], ["KThist"])
        cp("pool", Vaug[:, 0, :, :], Vaug[:, nt, :, :], [("Vaug", nt)], [("Vaug", 0)])
        wga = [ring_get(3), ring_get(4)]
        wco = ring_get(5)
        for br, (gch, och) in enumerate([((3, 4), 5), ((10, 11), 12), ((16, 17), 18)]):
            if br > 0:
                wga = [ring_get(gch[0]), ring_get(gch[1])]
                wco = ring_get(och)
            wov = wco[0][:, :].rearrange("p (k c) -> p k c", c=1024)
            for ob in range(8):
                pg, pgk = pj()
                wg_, wgk_ = wga[ob // 4]
                proj(pg, pgk, wg_, wgk_, (ob % 4) * 128, T)
                pz, pzk = pj()
                for kc4 in range(4):
                    mm(pz[:, :T], wov[:, kc4, ob * 128:(ob + 1) * 128], brY[br][:, kc4, :T], kc4 == 0, kc4 == 3,
                       [wco[1], ("brY%d" % br, kc4)], [pzk])
                act(st[2][:, :T], pg[:, :T], AF.Sigmoid, [pgk], ["st2"])
                msl = A[2 + (ob % 2)][:, (ob // 2) * T:(ob // 2) * T + T]
                mk = ak(2 + ob % 2, (ob // 2) * T, (ob // 2) * T + T)
                if br == 0:
                    tt("dve", msl, pz[:, :T], st[2][:, :T], ALU.mult, [pzk, "st2"], mk)
                elif br == 1:
                    tt("dve", st[3][:, :T], pz[:, :T], st[2][:, :T], ALU.mult, [pzk, "st2"], ["st3"])
                    tt("pool", msl, msl, st[3][:, :T], ALU.add, mk + ["st3"], mk)
                else:
                    tt("dve", st[3][:, :T], pz[:, :T], st[2][:, :T], ALU.mult, [pzk, "st2"], ["st3"])
                    tt("pool", mixb[:, ob, :T], msl, st[3][:, :T], ALU.add, mk + ["st3"], ["sqb"])
            ring_done()
            ring_done()
            ring_done()
        wo = [ring_get(19), ring_get(20)]
        for ob2 in range(8):
            psb, pk = pj()
            w_, wk_ = wo[ob2 // 4]
            wv = w_[:, :].rearrange("p (k c) -> p k c", c=512)
            for kc in range(KC):
                mm(psb[:, :T], wv[:, kc, (ob2 % 4) * 128:(ob2 % 4 + 1) * 128], mixb[:, kc, :T], kc == 0, kc == KC - 1, [wk_, "sqb"], [pk])
            tt("dve", hT[:, ob2, tok0:tok0 + T], hT[:, ob2, tok0:tok0 + T], psb[:, :T], ALU.add, [hk, pk], [hk])
        ring_done()
        ring_done()

    group(0, groups[0], True, False)
    if DBG:
        for i_ in range(3):
            dd = dout("d_brY%d" % i_, [128, 4 * TMAX])
            P.dma("pool", dd[:, :], brY[i_][:].rearrange("p a t -> p (a t)"), reads=[("brY%d" % i_, q_) for q_ in range(4)])
        dd = dout("d_uT", [128, 4 * (30 + TMAX)])
        P.dma("pool", dd[:, :], uT[:].rearrange("p a t -> p (a t)"), reads=["uT"] + [("uT", q_) for q_ in range(4)])
        dd = dout("d_mixb", [128, KC * TMAX])
        P.dma("pool", dd[:, :], sqb[:].rearrange("p a t -> p (a t)"), reads=["sqb"])
    f0 = cmask[:, 8:9]
    omf0 = cmask[:, 9:10]
    cp("dve", KM[:, :, :], KT[:, :, 96:128], ["KThist"], ["KM"])
    ms("dve", KM[:, :, 0:1], 0.0, ["KM"])
    cp("dve", VM[:, :, :], VMs[:, :, :], ["VMs"], ["VM"])
    for hq in range(8):
        P.dma("sp", VM[16:32, hq, :], Vaug[112:128, 0, hq // 4, :], reads=[("Vaug", 0), "VM"], writes=[("VMd", hq)])
    ms("dve", lbt[0:32, 56:57], 1.0, ["lbt2"])
    P.dma("sp", lbt[16:32, 56:57], cmask[16:32, 9:10], reads=["cmask", "lbt2"], writes=["lbt3"])
    ts("dve", VM0[:, :, :], VM[:, :, :], lbt[0:32, 56:57], None, ALU.mult, None, ["VM", "lbt2", "lbt3"] + [("VMd", q_) for q_ in range(8)], ["VM0"])
    P.dma("sp", exs[:, 0:120], uex_d.rearrange("p c t -> p (c t)"), writes=["exs"])
    ts("dve", uT[:, :, 0:30], uT[:, :, 0:30], f0, None, ALU.mult, None, ["uT", "cmask"], ["uT"])
    stt("dve", uT[:, :, 0:30], exs[:, 0:120].rearrange("p (c t) -> p c t", t=30), omf0, uT[:, :, 0:30], ALU.mult, ALU.add,
        ["exs", "cmask", "uT"], ["uT"])
    P.dma("sp", exs[:, 128:384], kex_d.rearrange("p c t -> p (c t)"), writes=["exs2"])
    ts("dve", KT[:, :, 0:128], KT[:, :, 0:128], f0, None, ALU.mult, None, ["KThist", "cmask"], ["KThist"])
    stt("dve", KT[:, :, 0:128], exs[:, 128:384].rearrange("p (c t) -> p c t", t=128), omf0, KT[:, :, 0:128], ALU.mult, ALU.add,
        ["exs2", "cmask", "KThist"], ["KThist"])
    P.dma("sp", exs[:, 384:512], vex_d[:, :], writes=["exs3"])
    ts("dve", Vaug[:, 0, :, 0:64], Vaug[:, 0, :, 0:64], f0, None, ALU.mult, None, [("Vaug", 0), "cmask"], [("Vaug", 0)])
    stt("dve", Vaug[:, 0, :, 0:64], exs[:, 384:512].rearrange("p (h d) -> p h d", d=64), omf0, Vaug[:, 0, :, 0:64], ALU.mult, ALU.add,
        ["exs3", "cmask", ("Vaug", 0)], [("Vaug", 0)])
    ts("dve", Vaug[:, 0, :, 64:128], Vaug[:, 0, :, 64:128], f0, omf0, ALU.mult, ALU.add, [("Vaug", 0), "cmask"], [("Vaug", 0)])
    P.dma("sp", Fall[:, :, :], fall_d.rearrange("c p h -> p c h"), writes=["Fall"])
    for j in range(NCORES - 1):
        mj = cmask[:, j:j + 1]
        P.dma("sp", exs[:, :], sall_d[j], writes=["exs", "exs2", "exs3"])
        ts("dve", lbt[:, 60:64], Fall[:, j, :], -1.0, None, ALU.add, None, ["Fall"], ["lbt4"])
        ts("dve", lbt[:, 60:64], lbt[:, 60:64], mj, onecol, ALU.mult, ALU.add, ["lbt4", "cmask", "cstf"], ["lbt4"])
        tt("dve", S[:, :].rearrange("p (h v) -> p h v", v=128), S[:, :].rearrange("p (h v) -> p h v", v=128),
           lbt[:, 60:64].unsqueeze(2).to_broadcast([128, 4, 128]), ALU.mult, ["S", "lbt4"], ["S"])
        stt("dve", S[:, :], exs[:, :], mj, S[:, :], ALU.mult, ALU.add, ["exs", "exs2", "exs3", "cmask", "S"], ["S"])
    for gi in range(1, len(groups)):
        group(gi, groups[gi], False, gi == 1)
    for kc in range(KC):
        P.dma("sp", h_out[:, kc, :], hT[:, kc, :], reads=[("hT", g) for g in range(len(groups))])
    P.finish()
    return nc, P


def _chunk1024(W):
    return np.ascontiguousarray(W.reshape(8, 128, 512).transpose(1, 0, 2).reshape(128, CW))


def _chunk512(W):
    return np.ascontiguousarray(W.reshape(4, 128, 1024).transpose(1, 0, 2).reshape(128, CW))


def _layer_chunks(inp, l):
    w = inp["w_in"][l]
    ch = np.zeros((NCH, 128, CW), np.float32)
    def c(i, a, b):
        ch[i] = _chunk1024(w[:, a:b])
    c(0, 0, 512); c(1, 512, 1024); c(2, 1024, 1536)
    c(3, 4864, 5376); c(4, 5376, 5888)
    ch[5] = _chunk512(inp["w_conv_out"][l])
    c(6, 1536, 2048); c(7, 2048, 2560); c(8, 2560, 3072); c(9, 3072, 3584)
    c(10, 5888, 6400); c(11, 6400, 6912)
    ch[12] = _chunk512(inp["w_hg_out"][l])
    c(13, 3584, 4096)
    ckv = np.zeros((1024, 512), np.float32)
    ckv[:, 0:64] = w[:, 4096:4160]; ckv[:, 64:128] = w[:, 4096:4160]
    ckv[:, 128:192] = w[:, 4160:4224]; ckv[:, 192:256] = w[:, 4160:4224]
    ckv[:, 256:384] = w[:, 4224:4352]
    ch[14] = _chunk1024(ckv)
    c(15, 4352, 4864); c(16, 6912, 7424); c(17, 7424, 7936)
    ch[18] = _chunk512(inp["w_att_out"][l])
    ch[19] = _chunk1024(inp["w_out"][l][:, 0:512]); ch[20] = _chunk1024(inp["w_out"][l][:, 512:1024])
    return ch


def _layer_prm(inp, l):
    p = np.zeros((128, NPRM), np.float32)
    p[:, P_NG:P_NG + 8] = inp["norm_g"][l].reshape(8, 128).T
    hlb = inp["hg_lower_bounds"]
    p[:, P_LBRAW:P_LBRAW + 16] = hlb.reshape(4, 4, 128).transpose(2, 1, 0).reshape(128, 16)
    p[:, P_LSEL + l] = 1.0
    p[:, P_KG] = np.tile(inp["k_norm_g"][l], 2)
    p[:, P_QG] = np.tile(inp["q_norm_g"][l], 2)
    p[:, P_GNG] = inp["hg_norm_g"][l]
    p[:, P_CB:P_CB + 4] = inp["conv_b"][l].reshape(4, 128).T
    p[:, P_LNG:P_LNG + 4] = inp["conv_ln_g"][l].reshape(4, 128).T
    p[:, P_LNB:P_LNB + 4] = inp["conv_ln_b"][l].reshape(4, 128).T
    p[:, P_CW:P_CW + 124] = inp["conv_w"][l].reshape(31, 4, 128).transpose(2, 1, 0).reshape(128, 124)
    return p


def _consts():
    c = np.zeros((128, NCST), np.float32)
    c[:, C_ID:C_ID + 128] = np.eye(128)
    c[:, C_ONES:C_ONES + 128] = 1.0
    c[0:64, C_BONES:C_BONES + 64] = 1.0
    c[64:128, C_BONES + 64:C_BONES + 128] = 1.0
    s = np.arange(128)[:, None] % 64
    t = np.arange(64)[None, :]
    c[:, C_TRI:C_TRI + 64] = (s <= t)
    c[:, C_VROW + 112:C_VROW + 128] = 1.0
    c[112:128, C_VCOL] = 1.0
    return c


_CACHE = {}


def _prog(phase):
    if phase not in _CACHE:
        _CACHE[phase] = build(phase)[0]
    return _CACHE[phase]


def kernel(x, meta_tokens, norm_g, w_in, conv_w, conv_b, conv_ln_g, conv_ln_b, w_conv_out,
           hg_lower_bounds, hg_norm_g, w_hg_out, q_norm_g, k_norm_g, attn_sinks, w_att_out, w_out):
    inp = dict(x=x, meta_tokens=meta_tokens, norm_g=norm_g, w_in=w_in, conv_w=conv_w, conv_b=conv_b,
               conv_ln_g=conv_ln_g, conv_ln_b=conv_ln_b, w_conv_out=w_conv_out, hg_lower_bounds=hg_lower_bounds,
               hg_norm_g=hg_norm_g, w_hg_out=w_hg_out, q_norm_g=q_norm_g, k_norm_g=k_norm_g, attn_sinks=attn_sinks,
               w_att_out=w_att_out, w_out=w_out)
    inp = {k: np.asarray(v, np.float32) for k, v in inp.items()}
    xs = inp["x"][0]
    cst = _consts()
    hTs = []
    for c in range(NCORES):
        h = np.zeros((TOK, D), np.float32)
        h[112:128] = inp["meta_tokens"]
        h[128:] = xs[c * OWN:(c + 1) * OWN]
        hTs.append(np.ascontiguousarray(h.T.reshape(KC, 128, TOK).transpose(1, 0, 2)))
    cmasks = []
    for c in range(NCORES):
        m = np.zeros((128, 16), np.float32)
        m[:, 0:8] = (np.arange(8) < c)[None, :]
        m[:, 8] = 1.0 if c == 0 else 0.0
        m[:, 9] = 0.0 if c == 0 else 1.0
        cmasks.append(m)
    ncA = _prog("A")
    ncB = _prog("B")
    cores = list(range(NCORES))
    for l in range(DEPTH):
        ch = _layer_chunks(inp, l)
        prm = _layer_prm(inp, l)
        chA = np.ascontiguousarray(ch[[7, 8, 0, 1, 14]])
        inA = [dict(hT=hTs[c], wch=chA, prm=prm, cst=cst, cmask=cmasks[c]) for c in cores]
        rA = run_bass_kernel_spmd(ncA, inA, core_ids=cores).results
        Sall = np.ascontiguousarray(np.stack([rA[c]["Sloc"] for c in cores]))
        Fall = np.ascontiguousarray(np.stack([rA[c]["Floc"] for c in cores]))
        sk = np.zeros((32, 8), np.float32)
        sk[0] = inp["attn_sinks"][l]
        inB = []
        for c in cores:
            p = max(c - 1, 0)
            inB.append(dict(hT=hTs[c], wch=ch, prm=prm, cst=cst, cmask=cmasks[c], sinks=sk, Sall=Sall, Fall=Fall,
                            uex=rA[p]["utail"], kex=rA[p]["ktail"], vex=rA[p]["vtail"]))
        rB = run_bass_kernel_spmd(ncB, inB, core_ids=cores).results
        hTs = [rB[c]["hT_out"] for c in cores]
    out = np.zeros((1, SEQ, D), np.float32)
    for c in cores:
        hc = hTs[c].transpose(1, 0, 2).reshape(D, TOK).T
        out[0, c * OWN:(c + 1) * OWN] = hc[128:]
    return out
```

```python
import contextlib
import numpy as np
import concourse.bass as bass
import concourse.mybir as mybir
from concourse.bass_utils import run_bass_kernel_spmd

F32 = mybir.dt.float32
BF16 = mybir.dt.bfloat16
AF = mybir.ActivationFunctionType
ALU = mybir.AluOpType
AX = mybir.AxisListType

NCORES = 8
D = 1024
KC = 8
SEQ = 16384
OWN = SEQ // NCORES
NT = 1 + OWN // 128
TOK = NT * 128
TG = 2
TMAX = TG * 128
DEPTH = 4
EPS = 1e-6
FL = 1e-30
NCH = 21
CW = 4096
DBG = False
ENGS = ("pe", "act", "dve", "pool", "sp")

P_NG, P_LBRAW, P_LSEL, P_KG, P_QG, P_GNG, P_CB, P_LNG, P_LNB, P_CW = 0, 8, 24, 28, 29, 30, 31, 35, 39, 43
NPRM = 43 + 124
C_ID, C_ONES, C_BONES, C_TRI, C_VROW, C_VCOL = 0, 128, 256, 384, 448, 576
NCST = 577


class _Op:
    __slots__ = ("eng", "fn", "reads", "writes", "dma", "idx", "waits", "flag", "sem", "val", "stage")


class Prog:
    def __init__(self, nc):
        self.nc = nc
        self.ops = []
        self.stack = contextlib.ExitStack()
        self.n_dma_sems = {"sp": 16, "pool": 12}

    def sb(self, name, shape, dtype):
        return self.stack.enter_context(self.nc.sbuf_tensor(name, list(shape), dtype))

    def ps(self, name, shape, dtype=F32):
        return self.stack.enter_context(self.nc.psum_tensor(name, list(shape), dtype))

    def add(self, eng, fn, reads=(), writes=(), dma=False):
        op = _Op()
        op.eng, op.fn, op.reads, op.writes, op.dma = eng, fn, tuple(reads), tuple(writes), dma
        op.idx = len(self.ops)
        op.stage = getattr(self, "stage", "")
        op.flag = dma
        self.ops.append(op)
        return op

    def dma(self, eng, out, in_, reads=(), writes=()):
        return self.add(eng, lambda e: e.dma_start(out=out, in_=in_), reads, writes, dma=True)

    def finish(self):
        nc, ops = self.nc, self.ops
        last_w, rdrs, deps_all = {}, {}, []
        for op in ops:
            deps = set()
            raw = set()
            for k in op.reads:
                w = last_w.get(k)
                if w is not None:
                    deps.add(w)
                    raw.add(w)
            for k in op.writes:
                w = last_w.get(k)
                if w is not None:
                    deps.add(w)
                r = rdrs.get(k)
                if r:
                    deps.update(r)
            deps.discard(op.idx)
            need = []
            for d in deps:
                dop = ops[d]
                if dop.eng == op.eng and not dop.dma and not op.dma:
                    if op.eng == "pe" or d not in raw:
                        continue
                need.append(d)
                dop.flag = True
            deps_all.append(need)
            for k in op.reads:
                rdrs.setdefault(k, []).append(op.idx)
            for k in op.writes:
                last_w[k] = op.idx
                rdrs[k] = []
        sem_eng = {e: self.stack.enter_context(nc.semaphore("s_" + e)) for e in ENGS if e != "sp"}
        dma_sems = {q: [self.stack.enter_context(nc.semaphore("d_%s%d" % (q, i))) for i in range(n)]
                    for q, n in self.n_dma_sems.items()}
        cnt = {e: 0 for e in ENGS}
        dma_rr = {q: 0 for q in dma_sems}
        dma_val, dma_prev = {}, {}
        for op in ops:
            if op.dma:
                q = op.eng
                i = dma_rr[q] % len(dma_sems[q])
                dma_rr[q] += 1
                v = dma_val.get((q, i), 0) + 16
                dma_val[(q, i)] = v
                op.sem, op.val = dma_sems[q][i], v
                p = dma_prev.get((q, i))
                if p is not None:
                    deps_all[op.idx].append(p)
                dma_prev[(q, i)] = op.idx
            elif op.flag:
                cnt[op.eng] += 1
                op.sem, op.val = sem_eng[op.eng], cnt[op.eng]
            else:
                op.sem = op.val = None
        seen = {e: {} for e in ENGS}
        for op in ops:
            best = {}
            for d in deps_all[op.idx]:
                dop = ops[d]
                key = id(dop.sem)
                if key not in best or best[key][1] < dop.val:
                    best[key] = (dop.sem, dop.val)
            sn = seen[op.eng]
            waits = []
            for key, (sem, val) in best.items():
                if sn.get(key, 0) >= val:
                    continue
                sn[key] = val
                waits.append((sem, val))
            op.waits = waits
        per = {e: [o for o in ops if o.eng == e] for e in ENGS}
        final = [(dma_sems[q][i], v) for (q, i), v in dma_val.items()]
        self.stats = {e: len(per[e]) for e in ENGS}
        self.stats["waits"] = sum(len(o.waits) for o in ops)

        def emit(e, name):
            for op in per[name]:
                for sem, val in op.waits:
                    e.wait_ge(sem, val)
                ins = op.fn(e)
                if op.sem is not None:
                    ins.then_inc(op.sem, 16 if op.dma else 1)
            if name == "sp":
                for sem, val in final:
                    e.wait_ge(sem, val)

        with nc.Block() as block:
            @block.tensor
            def _(e):
                emit(e, "pe")

            @block.scalar
            def _(e):
                emit(e, "act")

            @block.vector
            def _(e):
                emit(e, "dve")

            @block.gpsimd
            def _(e):
                emit(e, "pool")

            @block.sync
            def _(e):
                emit(e, "sp")
        self.stack.close()


def build(phase):
    nc = bass.Bass("TRN2", target_bir_lowering=False)
    P = Prog(nc)

    def din(name, shape, dt=F32):
        return nc.dram_tensor(name, list(shape), dt, kind="ExternalInput").ap()

    def dout(name, shape, dt=F32):
        return nc.dram_tensor(name, list(shape), dt, kind="ExternalOutput").ap()

    h_in = din("hT", [128, KC, TOK])
    nw = NCH if phase == "B" else 5
    w_in_d = din("wch", [nw, 128, CW])
    prm_d = din("prm", [128, NPRM])
    cst_d = din("cst", [128, NCST])
    cmask_d = din("cmask", [128, 16])
    wscr = nc.dram_tensor("wscr", [nw, 128, CW], BF16).ap()
    if phase == "B":
        sinks_d = din("sinks", [32, 8])
        sall_d = din("Sall", [NCORES, 128, 512])
        fall_d = din("Fall", [NCORES, 128, 4])
        uex_d = din("uex", [128, 4, 30])
        kex_d = din("kex", [128, 2, 128])
        vex_d = din("vex", [128, 128])
        h_out = dout("hT_out", [128, KC, TOK])
    else:
        s_out = dout("Sloc", [128, 512])
        f_out = dout("Floc", [128, 4])
        u_out = dout("utail", [128, 4, 30])
        k_out = dout("ktail", [128, 2, 128])
        v_out = dout("vtail", [128, 128])
    if phase == "A":
        cmap = {7: 0, 8: 1, 0: 2, 1: 3, 14: 4}
    else:
        cmap = {i: i for i in range(NCH)}

    T = TMAX
    hT = P.sb("hTs", [128, KC, TOK], F32)
    hnTs = [P.sb("hnT%d" % i, [128, KC, T], BF16) for i in range(2)]
    HN = {"hn": hnTs[0], "hk": ("hnT", 0)}
    sqb = P.sb("sqb", [128, KC, T], BF16)
    wr = [P.sb("wr%d" % i, [128, CW], BF16) for i in range(3)]
    prm = P.sb("prm_s", [128, NPRM], F32)
    cstf = P.sb("cstf", [128, NCST], F32)
    cstb = P.sb("cstb", [128, NCST], BF16)
    cmask = P.sb("cmask_s", [128, 16], F32)
    A = [P.sb("A%d" % i, [128, 4 * T], F32) for i in range(5)]
    st = [P.sb("st%d" % i, [128, T], F32) for i in range(4)]
    lbt = P.sb("lbt", [128, 64], F32)
    kA = P.sb("kA", [128, 4, T], BF16)
    kAtok = P.sb("kAtok", [128, TG, 512], BF16)
    vtok = P.sb("vtok", [128, TG, 512], BF16)
    S = P.sb("S", [128, 512], F32)
    Stmp = P.sb("Stmp", [128, 512], F32)
    E3 = P.sb("E3", [128, 4, 3, 2 * TG], F32)
    D3 = P.sb("D3", [128, 4, 3, 2 * TG], F32)
    Bend = P.sb("Bend", [128, 4, 2 * TG + 1], F32)
    Bmid = P.sb("Bmid", [128, 4, 2 * TG], F32)
    Ltot = P.sb("Ltot", [128, 4], F32)
    bfs = P.sb("bfs", [128, 2, T], BF16)
    if phase == "B":
        diag = P.sb("diag", [128, 124, 128], BF16)
        uT = P.sb("uT", [128, 4, 30 + T], BF16)
        yT = A[0]
        brY = [P.sb("brY%d" % i, [128, 4, T], BF16) for i in range(3)]
        qA = P.sb("qA", [128, 4, T], BF16)
        scm = P.sb("scm", [128, TG, 4, 64], BF16)
        Sbf = P.sb("Sbf", [128, 512], BF16)
        qn = P.sb("qn", [128, 4, T], BF16)
        KT = P.sb("KT", [128, 2, 128 + T], BF16)
        Vaug = P.sb("Vaug", [128, TG + 1, 2, 128], BF16)
        KM = P.sb("KM", [128, 2, 32], BF16)
        KMz = P.sb("KMz", [128, 2, 32], BF16)
        VM = P.sb("VM", [32, 8, 128], BF16)
        VM0 = P.sb("VM0", [32, 8, 128], BF16)
        VMs = P.sb("VMs", [32, 8, 128], BF16)
        sinks = P.sb("sinks_s", [32, 8], F32)
        pA = P.sb("pA", [128, 4, 128], BF16)
        pB = P.sb("pB", [128, 4, 128], BF16)
        pM = P.sb("pM", [32, 4, 128], BF16)
        rden = P.sb("rden", [128, 4, 128], F32)
        ao = P.sb("ao", [128, 2, 128], F32)
        exs = A[4][:, 0:512]
        Fall = P.sb("Fall_s", [128, NCORES, 4], F32)
        mixb = sqb
    else:
        utl = P.sb("utl", [128, 4, 128], F32)
    ps = [P.ps("ps%d" % i, [128, 512], F32) for i in range(7)]
    pst = P.ps("pst", [128, 512], BF16)
    pjc = [0]

    def pj():
        i = pjc[0] % 3
        pjc[0] += 1
        return ps[i], "ps%d" % i

    def ak(i, lo, hi):
        return [("A", i, u) for u in range(lo // 128, (hi + 127) // 128)]

    ident = cstb[:, C_ID:C_ID + 128]
    ones = cstb[:, C_ONES:C_ONES + 128]
    bones = cstb[:, C_BONES:C_BONES + 128]
    trim = cstb[:, C_TRI:C_TRI + 64]
    vrow = cstf[:, C_VROW:C_VROW + 128]
    vcol = cstf[:, C_VCOL:C_VCOL + 1]
    onecol = cstf[:, C_ONES:C_ONES + 1]

    def act(out, in_, func, reads, writes, scale=1.0, bias=0.0):
        P.add("act", lambda e: e.activation(out=out, in_=in_, func=func, bias=bias, scale=scale), reads, writes)

    def tt(eng, out, in0, in1, op, reads, writes):
        P.add(eng, lambda e: e.tensor_tensor(out=out, in0=in0, in1=in1, op=op), reads, writes)

    def ts(eng, out, in0, s1, s2, op0, op1, reads, writes):
        if op1 is None:
            P.add(eng, lambda e: e.tensor_scalar(out=out, in0=in0, scalar1=s1, scalar2=None, op0=op0), reads, writes)
        else:
            P.add(eng, lambda e: e.tensor_scalar(out=out, in0=in0, scalar1=s1, scalar2=s2, op0=op0, op1=op1), reads, writes)

    def stt(eng, out, in0, scalar, in1, op0, op1, reads, writes):
        P.add(eng, lambda e: e.scalar_tensor_tensor(out=out, in0=in0, scalar=scalar, in1=in1, op0=op0, op1=op1), reads, writes)

    def cp(eng, out, in_, reads, writes):
        P.add(eng, lambda e: e.tensor_copy(out=out, in_=in_), reads, writes)

    def mm(out, lhsT, rhs, start, stop, reads, writes):
        P.add("pe", lambda e: e.matmul(out, lhsT=lhsT, rhs=rhs, start=start, stop=stop), reads, writes)

    def tr(out, in_, reads, writes):
        P.add("pe", lambda e: e.transpose(out, in_, ident), reads, writes)

    def ms(eng, ap, val, writes):
        P.add(eng, lambda e: e.memset(ap, val), (), writes)

    P.dma("sp", prm[:], prm_d[:, :], writes=["prm"])
    P.dma("sp", cstf[:], cst_d[:, :], writes=["cstf"])
    P.dma("pool", cstb[:], cst_d[:, :], writes=["cstb"])
    P.dma("sp", cmask[:], cmask_d[:, :], writes=["cmask"])
    for kc in range(KC):
        P.dma("sp", hT[:, kc, :], h_in[:, kc, :], writes=[("hTl", kc)])
    if phase == "A":
        order = [7, 8, 0, 1, 14]
    else:
        order = list(range(NCH))
    for ci in order:
        P.dma("pool", wscr[cmap[ci]], w_in_d[cmap[ci]], writes=[("wscr", ci)])

    sched = []
    ring = {"next_load": 0, "next_use": 0}

    def ring_load():
        i = ring["next_load"]
        if i < len(sched):
            ci = sched[i]
            slot = i % 3
            P.dma("sp", wr[slot][:, :], wscr[cmap[ci]], reads=[("wscr", ci)], writes=[("wr", slot)])
            ring["next_load"] += 1

    def ring_get(ci):
        i = ring["next_use"]
        assert sched[i] == ci, (sched[i], ci, i)
        ring["next_use"] += 1
        slot = i % 3
        return wr[slot], ("wr", slot)

    def ring_done():
        ring_load()

    ng = prm[:, P_NG:P_NG + 8]
    lbraw = prm[:, P_LBRAW:P_LBRAW + 16].rearrange("p (h l) -> p h l", l=4)
    lsel = prm[:, P_LSEL:P_LSEL + 4]
    kg = prm[:, P_KG:P_KG + 1]
    qg = prm[:, P_QG:P_QG + 1]
    gng = prm[:, P_GNG:P_GNG + 1]
    cb = prm[:, P_CB:P_CB + 4]
    lng = prm[:, P_LNG:P_LNG + 4]
    lnb = prm[:, P_LNB:P_LNB + 4]
    cwv = prm[:, P_CW:P_CW + 124].rearrange("p (c k) -> p c k", k=31)
    mx = lbt[:, 0:4]
    ex = lbt[:, 4:20].rearrange("p (h l) -> p h l", l=4)
    sm = lbt[:, 20:24]
    lball = lbt[:, 24:40].rearrange("p (h l) -> p h l", l=4)
    lb = lbt[:, 40:44]
    oml = lbt[:, 44:48]
    flb = lbt[:, 48:52]
    negone = lbt[:, 52:53]
    P.add("dve", lambda e: e.tensor_reduce(out=mx, in_=lbraw, axis=AX.X, op=ALU.max), ["prm"], ["lbt"])
    tt("dve", ex, lbraw, mx.unsqueeze(2).to_broadcast([128, 4, 4]), ALU.subtract, ["prm", "lbt"], ["lbt"])
    act(ex, ex, AF.Exp, ["lbt"], ["lbt"])
    P.add("dve", lambda e: e.tensor_reduce(out=sm, in_=ex, axis=AX.X, op=ALU.add), ["lbt"], ["lbt"])
    P.add("dve", lambda e: e.reciprocal(out=sm, in_=sm), ["lbt"], ["lbt"])
    tt("dve", ex, ex, sm.unsqueeze(2).to_broadcast([128, 4, 4]), ALU.mult, ["lbt"], ["lbt"])
    ms("dve", lball[:, :, 0:1], 0.0, ["lbt"])
    for l in range(1, 4):
        tt("dve", lball[:, :, l:l + 1], lball[:, :, l - 1:l], ex[:, :, l:l + 1], ALU.add, ["lbt"], ["lbt"])
    ts("dve", lball, lball, 0.0, 1.0, ALU.max, ALU.min, ["lbt"], ["lbt"])
    tt("dve", lball, lball, lsel.unsqueeze(1).to_broadcast([128, 4, 4]), ALU.mult, ["lbt", "prm"], ["lbt"])
    P.add("dve", lambda e: e.tensor_reduce(out=lb, in_=lball, axis=AX.X, op=ALU.add), ["lbt"], ["lbt"])
    ts("dve", oml, lb, -1.0, 1.0, ALU.mult, ALU.add, ["lbt"], ["lbt"])
    ts("dve", flb, lb, -1.0, FL, ALU.mult, ALU.add, ["lbt"], ["lbt"])
    ms("dve", negone, -1.0, ["lbt"])

    ms("dve", S[:], 0.0, [("S", q_) for q_ in range(4)])
    ms("dve", Ltot[:], 0.0, ["Ltot"])
    ms("dve", Bend[:], 0.0, [("Bend", q_) for q_ in range(4)])

    if phase == "B":
        P.dma("sp", sinks[:], sinks_d[:, :], writes=["sinks"])
        for cg in range(4):
            for k in range(31):
                ts("pool", diag[:, cg * 31 + k, :], ident, cwv[:, cg, k:k + 1], None, ALU.mult, None,
                   ["cstb", "prm"], ["diag"])
        ms("pool", uT[:], 0.0, ["uT"] + [("uT", q_) for q_ in range(4)])
        ms("pool", KT[:], 0.0, ["KThist", ("KT", 0), ("KT", 1)])
        ms("pool", Vaug[:], 0.0, [("Vaug", q_) for q_ in range(TG + 1)])
        ms("pool", KMz[:], 0.0, ["KMz"])
        ms("dve", VMs[:], 0.0, ["VMs"])
        act(sinks[0:1, :], sinks[0:1, :], AF.Exp, ["sinks"], ["sinks"])
        cp("dve", VMs[0:1, :, 64:128], sinks[0:1, :].unsqueeze(2).to_broadcast([1, 8, 64]), ["sinks", "VMs"], ["VMs"])

    def norm(gi, tok0, T):
        hk = ("hT", gi)
        hn, hnk = hnTs[gi % 2], ("hnT", gi % 2)
        act(sqb[:, :, :T], hT[:, :, tok0:tok0 + T], AF.Square, [hk] + [("hTl", k_) for k_ in range(KC)], ["sqb"])
        for kc in range(KC):
            mm(ps[3][:, :T], ones, sqb[:, kc, :T], kc == 0, kc == KC - 1, ["cstb", "sqb"], ["ps3"])
        act(st[0][:, :T], ps[3][:, :T], AF.Ln, ["ps3"], ["st0"], scale=1.0 / D, bias=EPS)
        act(st[1][:, :T], st[0][:, :T], AF.Exp, ["st0"], ["st1"], scale=-0.5)
        for kc in range(KC):
            stt("dve", hn[:, kc, :T], hT[:, kc, tok0:tok0 + T], ng[:, kc:kc + 1], st[1][:, :T], ALU.mult, ALU.mult,
                [hk, "prm", "st1"], [hnk])

    def use_hn(gi):
        HN["hn"], HN["hk"] = hnTs[gi % 2], ("hnT", gi % 2)

    def proj(psb, pk, wbuf, wk, c0, T, width=128, ncol=512):
        wv = wbuf[:, :].rearrange("p (k c) -> p k c", c=ncol)
        for kc in range(KC):
            mm(psb[:width, :T], wv[:, kc, c0:c0 + width], HN["hn"][:, kc, :T], kc == 0, kc == KC - 1, [wk, HN["hk"]], [pk])

    def hg_gates(psz, pzk, hb, T, nchunk, g0):
        sl = slice(hb * T, (hb + 1) * T)
        X1, X2, X3, X4 = A[0][:, sl], A[1][:, sl], A[2][:, sl], A[3][:, sl]
        k1, k2, k3, k4 = ak(0, hb * T, hb * T + T), ak(1, hb * T, hb * T + T), ak(2, hb * T, hb * T + T), ak(3, hb * T, hb * T + T)
        act(X1, psz[:, :T], AF.Sigmoid, [pzk], k1)
        ts("dve", X1, X1, oml[:, hb:hb + 1], flb[:, hb:hb + 1], ALU.mult, ALU.max, k1 + ["lbt"], k1)
        act(X2, X1, AF.Ln, k1 + ["lbt"], k2, bias=lb[:, hb:hb + 1])
        ts("dve", X1, X1, negone, oml[:, hb:hb + 1], ALU.mult, ALU.add, k1 + ["lbt"], k1)
        if g0:
            tt("dve", X2, X2, vrow[:, :T], ALU.mult, k2 + ["cstf"], k2)
            tt("dve", X1, X1, vrow[:, :T], ALU.mult, k1 + ["cstf"], k1)
        P.add("dve", lambda e: e.tensor_tensor_scan(out=X3, data0=X2, data1=X2, initial=0.0, op0=ALU.add, op1=ALU.bypass),
              k2, k3)
        B3 = X3.rearrange("p (c t) -> p c t", t=64)
        cp("dve", Bmid[:, hb, :nchunk].unsqueeze(2), B3[:, :, 31:32], k3, [("Bmid", hb)])
        cp("dve", Bend[:, hb, 1:1 + nchunk].unsqueeze(2), B3[:, :, 63:64], k3, [("Bend", hb)])
        tt("dve", D3[:, hb, 0, :nchunk], Bend[:, hb, 1:1 + nchunk], Bend[:, hb, 0:nchunk], ALU.subtract, [("Bend", hb)], [("D3", hb)])
        tt("dve", D3[:, hb, 1, :nchunk], Bmid[:, hb, :nchunk], Bend[:, hb, 0:nchunk], ALU.subtract, [("Bend", hb), ("Bmid", hb)], [("D3", hb)])
        tt("dve", D3[:, hb, 2, :nchunk], Bend[:, hb, 1:1 + nchunk], Bmid[:, hb, :nchunk], ALU.subtract, [("Bend", hb), ("Bmid", hb)], [("D3", hb)])
        act(E3[:, hb, :, :nchunk], D3[:, hb, :, :nchunk], AF.Exp, [("D3", hb)], [("E3", hb)])
        tt("dve", Ltot[:, hb:hb + 1], Ltot[:, hb:hb + 1], Bend[:, hb, nchunk:nchunk + 1], ALU.add, [("Bend", hb), "Ltot"], ["Ltot"])
        tt("dve", B3, B3, Bmid[:, hb, :nchunk].unsqueeze(2).to_broadcast([128, nchunk, 64]), ALU.subtract, k3 + [("Bmid", hb)], k3)
        act(X4, X3, AF.Exp, k3, k4, scale=-1.0)
        tt("dve", kA[:, hb, :T], X1, X4, ALU.mult, k1 + k4, [("kA", hb)])

    def hg_vtok(wbuf, wk, nt, T):
        wv = wbuf[:, :].rearrange("p (k c) -> p k c", c=512)
        for ti in range(nt):
            psb, pk = pj()
            for kc in range(KC):
                mm(psb[:, :], HN["hn"][:, kc, ti * 128:(ti + 1) * 128], wv[:, kc, :], kc == 0, kc == KC - 1, [wk, HN["hk"]], [pk])
            act(vtok[:, ti, :], psb[:, :], AF.Copy, [pk], [("vtok", ti)])

    def hg_ktok(nt):
        for ti in range(nt):
            for hb in range(4):
                tr(pst[:, hb * 128:(hb + 1) * 128], kA[:, hb, ti * 128:(ti + 1) * 128], [("kA", hb), "cstb"], ["pst"])
            cp("dve", kAtok[:, ti, :], pst[:, :], ["pst"], [("kAtok", ti)])

    def hg_state(ti, half):
        cj = ti * 2 + half
        rows = slice(half * 64, half * 64 + 64)
        for hb in range(4):
            mm(ps[5][:, hb * 128:(hb + 1) * 128], kAtok[rows, ti, hb * 128:(hb + 1) * 128],
               vtok[rows, ti, hb * 128:(hb + 1) * 128], True, True, [("kAtok", ti), ("vtok", ti)], ["ps5"])
        for hb in range(4):
            hs = slice(hb * 128, (hb + 1) * 128)
            ts("dve", Stmp[:, hs], ps[5][:, hs], E3[:, hb, 2, cj:cj + 1], None, ALU.mult, None, ["ps5", ("E3", hb)], [("Stmp", hb)])
            stt("dve", S[:, hs], S[:, hs], E3[:, hb, 0, cj:cj + 1], Stmp[:, hs], ALU.mult, ALU.add,
                [("S", hb), ("Stmp", hb), ("E3", hb)], [("S", hb)])

    def knorm(psb, pk, out_ap, out_keys, gcol, T):
        act(bfs[:, 0, :T], psb[:, :T], AF.Square, [pk], ["bfs0"])
        mm(ps[3][:, :T], bones, bfs[:, 0, :T], True, True, ["cstb", "bfs0"], ["ps3"])
        act(st[2][:, :T], ps[3][:, :T], AF.Ln, ["ps3"], ["st2"], scale=1.0 / 64, bias=EPS)
        act(st[3][:, :T], st[2][:, :T], AF.Exp, ["st2"], ["st3"], scale=-0.5)
        stt("dve", out_ap, psb[:, :T], gcol, st[3][:, :T], ALU.mult, ALU.mult, [pk, "prm", "st3"], out_keys)

    if phase == "A":
        groups = [list(range(1 + g * TG, 1 + (g + 1) * TG)) for g in range((NT - 1) // TG)]
        for gi in range(len(groups)):
            sched.extend([7, 8])
        sched.extend([0, 1, 14])
        for _ in range(3):
            ring_load()
        for gi, tiles in enumerate(groups):
            nt = len(tiles)
            T = nt * 128
            tok0 = tiles[0] * 128
            nchunk = 2 * nt
            if gi == 0:
                norm(gi, tok0, T)
            use_hn(gi)
            wf, wfk = ring_get(7)
            for hb in range(4):
                psb, pk = pj()
                proj(psb, pk, wf, wfk, hb * 128, T)
                hg_gates(psb, pk, hb, T, nchunk, False)
            ring_done()
            wi, wik = ring_get(8)
            hg_vtok(wi, wik, nt, T)
            ring_done()
            hg_ktok(nt)
            if gi + 1 < len(groups):
                nt2 = groups[gi + 1]
                norm(gi + 1, nt2[0] * 128, len(nt2) * 128)
            for ti in range(nt):
                for half in range(2):
                    hg_state(ti, half)
            if gi == len(groups) - 1:
                lt = (nt - 1) * 128
                wa, wak = ring_get(0)
                wb, wbk = ring_get(1)
                wva = wa[:, :].rearrange("p (k c) -> p k c", c=512)
                wvb = wb[:, :].rearrange("p (k c) -> p k c", c=512)
                for cg in range(4):
                    pa, pak = pj()
                    pb, pbk = pj()
                    for kc in range(KC):
                        mm(pa[:, :128], wva[:, kc, cg * 128:(cg + 1) * 128], HN["hn"][:, kc, lt:lt + 128], kc == 0, kc == KC - 1, [wak, HN["hk"]], [pak])
                    for kc in range(KC):
                        mm(pb[:, :128], wvb[:, kc, cg * 128:(cg + 1) * 128], HN["hn"][:, kc, lt:lt + 128], kc == 0, kc == KC - 1, [wbk, HN["hk"]], [pbk])
                    act(st[2][:, :128], pb[:, :128], AF.Sigmoid, [pbk], ["st2"])
                    tt("dve", utl[:, cg, :], pa[:, :128], st[2][:, :128], ALU.mult, [pak, "st2"], ["utl"])
                ring_done()
                ring_done()
                P.dma("sp", u_out[:, :, :], utl[:, :, 98:128], reads=["utl"])
                wc, wck = ring_get(14)
                wvc = wc[:, :].rearrange("p (k c) -> p k c", c=512)
                for kvh in range(2):
                    psb, pk = pj()
                    for kc in range(KC):
                        mm(psb[:, :128], wvc[:, kc, kvh * 128:(kvh + 1) * 128], HN["hn"][:, kc, lt:lt + 128], kc == 0, kc == KC - 1, [wck, HN["hk"]], [pk])
                    knorm(psb, pk, A[4][:, kvh * 128:(kvh + 1) * 128], ak(4, kvh * 128, kvh * 128 + 128), kg, 128)
                    P.dma("sp", k_out[:, kvh, :], A[4][:, kvh * 128:(kvh + 1) * 128], reads=ak(4, kvh * 128, kvh * 128 + 128))
                psb, pk = pj()
                for kc in range(KC):
                    mm(psb[:, :128], HN["hn"][:, kc, lt:lt + 128], wvc[:, kc, 256:384], kc == 0, kc == KC - 1, [wck, HN["hk"]], [pk])
                act(A[4][:, 256:384], psb[:, :128], AF.Copy, [pk], ak(4, 256, 384))
                P.dma("sp", v_out[:, :], A[4][:, 256:384], reads=ak(4, 256, 384))
                ring_done()
        if DBG:
            for nm, tile_, keys in [("d_lbt", lbt, ["lbt"]), ("d_Ltot", Ltot, ["Ltot"]), ("d_Bend", Bend, [("Bend", h_) for h_ in range(4)]),
                                    ("d_D3", D3, [("D3", h_) for h_ in range(4)]), ("d_E3", E3, [("E3", h_) for h_ in range(4)]),
                                    ("d_A0", A[0], ak(0, 0, 4 * TMAX)), ("d_A1", A[1], ak(1, 0, 4 * TMAX)), ("d_A2", A[2], ak(2, 0, 4 * TMAX)),
                                    ("d_A3", A[3], ak(3, 0, 4 * TMAX))]:
                shp = list(tile_.shape)
                dd = dout(nm, [shp[0], int(np.prod(shp[1:]))])
                src = tile_[:] if len(shp) == 2 else (tile_[:].rearrange("p a b -> p (a b)") if len(shp) == 3 else tile_[:].rearrange("p a b c -> p (a b c)"))
                P.dma("sp", dd[:, :], src, reads=keys)
        act(Ltot[:], Ltot[:], AF.Exp, ["Ltot"], ["Ltot"])
        P.dma("sp", f_out[:, :], Ltot[:], reads=["Ltot"])
        P.dma("sp", s_out[:, :], S[:], reads=[("S", q_) for q_ in range(4)])
        P.finish()
        return nc, P

    groups = [[0]] + [list(range(1 + g * TG, 1 + (g + 1) * TG)) for g in range((NT - 1) // TG)]
    group_chunks = [0, 1, 2, 7, 6, 8, 9, 13, 14, 15] + [3, 4, 5, 10, 11, 12, 16, 17, 18] + [19, 20]
    for _ in groups:
        sched.extend(group_chunks)
    for _ in range(3):
        ring_load()

    normed = set()

    def group(gi, tiles, g0, vm_first):
        nt = len(tiles)
        T = nt * 128
        tok0 = tiles[0] * 128
        nchunk = 2 * nt
        hk = ("hT", gi)
        P.stage = "norm"
        if gi not in normed:
            norm(gi, tok0, T)
            normed.add(gi)
        use_hn(gi)
        P.stage = "conv_glu"
        wa, wak = ring_get(0)
        wb, wbk = ring_get(1)
        for cg in range(4):
            pa, pak = pj()
            pb, pbk = pj()
            proj(pa, pak, wa, wak, cg * 128, T)
            proj(pb, pbk, wb, wbk, cg * 128, T)
            act(st[2][:, :T], pb[:, :T], AF.Sigmoid, [pbk], ["st2"])
            if g0:
                tt("dve", st[2][:, :T], st[2][:, :T], vrow[:, :T], ALU.mult, ["st2", "cstf"], ["st2"])
            tt("dve", uT[:, cg, 30:30 + T], pa[:, :T], st[2][:, :T], ALU.mult, [pak, "st2"], [("uT", cg)])
        ring_done()
        ring_done()
        P.stage = "conv_taps"
        for cg in range(4):
            pc, pck = pj()
            for k in range(31):
                mm(pc[:, :T], diag[:, cg * 31 + k, :], uT[:, cg, k:k + T], k == 0, k == 30, ["diag", ("uT", cg), "uT"], [pck])
            yk = ak(0, cg * T, cg * T + T)
            act(yT[:, cg * T:(cg + 1) * T], pc[:, :T], AF.Identity, [pck, "prm"], yk, bias=cb[:, cg:cg + 1])
            act(bfs[:, 0, :T], pc[:, :T], AF.Square, [pck, "prm"], ["bfs0"], bias=cb[:, cg:cg + 1])
            cp("pool", bfs[:, 1, :T], yT[:, cg * T:(cg + 1) * T], yk, ["bfs1"])
            mm(ps[3][:, :T], ones, bfs[:, 1, :T], cg == 0, cg == 3, ["cstb", "bfs1"], ["ps3"])
            mm(ps[4][:, :T], ones, bfs[:, 0, :T], cg == 0, cg == 3, ["cstb", "bfs0"], ["ps4"])
        cp("pool", uT[:, :, 0:30], uT[:, :, T:T + 30], [("uT", c) for c in range(4)], ["uT"])
        act(st[0][:, :T], ps[3][:, :T], AF.Copy, ["ps3"], ["st0"], scale=1.0 / 512)
        tt("dve", st[1][:, :T], st[0][:, :T], st[0][:, :T], ALU.mult, ["st0"], ["st1"])
        stt("dve", st[1][:, :T], ps[4][:, :T], 1.0 / 512, st[1][:, :T], ALU.mult, ALU.subtract, ["ps4", "st1"], ["st1"])
        act(st[1][:, :T], st[1][:, :T], AF.Ln, ["st1"], ["st1"], bias=EPS)
        act(st[1][:, :T], st[1][:, :T], AF.Exp, ["st1"], ["st1"], scale=-0.5)
        P.stage = "conv_gate"
        wg, wgk = ring_get(2)
        for cg in range(4):
            pg, pgk = pj()
            proj(pg, pgk, wg, wgk, cg * 128, T)
            ysl = yT[:, cg * T:(cg + 1) * T]
            yk = ak(0, cg * T, cg * T + T)
            tt("dve", ysl, ysl, st[0][:, :T], ALU.subtract, yk + ["st0"], yk)
            tt("dve", ysl, ysl, st[1][:, :T], ALU.mult, yk + ["st1"], yk)
            act(ysl, ysl, AF.Silu, yk + ["prm"], yk, scale=lng[:, cg:cg + 1], bias=lnb[:, cg:cg + 1])
            act(st[2][:, :T], pg[:, :T], AF.Silu, [pgk], ["st2"])
            tt("dve", brY[0][:, cg, :T], ysl, st[2][:, :T], ALU.mult, yk + ["st2"], [("brY0", cg)])
        ring_done()
        P.stage = "hg_gates"
        wf, wfk = ring_get(7)
        for hb in range(4):
            psb, pk = pj()
            proj(psb, pk, wf, wfk, hb * 128, T)
            hg_gates(psb, pk, hb, T, nchunk, g0)
        ring_done()
        wq, wqk = ring_get(6)
        for hb in range(4):
            sl = slice(hb * T, (hb + 1) * T)
            psb, pk = pj()
            proj(psb, pk, wq, wqk, hb * 128, T)
            act(A[4][:, sl], A[2][:, sl], AF.Exp, ak(2, hb * T, hb * T + T), ak(4, hb * T, hb * T + T))
            act(A[0][:, sl], psb[:, :T], AF.Silu, [pk], ak(0, hb * T, hb * T + T))
            tt("dve", qA[:, hb, :T], A[0][:, sl], A[4][:, sl], ALU.mult, ak(0, hb * T, hb * T + T) + ak(4, hb * T, hb * T + T), [("qA", hb)])
        ring_done()
        P.stage = "hg_vk"
        wi, wik = ring_get(8)
        hg_vtok(wi, wik, nt, T)
        ring_done()
        hg_ktok(nt)
        wgt, wgtk = ring_get(9)
        for hb in range(4):
            psb, pk = pj()
            proj(psb, pk, wgt, wgtk, hb * 128, T)
            act(A[1][:, hb * T:(hb + 1) * T], psb[:, :T], AF.Silu, [pk], ak(1, hb * T, hb * T + T))
        ring_done()
        P.stage = "hg_core"
        for ti in range(nt):
            tsl = slice(ti * 128, (ti + 1) * 128)
            for hb in range(4):
                for half in range(2):
                    c0 = ti * 128 + half * 64
                    mm(ps[4][half * 64:half * 64 + 64, hb * 64:(hb + 1) * 64], kA[:, hb, c0:c0 + 64], qA[:, hb, c0:c0 + 64],
                       True, True, [("kA", hb), ("qA", hb)], ["ps4"])
            tt("dve", scm[:, ti, :, :], ps[4][:, 0:256].rearrange("p (h t) -> p h t", t=64),
               trim.unsqueeze(1).to_broadcast([128, 4, 64]), ALU.mult, ["ps4", "cstb"], [("scm", ti)])
            for half in range(2):
                cj = ti * 2 + half
                rows = slice(half * 64, half * 64 + 64)
                c0 = ti * 128 + half * 64
                for hb in range(4):
                    hs = slice(hb * 128, (hb + 1) * 128)
                    ts("dve", Sbf[:, hs], S[:, hs], E3[:, hb, 1, cj:cj + 1], None, ALU.mult, None, [("S", hb), ("E3", hb)], [("Sbf", hb)])
                for hb in range(4):
                    hs = slice(hb * 128, (hb + 1) * 128)
                    osl = ps[6][:, hb * 128 + half * 64: hb * 128 + half * 64 + 64]
                    mm(osl, vtok[rows, ti, hs], scm[rows, ti, hb, :], True, False, [("vtok", ti), ("scm", ti)], ["ps6"])
                    mm(osl, Sbf[:, hs], qA[:, hb, c0:c0 + 64], False, True, [("Sbf", hb), ("qA", hb)], ["ps6"])
                hg_state(ti, half)
            bff = bfs[:, :, :].rearrange("p a t -> p (a t)")[:, 0:512]
            a3k = ak(3, 0, 512)
            act(bff, ps[6][:, :], AF.Square, ["ps6"], ["bfs0", "bfs1"])
            mm(ps[3][:, :], ones, bff, True, True, ["cstb", "bfs0", "bfs1"], ["ps3"])
            act(A[3][:, 0:512], ps[3][:, :], AF.Ln, ["ps3"], a3k, scale=1.0 / 128, bias=EPS)
            act(A[3][:, 0:512], A[3][:, 0:512], AF.Exp, a3k, a3k, scale=-0.5)
            tt("dve", A[3][:, 0:512], ps[6][:, :], A[3][:, 0:512], ALU.mult, ["ps6"] + a3k, a3k)
            for hb in range(4):
                g0_, g1_ = hb * T + ti * 128, hb * T + (ti + 1) * 128
                stt("dve", brY[1][:, hb, tsl], A[3][:, hb * 128:(hb + 1) * 128], gng, A[1][:, g0_:g1_],
                    ALU.mult, ALU.mult, a3k + ["prm"] + ak(1, g0_, g1_), [("brY1", hb)])
        P.stage = "att_proj"
        wq2, wq2k = ring_get(13)
        for qb in range(4):
            psb, pk = pj()
            proj(psb, pk, wq2, wq2k, qb * 128, T)
            knorm(psb, pk, qn[:, qb, :T], [("qn", qb)], qg, T)
        ring_done()
        wc, wck = ring_get(14)
        wvc = wc[:, :].rearrange("p (k c) -> p k c", c=512)
        for kvh in range(2):
            psb, pk = pj()
            proj(psb, pk, wc, wck, kvh * 128, T)
            knorm(psb, pk, KT[:, kvh, 128:128 + T], [("KT", kvh)], kg, T)
        for ti in range(nt):
            psb, pk = pj()
            for kc in range(KC):
                mm(psb[:, :128], HN["hn"][:, kc, ti * 128:(ti + 1) * 128], wvc[:, kc, 256:384], kc == 0, kc == KC - 1, [wck, HN["hk"]], [pk])
            act(Vaug[:, 1 + ti, :, 0:64], psb[:, 0:128].rearrange("p (h d) -> p h d", d=64), AF.Copy, [pk], [("Vaug", 1 + ti)])
            if g0:
                ts("dve", Vaug[:, 1 + ti, :, 0:64], Vaug[:, 1 + ti, :, 0:64], vcol, None, ALU.mult, None, [("Vaug", 1 + ti), "cstf"], [("Vaug", 1 + ti)])
                cp("dve", Vaug[:, 1 + ti, :, 64:128], vcol.unsqueeze(1).to_broadcast([128, 2, 64]), [("Vaug", 1 + ti), "cstf"], [("Vaug", 1 + ti)])
            else:
                ms("dve", Vaug[:, 1 + ti, :, 64:128], 1.0, [("Vaug", 1 + ti)])
        ring_done()
        wgc, wgck = ring_get(15)
        for qb in range(4):
            psb, pk = pj()
            proj(psb, pk, wgc, wgck, qb * 128, T)
            act(A[1][:, qb * T:(qb + 1) * T], psb[:, :T], AF.Silu, [pk], ak(1, qb * T, qb * T + T))
        ring_done()
        P.stage = "att_core"
        for ti in range(nt):
            tsl = slice(ti * 128, (ti + 1) * 128)
            hist = slice(ti * 128, (ti + 1) * 128)
            cur = slice((ti + 1) * 128, (ti + 2) * 128)
            if g0:
                km, kmk, vm, vmk = KMz, "KMz", VMs, "VMs"
            elif vm_first and ti == 0:
                km, kmk, vm, vmk = KM, "KM", VM0, "VM0"
            else:
                km, kmk, vm, vmk = KM, "KM", VM, "VMall"
            for kvh in range(2):
                for g in range(4):
                    hq = kvh * 4 + g
                    qb = hq // 2
                    r = slice((hq % 2) * 64, (hq % 2) * 64 + 64)
                    mm(ps[5][:, g * 128:(g + 1) * 128], KT[r, kvh, hist], qn[r, qb, tsl], True, True, [("KT", kvh), "KThist", ("qn", qb)], ["ps5"])
                    mm(ps[6][:, g * 128:(g + 1) * 128], KT[r, kvh, cur], qn[r, qb, tsl], True, True, [("KT", kvh), ("qn", qb)], ["ps6"])
                    mm(ps[4][0:32, g * 128:(g + 1) * 128], km[r, kvh, :], qn[r, qb, tsl], True, True, [kmk, ("qn", qb)], ["ps4"])
                act(pA[:, :, :], ps[5][:, :].rearrange("p (g t) -> p g t", t=128), AF.Exp, ["ps5"], ["pA"], scale=0.125)
                act(pB[:, :, :], ps[6][:, :].rearrange("p (g t) -> p g t", t=128), AF.Exp, ["ps6"], ["pB"], scale=0.125)
                act(pM[:, :, :], ps[4][0:32, :].rearrange("p (g t) -> p g t", t=128), AF.Exp, ["ps4"], ["pM"], scale=0.125)
                for g in range(4):
                    hq = kvh * 4 + g
                    o0 = ps[3][:, g * 128:g * 128 + 64]
                    o1 = ps[3][:, g * 128 + 64:g * 128 + 128]
                    mm(o0, Vaug[:, ti, kvh, :], pA[:, g, 0:64], True, False, [("Vaug", ti), "pA"], ["ps3"])
                    mm(o0, Vaug[0:64, ti + 1, kvh, :], pB[0:64, g, 0:64], False, False, [("Vaug", ti + 1), "pB"], ["ps3"])
                    vmr = ["VM"] + [("VMd", q_) for q_ in range(8)] if vmk == "VMall" else [vmk]
                    mm(o0, vm[0:32, hq, :], pM[0:32, g, 0:64], False, True, vmr + ["pM"], ["ps3"])
                    mm(o1, Vaug[64:128, ti, kvh, :], pA[64:128, g, 64:128], True, False, [("Vaug", ti), "pA"], ["ps3"])
                    mm(o1, Vaug[:, ti + 1, kvh, :], pB[:, g, 64:128], False, False, [("Vaug", ti + 1), "pB"], ["ps3"])
                    mm(o1, vm[0:32, hq, :], pM[0:32, g, 64:128], False, True, vmr + ["pM"], ["ps3"])
                P.add("dve", lambda e: e.reciprocal(out=rden[64:128, :, :], in_=ps[3][64:128, :].rearrange("p (g t) -> p g t", t=128)),
                      ["ps3"], ["rden"])
                pv = ps[3][0:64, :].rearrange("p (g t) -> p g t", t=128)
                for par in range(2):
                    orow = slice(par * 64, par * 64 + 64)
                    qb0 = kvh * 2
                    tt("dve", ao[orow, :, :], pv[:, par::2, :], rden[64:128, par::2, :], ALU.mult, ["ps3", "rden"], [("ao", par)])
                    gk = ak(1, qb0 * T, (qb0 + 2) * T)
                    tt("dve", brY[2][orow, qb0:qb0 + 2, tsl], ao[orow, :, :],
                       A[1][orow, :].rearrange("p (q t) -> p q t", t=T)[:, qb0:qb0 + 2, tsl], ALU.mult,
                       [("ao", par)] + gk, [("brY2", qb0), ("brY2", qb0 + 1)])
        cp("pool", KT[:, :, 0:128], KT[:, :, T:T + 128], [("KT", 0), ("KT", 1)], ["KThist"])
        cp("pool", Vaug[:, 0, :, :], Vaug[:, nt, :, :], [("Vaug", nt)], [("Vaug", 0)])
        if gi + 1 < len(groups):
            P.stage = "norm"
            nt2 = groups[gi + 1]
            norm(gi + 1, nt2[0] * 128, len(nt2) * 128)
            normed.add(gi + 1)
        P.stage = "mix"
        wga = [ring_get(3), ring_get(4)]
        wco = ring_get(5)
        for br, (gch, och) in enumerate([((3, 4), 5), ((10, 11), 12), ((16, 17), 18)]):
            if br > 0:
                wga = [ring_get(gch[0]), ring_get(gch[1])]
                wco = ring_get(och)
            wov = wco[0][:, :].rearrange("p (k c) -> p k c", c=1024)
            for ob in range(8):
                pg, pgk = pj()
                wg_, wgk_ = wga[ob // 4]
                proj(pg, pgk, wg_, wgk_, (ob % 4) * 128, T)
                pz, pzk = pj()
                for kc4 in range(4):
                    mm(pz[:, :T], wov[:, kc4, ob * 128:(ob + 1) * 128], brY[br][:, kc4, :T], kc4 == 0, kc4 == 3,
                       [wco[1], ("brY%d" % br, kc4)], [pzk])
                act(st[2][:, :T], pg[:, :T], AF.Sigmoid, [pgk], ["st2"])
                msl = A[2 + (ob % 2)][:, (ob // 2) * T:(ob // 2) * T + T]
                mk = ak(2 + ob % 2, (ob // 2) * T, (ob // 2) * T + T)
                if br == 0:
                    tt("dve", msl, pz[:, :T], st[2][:, :T], ALU.mult, [pzk, "st2"], mk)
                elif br == 1:
                    tt("dve", st[3][:, :T], pz[:, :T], st[2][:, :T], ALU.mult, [pzk, "st2"], ["st3"])
                    tt("pool", msl, msl, st[3][:, :T], ALU.add, mk + ["st3"], mk)
                else:
                    tt("dve", st[3][:, :T], pz[:, :T], st[2][:, :T], ALU.mult, [pzk, "st2"], ["st3"])
                    tt("pool", mixb[:, ob, :T], msl, st[3][:, :T], ALU.add, mk + ["st3"], ["sqb"])
            ring_done()
            ring_done()
            ring_done()
        P.stage = "out"
        wo = [ring_get(19), ring_get(20)]
        for ob2 in range(8):
            psb, pk = pj()
            w_, wk_ = wo[ob2 // 4]
            wv = w_[:, :].rearrange("p (k c) -> p k c", c=512)
            for kc in range(KC):
                mm(psb[:, :T], wv[:, kc, (ob2 % 4) * 128:(ob2 % 4 + 1) * 128], mixb[:, kc, :T], kc == 0, kc == KC - 1, [wk_, "sqb"], [pk])
            tt("dve", hT[:, ob2, tok0:tok0 + T], hT[:, ob2, tok0:tok0 + T], psb[:, :T], ALU.add, [hk, pk], [hk])
        ring_done()
        ring_done()

    group(0, groups[0], True, False)
    if DBG:
        for i_ in range(3):
            dd = dout("d_brY%d" % i_, [128, 4 * TMAX])
            P.dma("pool", dd[:, :], brY[i_][:].rearrange("p a t -> p (a t)"), reads=[("brY%d" % i_, q_) for q_ in range(4)])
        dd = dout("d_uT", [128, 4 * (30 + TMAX)])
        P.dma("pool", dd[:, :], uT[:].rearrange("p a t -> p (a t)"), reads=["uT"] + [("uT", q_) for q_ in range(4)])
        dd = dout("d_mixb", [128, KC * TMAX])
        P.dma("pool", dd[:, :], sqb[:].rearrange("p a t -> p (a t)"), reads=["sqb"])
    f0 = cmask[:, 8:9]
    omf0 = cmask[:, 9:10]
    cp("dve", KM[:, :, :], KT[:, :, 96:128], ["KThist"], ["KM"])
    ms("dve", KM[:, :, 0:1], 0.0, ["KM"])
    cp("dve", VM[:, :, :], VMs[:, :, :], ["VMs"], ["VM"])
    for hq in range(8):
        P.dma("sp", VM[16:32, hq, :], Vaug[112:128, 0, hq // 4, :], reads=[("Vaug", 0), "VM"], writes=[("VMd", hq)])
    ms("dve", lbt[0:32, 56:57], 1.0, ["lbt2"])
    P.dma("sp", lbt[16:32, 56:57], cmask[16:32, 9:10], reads=["cmask", "lbt2"], writes=["lbt3"])
    ts("dve", VM0[:, :, :], VM[:, :, :], lbt[0:32, 56:57], None, ALU.mult, None, ["VM", "lbt2", "lbt3"] + [("VMd", q_) for q_ in range(8)], ["VM0"])
    EXK = ak(4, 0, 512)
    SKEYS = [("S", q_) for q_ in range(4)]
    P.dma("sp", exs[:, 0:120], uex_d.rearrange("p c t -> p (c t)"), writes=EXK)
    ts("dve", uT[:, :, 0:30], uT[:, :, 0:30], f0, None, ALU.mult, None, ["uT", "cmask"], ["uT"])
    stt("dve", uT[:, :, 0:30], exs[:, 0:120].rearrange("p (c t) -> p c t", t=30), omf0, uT[:, :, 0:30], ALU.mult, ALU.add,
        EXK + ["cmask", "uT"], ["uT"])
    P.dma("sp", exs[:, 128:384], kex_d.rearrange("p c t -> p (c t)"), writes=EXK)
    ts("dve", KT[:, :, 0:128], KT[:, :, 0:128], f0, None, ALU.mult, None, ["KThist", "cmask"], ["KThist"])
    stt("dve", KT[:, :, 0:128], exs[:, 128:384].rearrange("p (c t) -> p c t", t=128), omf0, KT[:, :, 0:128], ALU.mult, ALU.add,
        EXK + ["cmask", "KThist"], ["KThist"])
    P.dma("sp", exs[:, 384:512], vex_d[:, :], writes=EXK)
    ts("dve", Vaug[:, 0, :, 0:64], Vaug[:, 0, :, 0:64], f0, None, ALU.mult, None, [("Vaug", 0), "cmask"], [("Vaug", 0)])
    stt("dve", Vaug[:, 0, :, 0:64], exs[:, 384:512].rearrange("p (h d) -> p h d", d=64), omf0, Vaug[:, 0, :, 0:64], ALU.mult, ALU.add,
        EXK + ["cmask", ("Vaug", 0)], [("Vaug", 0)])
    ts("dve", Vaug[:, 0, :, 64:128], Vaug[:, 0, :, 64:128], f0, omf0, ALU.mult, ALU.add, [("Vaug", 0), "cmask"], [("Vaug", 0)])
    P.dma("sp", Fall[:, :, :], fall_d.rearrange("c p h -> p c h"), writes=["Fall"])
    for j in range(NCORES - 1):
        mj = cmask[:, j:j + 1]
        P.dma("sp", exs[:, :], sall_d[j], writes=EXK)
        ts("dve", lbt[:, 60:64], Fall[:, j, :], -1.0, None, ALU.add, None, ["Fall"], ["lbt4"])
        ts("dve", lbt[:, 60:64], lbt[:, 60:64], mj, onecol, ALU.mult, ALU.add, ["lbt4", "cmask", "cstf"], ["lbt4"])
        tt("dve", S[:, :].rearrange("p (h v) -> p h v", v=128), S[:, :].rearrange("p (h v) -> p h v", v=128),
           lbt[:, 60:64].unsqueeze(2).to_broadcast([128, 4, 128]), ALU.mult, SKEYS + ["lbt4"], SKEYS)
        stt("dve", S[:, :], exs[:, :], mj, S[:, :], ALU.mult, ALU.add, EXK + ["cmask"] + SKEYS, SKEYS)
    for gi in range(1, len(groups)):
        group(gi, groups[gi], False, gi == 1)
    for kc in range(KC):
        P.dma("sp", h_out[:, kc, :], hT[:, kc, :], reads=[("hT", g) for g in range(len(groups))])
    P.finish()
    return nc, P


def _chunk1024(W):
    return np.ascontiguousarray(W.reshape(8, 128, 512).transpose(1, 0, 2).reshape(128, CW))


def _chunk512(W):
    return np.ascontiguousarray(W.reshape(4, 128, 1024).transpose(1, 0, 2).reshape(128, CW))


def _layer_chunks(inp, l):
    w = inp["w_in"][l]
    ch = np.zeros((NCH, 128, CW), np.float32)
    def c(i, a, b):
        ch[i] = _chunk1024(w[:, a:b])
    c(0, 0, 512); c(1, 512, 1024); c(2, 1024, 1536)
    c(3, 4864, 5376); c(4, 5376, 5888)
    ch[5] = _chunk512(inp["w_conv_out"][l])
    c(6, 1536, 2048); c(7, 2048, 2560); c(8, 2560, 3072); c(9, 3072, 3584)
    c(10, 5888, 6400); c(11, 6400, 6912)
    ch[12] = _chunk512(inp["w_hg_out"][l])
    c(13, 3584, 4096)
    ckv = np.zeros((1024, 512), np.float32)
    ckv[:, 0:64] = w[:, 4096:4160]; ckv[:, 64:128] = w[:, 4096:4160]
    ckv[:, 128:192] = w[:, 4160:4224]; ckv[:, 192:256] = w[:, 4160:4224]
    ckv[:, 256:384] = w[:, 4224:4352]
    ch[14] = _chunk1024(ckv)
    c(15, 4352, 4864); c(16, 6912, 7424); c(17, 7424, 7936)
    ch[18] = _chunk512(inp["w_att_out"][l])
    ch[19] = _chunk1024(inp["w_out"][l][:, 0:512]); ch[20] = _chunk1024(inp["w_out"][l][:, 512:1024])
    return ch


def _layer_prm(inp, l):
    p = np.zeros((128, NPRM), np.float32)
    p[:, P_NG:P_NG + 8] = inp["norm_g"][l].reshape(8, 128).T
    hlb = inp["hg_lower_bounds"]
    p[:, P_LBRAW:P_LBRAW + 16] = hlb.reshape(4, 4, 128).transpose(2, 1, 0).reshape(128, 16)
    p[:, P_LSEL + l] = 1.0
    p[:, P_KG] = np.tile(inp["k_norm_g"][l], 2)
    p[:, P_QG] = np.tile(inp["q_norm_g"][l], 2)
    p[:, P_GNG] = inp["hg_norm_g"][l]
    p[:, P_CB:P_CB + 4] = inp["conv_b"][l].reshape(4, 128).T
    p[:, P_LNG:P_LNG + 4] = inp["conv_ln_g"][l].reshape(4, 128).T
    p[:, P_LNB:P_LNB + 4] = inp["conv_ln_b"][l].reshape(4, 128).T
    p[:, P_CW:P_CW + 124] = inp["conv_w"][l].reshape(31, 4, 128).transpose(2, 1, 0).reshape(128, 124)
    return p


def _consts():
    c = np.zeros((128, NCST), np.float32)
    c[:, C_ID:C_ID + 128] = np.eye(128)
    c[:, C_ONES:C_ONES + 128] = 1.0
    c[0:64, C_BONES:C_BONES + 64] = 1.0
    c[64:128, C_BONES + 64:C_BONES + 128] = 1.0
    s = np.arange(128)[:, None] % 64
    t = np.arange(64)[None, :]
    c[:, C_TRI:C_TRI + 64] = (s <= t)
    c[:, C_VROW + 112:C_VROW + 128] = 1.0
    c[112:128, C_VCOL] = 1.0
    return c


_CACHE = {}


def _prog(phase):
    if phase not in _CACHE:
        _CACHE[phase] = build(phase)[0]
    return _CACHE[phase]


def kernel(x, meta_tokens, norm_g, w_in, conv_w, conv_b, conv_ln_g, conv_ln_b, w_conv_out,
           hg_lower_bounds, hg_norm_g, w_hg_out, q_norm_g, k_norm_g, attn_sinks, w_att_out, w_out):
    inp = dict(x=x, meta_tokens=meta_tokens, norm_g=norm_g, w_in=w_in, conv_w=conv_w, conv_b=conv_b,
               conv_ln_g=conv_ln_g, conv_ln_b=conv_ln_b, w_conv_out=w_conv_out, hg_lower_bounds=hg_lower_bounds,
               hg_norm_g=hg_norm_g, w_hg_out=w_hg_out, q_norm_g=q_norm_g, k_norm_g=k_norm_g, attn_sinks=attn_sinks,
               w_att_out=w_att_out, w_out=w_out)
    inp = {k: np.asarray(v, np.float32) for k, v in inp.items()}
    xs = inp["x"][0]
    cst = _consts()
    hTs = []
    for c in range(NCORES):
        h = np.zeros((TOK, D), np.float32)
        h[112:128] = inp["meta_tokens"]
        h[128:] = xs[c * OWN:(c + 1) * OWN]
        hTs.append(np.ascontiguousarray(h.T.reshape(KC, 128, TOK).transpose(1, 0, 2)))
    cmasks = []
    for c in range(NCORES):
        m = np.zeros((128, 16), np.float32)
        m[:, 0:8] = (np.arange(8) < c)[None, :]
        m[:, 8] = 1.0 if c == 0 else 0.0
        m[:, 9] = 0.0 if c == 0 else 1.0
        cmasks.append(m)
    ncA = _prog("A")
    ncB = _prog("B")
    cores = list(range(NCORES))
    for l in range(DEPTH):
        ch = _layer_chunks(inp, l)
        prm = _layer_prm(inp, l)
        chA = np.ascontiguousarray(ch[[7, 8, 0, 1, 14]])
        inA = [dict(hT=hTs[c], wch=chA, prm=prm, cst=cst, cmask=cmasks[c]) for c in cores]
        rA = run_bass_kernel_spmd(ncA, inA, core_ids=cores).results
        Sall = np.ascontiguousarray(np.stack([rA[c]["Sloc"] for c in cores]))
        Fall = np.ascontiguousarray(np.stack([rA[c]["Floc"] for c in cores]))
        sk = np.zeros((32, 8), np.float32)
        sk[0] = inp["attn_sinks"][l]
        inB = []
        for c in cores:
            p = max(c - 1, 0)
            inB.append(dict(hT=hTs[c], wch=ch, prm=prm, cst=cst, cmask=cmasks[c], sinks=sk, Sall=Sall, Fall=Fall,
                            uex=rA[p]["utail"], kex=rA[p]["ktail"], vex=rA[p]["vtail"]))
        rB = run_bass_kernel_spmd(ncB, inB, core_ids=cores).results
        hTs = [rB[c]["hT_out"] for c in cores]
    out = np.zeros((1, SEQ, D), np.float32)
    for c in cores:
        hc = hTs[c].transpose(1, 0, 2).reshape(D, TOK).T
        out[0, c * OWN:(c + 1) * OWN] = hc[128:]
    return out
```

```python
import contextlib
import numpy as np
import concourse.bass as bass
import concourse.mybir as mybir
from concourse.bass_utils import run_bass_kernel_spmd

F32 = mybir.dt.float32
BF16 = mybir.dt.bfloat16
AF = mybir.ActivationFunctionType
ALU = mybir.AluOpType
AX = mybir.AxisListType

NCORES = 8
D = 1024
KC = 8
SEQ = 16384
OWN = SEQ // NCORES
NT = 1 + OWN // 128
TOK = NT * 128
TG = 2
TG_A = 4
TMAX = TG * 128
DEPTH = 4
EPS = 1e-6
FL = 1e-30
NCH = 21
CW = 4096
DBG = False
ENGS = ("pe", "act", "dve", "pool", "sp")

P_NG, P_LBRAW, P_LSEL, P_KG, P_QG, P_GNG, P_CB, P_LNG, P_LNB, P_CW = 0, 8, 24, 28, 29, 30, 31, 35, 39, 43
NPRM = 43 + 124
C_ID, C_ONES, C_BONES, C_TRI, C_VROW, C_VCOL = 0, 128, 256, 384, 448, 576
NCST = 577


class _Op:
    __slots__ = ("eng", "fn", "reads", "writes", "dma", "idx", "waits", "flag", "sem", "val", "stage")


class Prog:
    def __init__(self, nc):
        self.nc = nc
        self.ops = []
        self.stack = contextlib.ExitStack()
        self.n_dma_sems = {"sp": 16, "pool": 12}

    def sb(self, name, shape, dtype):
        return self.stack.enter_context(self.nc.sbuf_tensor(name, list(shape), dtype))

    def ps(self, name, shape, dtype=F32):
        return self.stack.enter_context(self.nc.psum_tensor(name, list(shape), dtype))

    def add(self, eng, fn, reads=(), writes=(), dma=False):
        op = _Op()
        op.eng, op.fn, op.reads, op.writes, op.dma = eng, fn, tuple(reads), tuple(writes), dma
        op.idx = len(self.ops)
        op.stage = getattr(self, "stage", "")
        op.flag = dma
        self.ops.append(op)
        return op

    def dma(self, eng, out, in_, reads=(), writes=()):
        return self.add(eng, lambda e: e.dma_start(out=out, in_=in_), reads, writes, dma=True)

    def finish(self):
        nc, ops = self.nc, self.ops
        last_w, rdrs, deps_all = {}, {}, []
        for op in ops:
            deps = set()
            raw = set()
            for k in op.reads:
                w = last_w.get(k)
                if w is not None:
                    deps.add(w)
                    raw.add(w)
            for k in op.writes:
                w = last_w.get(k)
                if w is not None:
                    deps.add(w)
                r = rdrs.get(k)
                if r:
                    deps.update(r)
            deps.discard(op.idx)
            need = []
            for d in deps:
                dop = ops[d]
                if dop.eng == op.eng and not dop.dma and not op.dma:
                    if op.eng == "pe" or d not in raw:
                        continue
                need.append(d)
                dop.flag = True
            deps_all.append(need)
            for k in op.reads:
                rdrs.setdefault(k, []).append(op.idx)
            for k in op.writes:
                last_w[k] = op.idx
                rdrs[k] = []
        sem_eng = {e: self.stack.enter_context(nc.semaphore("s_" + e)) for e in ENGS if e != "sp"}
        dma_sems = {q: [self.stack.enter_context(nc.semaphore("d_%s%d" % (q, i))) for i in range(n)]
                    for q, n in self.n_dma_sems.items()}
        cnt = {e: 0 for e in ENGS}
        dma_rr = {q: 0 for q in dma_sems}
        dma_val, dma_prev = {}, {}
        for op in ops:
            if op.dma:
                q = op.eng
                i = dma_rr[q] % len(dma_sems[q])
                dma_rr[q] += 1
                v = dma_val.get((q, i), 0) + 16
                dma_val[(q, i)] = v
                op.sem, op.val = dma_sems[q][i], v
                p = dma_prev.get((q, i))
                if p is not None:
                    deps_all[op.idx].append(p)
                dma_prev[(q, i)] = op.idx
            elif op.flag:
                cnt[op.eng] += 1
                op.sem, op.val = sem_eng[op.eng], cnt[op.eng]
            else:
                op.sem = op.val = None
        seen = {e: {} for e in ENGS}
        for op in ops:
            best = {}
            for d in deps_all[op.idx]:
                dop = ops[d]
                key = id(dop.sem)
                if key not in best or best[key][1] < dop.val:
                    best[key] = (dop.sem, dop.val)
            sn = seen[op.eng]
            waits = []
            for key, (sem, val) in best.items():
                if sn.get(key, 0) >= val:
                    continue
                sn[key] = val
                waits.append((sem, val))
            op.waits = waits
        per = {e: [o for o in ops if o.eng == e] for e in ENGS}
        final = [(dma_sems[q][i], v) for (q, i), v in dma_val.items()]
        self.stats = {e: len(per[e]) for e in ENGS}
        self.stats["waits"] = sum(len(o.waits) for o in ops)

        def emit(e, name):
            for op in per[name]:
                for sem, val in op.waits:
                    e.wait_ge(sem, val)
                ins = op.fn(e)
                if op.sem is not None:
                    ins.then_inc(op.sem, 16 if op.dma else 1)
            if name == "sp":
                for sem, val in final:
                    e.wait_ge(sem, val)

        with nc.Block() as block:
            @block.tensor
            def _(e):
                emit(e, "pe")

            @block.scalar
            def _(e):
                emit(e, "act")

            @block.vector
            def _(e):
                emit(e, "dve")

            @block.gpsimd
            def _(e):
                emit(e, "pool")

            @block.sync
            def _(e):
                emit(e, "sp")
        self.stack.close()


def build(phase):
    nc = bass.Bass("TRN2", target_bir_lowering=False)
    P = Prog(nc)
    TGL = TG_A if phase == "A" else TG
    TML = TGL * 128

    def din(name, shape, dt=F32):
        return nc.dram_tensor(name, list(shape), dt, kind="ExternalInput").ap()

    def dout(name, shape, dt=F32):
        return nc.dram_tensor(name, list(shape), dt, kind="ExternalOutput").ap()

    h_in = din("hT", [128, KC, TOK])
    nw = NCH if phase == "B" else 5
    w_in_d = din("wch", [nw, 128, CW])
    prm_d = din("prm", [128, NPRM])
    cst_d = din("cst", [128, NCST])
    cmask_d = din("cmask", [128, 16])
    wscr = nc.dram_tensor("wscr", [nw, 128, CW], BF16).ap()
    if phase == "B":
        sinks_d = din("sinks", [32, 8])
        sall_d = din("Sall", [NCORES, 128, 512])
        fall_d = din("Fall", [NCORES, 128, 4])
        uex_d = din("uex", [128, 4, 30])
        kex_d = din("kex", [128, 2, 128])
        vex_d = din("vex", [128, 128])
        h_out = dout("hT_out", [128, KC, TOK])
    else:
        s_out = dout("Sloc", [128, 512])
        f_out = dout("Floc", [128, 4])
        u_out = dout("utail", [128, 4, 30])
        k_out = dout("ktail", [128, 2, 128])
        v_out = dout("vtail", [128, 128])
    if phase == "A":
        cmap = {7: 0, 8: 1, 0: 2, 1: 3, 14: 4}
    else:
        cmap = {i: i for i in range(NCH)}

    T = TML
    hT = P.sb("hTs", [128, KC, TOK], F32)
    hnTs = [P.sb("hnT%d" % i, [128, KC, T], BF16) for i in range(2)]
    HN = {"hn": hnTs[0], "hk": ("hnT", 0)}
    sqb = P.sb("sqb", [128, KC, T], BF16)
    wr = [P.sb("wr%d" % i, [128, CW], BF16) for i in range(3)]
    prm = P.sb("prm_s", [128, NPRM], F32)
    cstf = P.sb("cstf", [128, NCST], F32)
    cstb = P.sb("cstb", [128, NCST], BF16)
    cmask = P.sb("cmask_s", [128, 16], F32)
    A = [P.sb("A%d" % i, [128, 4 * T], F32) for i in range(5)]
    st = [P.sb("st%d" % i, [128, T], F32) for i in range(4)]
    lbt = P.sb("lbt", [128, 64], F32)
    kA = P.sb("kA", [128, 4, T], BF16)
    kAtok = P.sb("kAtok", [128, TGL, 512], BF16)
    vtok = P.sb("vtok", [128, TGL, 512], BF16)
    S = P.sb("S", [128, 512], F32)
    Stmp = P.sb("Stmp", [128, 512], F32)
    E3 = P.sb("E3", [128, 4, 3, 2 * TGL], F32)
    D3 = P.sb("D3", [128, 4, 3, 2 * TGL], F32)
    Bend = P.sb("Bend", [128, 4, 2 * TGL + 1], F32)
    Bmid = P.sb("Bmid", [128, 4, 2 * TGL], F32)
    Ltot = P.sb("Ltot", [128, 4], F32)
    bfs = P.sb("bfs", [128, 2, T], BF16)
    if phase == "B":
        diag = P.sb("diag", [128, 124, 128], BF16)
        uT = P.sb("uT", [128, 4, 30 + T], BF16)
        yT = A[0]
        brY = [P.sb("brY%d" % i, [128, 4, T], BF16) for i in range(3)]
        qA = P.sb("qA", [128, 4, T], BF16)
        scm = P.sb("scm", [128, TGL, 4, 64], BF16)
        Sbf = P.sb("Sbf", [128, 512], BF16)
        qn = P.sb("qn", [128, 4, T], BF16)
        KT = P.sb("KT", [128, 2, 128 + T], BF16)
        Vaug = P.sb("Vaug", [128, TGL + 1, 2, 128], BF16)
        KM = P.sb("KM", [128, 2, 32], BF16)
        KMz = P.sb("KMz", [128, 2, 32], BF16)
        VM = P.sb("VM", [32, 8, 128], BF16)
        VM0 = P.sb("VM0", [32, 8, 128], BF16)
        VMs = P.sb("VMs", [32, 8, 128], BF16)
        sinks = P.sb("sinks_s", [32, 8], F32)
        pA = P.sb("pA", [128, 4, 128], BF16)
        pB = P.sb("pB", [128, 4, 128], BF16)
        pM = P.sb("pM", [32, 4, 128], BF16)
        rden = P.sb("rden", [128, 4, 128], F32)
        ao = P.sb("ao", [128, 2, 128], F32)
        exs = A[4][:, 0:512]
        Fall = P.sb("Fall_s", [128, NCORES, 4], F32)
        mixb = sqb
    else:
        utl = P.sb("utl", [128, 4, 128], F32)
    ps = [P.ps("ps%d" % i, [128, 512], F32) for i in range(7)]
    pst = P.ps("pst", [128, 512], BF16)
    pjc = [0]

    def pj():
        i = pjc[0] % 3
        pjc[0] += 1
        return ps[i], "ps%d" % i

    def ak(i, lo, hi):
        return [("A", i, u) for u in range(lo // 128, (hi + 127) // 128)]

    ident = cstb[:, C_ID:C_ID + 128]
    ones = cstb[:, C_ONES:C_ONES + 128]
    bones = cstb[:, C_BONES:C_BONES + 128]
    trim = cstb[:, C_TRI:C_TRI + 64]
    vrow = cstf[:, C_VROW:C_VROW + 128]
    vcol = cstf[:, C_VCOL:C_VCOL + 1]
    onecol = cstf[:, C_ONES:C_ONES + 1]

    def act(out, in_, func, reads, writes, scale=1.0, bias=0.0):
        P.add("act", lambda e: e.activation(out=out, in_=in_, func=func, bias=bias, scale=scale), reads, writes)

    def tt(eng, out, in0, in1, op, reads, writes):
        P.add(eng, lambda e: e.tensor_tensor(out=out, in0=in0, in1=in1, op=op), reads, writes)

    def ts(eng, out, in0, s1, s2, op0, op1, reads, writes):
        if op1 is None:
            P.add(eng, lambda e: e.tensor_scalar(out=out, in0=in0, scalar1=s1, scalar2=None, op0=op0), reads, writes)
        else:
            P.add(eng, lambda e: e.tensor_scalar(out=out, in0=in0, scalar1=s1, scalar2=s2, op0=op0, op1=op1), reads, writes)

    def stt(eng, out, in0, scalar, in1, op0, op1, reads, writes):
        P.add(eng, lambda e: e.scalar_tensor_tensor(out=out, in0=in0, scalar=scalar, in1=in1, op0=op0, op1=op1), reads, writes)

    def cp(eng, out, in_, reads, writes):
        P.add(eng, lambda e: e.tensor_copy(out=out, in_=in_), reads, writes)

    def mm(out, lhsT, rhs, start, stop, reads, writes):
        P.add("pe", lambda e: e.matmul(out, lhsT=lhsT, rhs=rhs, start=start, stop=stop), reads, writes)

    def tr(out, in_, reads, writes):
        P.add("pe", lambda e: e.transpose(out, in_, ident), reads, writes)

    def ms(eng, ap, val, writes):
        P.add(eng, lambda e: e.memset(ap, val), (), writes)

    P.dma("sp", prm[:], prm_d[:, :], writes=["prm"])
    P.dma("sp", cstf[:], cst_d[:, :], writes=["cstf"])
    P.dma("pool", cstb[:], cst_d[:, :], writes=["cstb"])
    P.dma("sp", cmask[:], cmask_d[:, :], writes=["cmask"])
    for kc in range(KC):
        P.dma("sp", hT[:, kc, :], h_in[:, kc, :], writes=[("hTl", kc)])
    if phase == "A":
        order = [7, 8, 0, 1, 14]
    else:
        order = list(range(NCH))
    for ci in order:
        P.dma("pool", wscr[cmap[ci]], w_in_d[cmap[ci]], writes=[("wscr", ci)])

    sched = []
    ring = {"next_load": 0, "next_use": 0}

    def ring_load():
        i = ring["next_load"]
        if i < len(sched):
            ci = sched[i]
            slot = i % 3
            P.dma("sp", wr[slot][:, :], wscr[cmap[ci]], reads=[("wscr", ci)], writes=[("wr", slot)])
            ring["next_load"] += 1

    def ring_get(ci):
        i = ring["next_use"]
        assert sched[i] == ci, (sched[i], ci, i)
        ring["next_use"] += 1
        slot = i % 3
        return wr[slot], ("wr", slot)

    def ring_done():
        ring_load()

    ng = prm[:, P_NG:P_NG + 8]
    lbraw = prm[:, P_LBRAW:P_LBRAW + 16].rearrange("p (h l) -> p h l", l=4)
    lsel = prm[:, P_LSEL:P_LSEL + 4]
    kg = prm[:, P_KG:P_KG + 1]
    qg = prm[:, P_QG:P_QG + 1]
    gng = prm[:, P_GNG:P_GNG + 1]
    cb = prm[:, P_CB:P_CB + 4]
    lng = prm[:, P_LNG:P_LNG + 4]
    lnb = prm[:, P_LNB:P_LNB + 4]
    cwv = prm[:, P_CW:P_CW + 124].rearrange("p (c k) -> p c k", k=31)
    mx = lbt[:, 0:4]
    ex = lbt[:, 4:20].rearrange("p (h l) -> p h l", l=4)
    sm = lbt[:, 20:24]
    lball = lbt[:, 24:40].rearrange("p (h l) -> p h l", l=4)
    lb = lbt[:, 40:44]
    oml = lbt[:, 44:48]
    flb = lbt[:, 48:52]
    negone = lbt[:, 52:53]
    P.add("dve", lambda e: e.tensor_reduce(out=mx, in_=lbraw, axis=AX.X, op=ALU.max), ["prm"], ["lbt"])
    tt("dve", ex, lbraw, mx.unsqueeze(2).to_broadcast([128, 4, 4]), ALU.subtract, ["prm", "lbt"], ["lbt"])
    act(ex, ex, AF.Exp, ["lbt"], ["lbt"])
    P.add("dve", lambda e: e.tensor_reduce(out=sm, in_=ex, axis=AX.X, op=ALU.add), ["lbt"], ["lbt"])
    P.add("dve", lambda e: e.reciprocal(out=sm, in_=sm), ["lbt"], ["lbt"])
    tt("dve", ex, ex, sm.unsqueeze(2).to_broadcast([128, 4, 4]), ALU.mult, ["lbt"], ["lbt"])
    ms("dve", lball[:, :, 0:1], 0.0, ["lbt"])
    for l in range(1, 4):
        tt("dve", lball[:, :, l:l + 1], lball[:, :, l - 1:l], ex[:, :, l:l + 1], ALU.add, ["lbt"], ["lbt"])
    ts("dve", lball, lball, 0.0, 1.0, ALU.max, ALU.min, ["lbt"], ["lbt"])
    tt("dve", lball, lball, lsel.unsqueeze(1).to_broadcast([128, 4, 4]), ALU.mult, ["lbt", "prm"], ["lbt"])
    P.add("dve", lambda e: e.tensor_reduce(out=lb, in_=lball, axis=AX.X, op=ALU.add), ["lbt"], ["lbt"])
    ts("dve", oml, lb, -1.0, 1.0, ALU.mult, ALU.add, ["lbt"], ["lbt"])
    ts("dve", flb, lb, -1.0, FL, ALU.mult, ALU.add, ["lbt"], ["lbt"])
    ms("dve", negone, -1.0, ["lbt"])

    ms("dve", S[:], 0.0, [("S", q_) for q_ in range(4)])
    ms("dve", Ltot[:], 0.0, ["Ltot"])
    ms("dve", Bend[:], 0.0, [("Bend", q_) for q_ in range(4)])

    if phase == "B":
        P.dma("sp", sinks[:], sinks_d[:, :], writes=["sinks"])
        for cg in range(4):
            for k in range(31):
                ts("pool", diag[:, cg * 31 + k, :], ident, cwv[:, cg, k:k + 1], None, ALU.mult, None,
                   ["cstb", "prm"], ["diag"])
        ms("pool", uT[:], 0.0, ["uT"] + [("uT", q_) for q_ in range(4)])
        ms("pool", KT[:], 0.0, ["KThist", ("KT", 0), ("KT", 1)])
        ms("pool", Vaug[:], 0.0, [("Vaug", q_) for q_ in range(TGL + 1)])
        ms("pool", KMz[:], 0.0, ["KMz"])
        ms("dve", VMs[:], 0.0, ["VMs"])
        act(sinks[0:1, :], sinks[0:1, :], AF.Exp, ["sinks"], ["sinks"])
        cp("dve", VMs[0:1, :, 64:128], sinks[0:1, :].unsqueeze(2).to_broadcast([1, 8, 64]), ["sinks", "VMs"], ["VMs"])

    def norm(gi, tok0, T):
        hk = ("hT", gi)
        hn, hnk = hnTs[gi % 2], ("hnT", gi % 2)
        act(sqb[:, :, :T], hT[:, :, tok0:tok0 + T], AF.Square, [hk] + [("hTl", k_) for k_ in range(KC)], ["sqb"])
        for kc in range(KC):
            mm(ps[3][:, :T], ones, sqb[:, kc, :T], kc == 0, kc == KC - 1, ["cstb", "sqb"], ["ps3"])
        act(st[0][:, :T], ps[3][:, :T], AF.Ln, ["ps3"], ["st0"], scale=1.0 / D, bias=EPS)
        act(st[1][:, :T], st[0][:, :T], AF.Exp, ["st0"], ["st1"], scale=-0.5)
        for kc in range(KC):
            stt("dve", hn[:, kc, :T], hT[:, kc, tok0:tok0 + T], ng[:, kc:kc + 1], st[1][:, :T], ALU.mult, ALU.mult,
                [hk, "prm", "st1"], [hnk])

    def use_hn(gi):
        HN["hn"], HN["hk"] = hnTs[gi % 2], ("hnT", gi % 2)

    def proj(psb, pk, wbuf, wk, c0, T, width=128, ncol=512):
        wv = wbuf[:, :].rearrange("p (k c) -> p k c", c=ncol)
        for kc in range(KC):
            mm(psb[:width, :T], wv[:, kc, c0:c0 + width], HN["hn"][:, kc, :T], kc == 0, kc == KC - 1, [wk, HN["hk"]], [pk])

    def hg_gates(psz, pzk, hb, T, nchunk, g0):
        sl = slice(hb * T, (hb + 1) * T)
        X1, X2, X3, X4 = A[0][:, sl], A[1][:, sl], A[2][:, sl], A[3][:, sl]
        k1, k2, k3, k4 = ak(0, hb * T, hb * T + T), ak(1, hb * T, hb * T + T), ak(2, hb * T, hb * T + T), ak(3, hb * T, hb * T + T)
        act(X1, psz[:, :T], AF.Sigmoid, [pzk], k1)
        ts("dve", X1, X1, oml[:, hb:hb + 1], flb[:, hb:hb + 1], ALU.mult, ALU.max, k1 + ["lbt"], k1)
        act(X2, X1, AF.Ln, k1 + ["lbt"], k2, bias=lb[:, hb:hb + 1])
        ts("dve", X1, X1, negone, oml[:, hb:hb + 1], ALU.mult, ALU.add, k1 + ["lbt"], k1)
        if g0:
            tt("dve", X2, X2, vrow[:, :T], ALU.mult, k2 + ["cstf"], k2)
            tt("dve", X1, X1, vrow[:, :T], ALU.mult, k1 + ["cstf"], k1)
        P.add("dve", lambda e: e.tensor_tensor_scan(out=X3, data0=X2, data1=X2, initial=0.0, op0=ALU.add, op1=ALU.bypass),
              k2, k3)
        B3 = X3.rearrange("p (c t) -> p c t", t=64)
        cp("dve", Bmid[:, hb, :nchunk].unsqueeze(2), B3[:, :, 31:32], k3, [("Bmid", hb)])
        cp("dve", Bend[:, hb, 1:1 + nchunk].unsqueeze(2), B3[:, :, 63:64], k3, [("Bend", hb)])
        tt("dve", D3[:, hb, 0, :nchunk], Bend[:, hb, 1:1 + nchunk], Bend[:, hb, 0:nchunk], ALU.subtract, [("Bend", hb)], [("D3", hb)])
        tt("dve", D3[:, hb, 1, :nchunk], Bmid[:, hb, :nchunk], Bend[:, hb, 0:nchunk], ALU.subtract, [("Bend", hb), ("Bmid", hb)], [("D3", hb)])
        tt("dve", D3[:, hb, 2, :nchunk], Bend[:, hb, 1:1 + nchunk], Bmid[:, hb, :nchunk], ALU.subtract, [("Bend", hb), ("Bmid", hb)], [("D3", hb)])
        act(E3[:, hb, :, :nchunk], D3[:, hb, :, :nchunk], AF.Exp, [("D3", hb)], [("E3", hb)])
        tt("dve", Ltot[:, hb:hb + 1], Ltot[:, hb:hb + 1], Bend[:, hb, nchunk:nchunk + 1], ALU.add, [("Bend", hb), "Ltot"], ["Ltot"])
        tt("dve", B3, B3, Bmid[:, hb, :nchunk].unsqueeze(2).to_broadcast([128, nchunk, 64]), ALU.subtract, k3 + [("Bmid", hb)], k3)
        act(X4, X3, AF.Exp, k3, k4, scale=-1.0)
        tt("dve", kA[:, hb, :T], X1, X4, ALU.mult, k1 + k4, [("kA", hb)])

    def hg_vtok(wbuf, wk, nt, T):
        wv = wbuf[:, :].rearrange("p (k c) -> p k c", c=512)
        for ti in range(nt):
            psb, pk = pj()
            for kc in range(KC):
                mm(psb[:, :], HN["hn"][:, kc, ti * 128:(ti + 1) * 128], wv[:, kc, :], kc == 0, kc == KC - 1, [wk, HN["hk"]], [pk])
            act(vtok[:, ti, :], psb[:, :], AF.Copy, [pk], [("vtok", ti)])

    def hg_ktok(nt):
        for ti in range(nt):
            for hb in range(4):
                tr(pst[:, hb * 128:(hb + 1) * 128], kA[:, hb, ti * 128:(ti + 1) * 128], [("kA", hb), "cstb"], ["pst"])
            cp("dve", kAtok[:, ti, :], pst[:, :], ["pst"], [("kAtok", ti)])

    def hg_state(ti, half):
        cj = ti * 2 + half
        rows = slice(half * 64, half * 64 + 64)
        for hb in range(4):
            mm(ps[5][:, hb * 128:(hb + 1) * 128], kAtok[rows, ti, hb * 128:(hb + 1) * 128],
               vtok[rows, ti, hb * 128:(hb + 1) * 128], True, True, [("kAtok", ti), ("vtok", ti)], ["ps5"])
        for hb in range(4):
            hs = slice(hb * 128, (hb + 1) * 128)
            ts("dve", Stmp[:, hs], ps[5][:, hs], E3[:, hb, 2, cj:cj + 1], None, ALU.mult, None, ["ps5", ("E3", hb)], [("Stmp", hb)])
            stt("dve", S[:, hs], S[:, hs], E3[:, hb, 0, cj:cj + 1], Stmp[:, hs], ALU.mult, ALU.add,
                [("S", hb), ("Stmp", hb), ("E3", hb)], [("S", hb)])

    def knorm(psb, pk, out_ap, out_keys, gcol, T):
        act(bfs[:, 0, :T], psb[:, :T], AF.Square, [pk], ["bfs0"])
        mm(ps[3][:, :T], bones, bfs[:, 0, :T], True, True, ["cstb", "bfs0"], ["ps3"])
        act(st[2][:, :T], ps[3][:, :T], AF.Ln, ["ps3"], ["st2"], scale=1.0 / 64, bias=EPS)
        act(st[3][:, :T], st[2][:, :T], AF.Exp, ["st2"], ["st3"], scale=-0.5)
        stt("dve", out_ap, psb[:, :T], gcol, st[3][:, :T], ALU.mult, ALU.mult, [pk, "prm", "st3"], out_keys)

    if phase == "A":
        groups = [list(range(1 + g * TGL, 1 + (g + 1) * TGL)) for g in range((NT - 1) // TGL)]
        for gi in range(len(groups)):
            sched.extend([7, 8])
        sched.extend([0, 1, 14])
        for _ in range(3):
            ring_load()
        for gi, tiles in enumerate(groups):
            nt = len(tiles)
            T = nt * 128
            tok0 = tiles[0] * 128
            nchunk = 2 * nt
            if gi == 0:
                norm(gi, tok0, T)
            use_hn(gi)
            wf, wfk = ring_get(7)
            for hb in range(4):
                psb, pk = pj()
                proj(psb, pk, wf, wfk, hb * 128, T)
                hg_gates(psb, pk, hb, T, nchunk, False)
            ring_done()
            wi, wik = ring_get(8)
            hg_vtok(wi, wik, nt, T)
            ring_done()
            hg_ktok(nt)
            if gi + 1 < len(groups):
                nt2 = groups[gi + 1]
                norm(gi + 1, nt2[0] * 128, len(nt2) * 128)
            for ti in range(nt):
                for half in range(2):
                    hg_state(ti, half)
            if gi == len(groups) - 1:
                lt = (nt - 1) * 128
                wa, wak = ring_get(0)
                wb, wbk = ring_get(1)
                wva = wa[:, :].rearrange("p (k c) -> p k c", c=512)
                wvb = wb[:, :].rearrange("p (k c) -> p k c", c=512)
                for cg in range(4):
                    pa, pak = pj()
                    pb, pbk = pj()
                    for kc in range(KC):
                        mm(pa[:, :128], wva[:, kc, cg * 128:(cg + 1) * 128], HN["hn"][:, kc, lt:lt + 128], kc == 0, kc == KC - 1, [wak, HN["hk"]], [pak])
                    for kc in range(KC):
                        mm(pb[:, :128], wvb[:, kc, cg * 128:(cg + 1) * 128], HN["hn"][:, kc, lt:lt + 128], kc == 0, kc == KC - 1, [wbk, HN["hk"]], [pbk])
                    act(st[2][:, :128], pb[:, :128], AF.Sigmoid, [pbk], ["st2"])
                    tt("dve", utl[:, cg, :], pa[:, :128], st[2][:, :128], ALU.mult, [pak, "st2"], ["utl"])
                ring_done()
                ring_done()
                P.dma("sp", u_out[:, :, :], utl[:, :, 98:128], reads=["utl"])
                wc, wck = ring_get(14)
                wvc = wc[:, :].rearrange("p (k c) -> p k c", c=512)
                for kvh in range(2):
                    psb, pk = pj()
                    for kc in range(KC):
                        mm(psb[:, :128], wvc[:, kc, kvh * 128:(kvh + 1) * 128], HN["hn"][:, kc, lt:lt + 128], kc == 0, kc == KC - 1, [wck, HN["hk"]], [pk])
                    knorm(psb, pk, A[4][:, kvh * 128:(kvh + 1) * 128], ak(4, kvh * 128, kvh * 128 + 128), kg, 128)
                    P.dma("sp", k_out[:, kvh, :], A[4][:, kvh * 128:(kvh + 1) * 128], reads=ak(4, kvh * 128, kvh * 128 + 128))
                psb, pk = pj()
                for kc in range(KC):
                    mm(psb[:, :128], HN["hn"][:, kc, lt:lt + 128], wvc[:, kc, 256:384], kc == 0, kc == KC - 1, [wck, HN["hk"]], [pk])
                act(A[4][:, 256:384], psb[:, :128], AF.Copy, [pk], ak(4, 256, 384))
                P.dma("sp", v_out[:, :], A[4][:, 256:384], reads=ak(4, 256, 384))
                ring_done()
        if DBG:
            for nm, tile_, keys in [("d_lbt", lbt, ["lbt"]), ("d_Ltot", Ltot, ["Ltot"]), ("d_Bend", Bend, [("Bend", h_) for h_ in range(4)]),
                                    ("d_D3", D3, [("D3", h_) for h_ in range(4)]), ("d_E3", E3, [("E3", h_) for h_ in range(4)]),
                                    ("d_A0", A[0], ak(0, 0, 4 * TML)), ("d_A1", A[1], ak(1, 0, 4 * TML)), ("d_A2", A[2], ak(2, 0, 4 * TML)),
                                    ("d_A3", A[3], ak(3, 0, 4 * TML))]:
                shp = list(tile_.shape)
                dd = dout(nm, [shp[0], int(np.prod(shp[1:]))])
                src = tile_[:] if len(shp) == 2 else (tile_[:].rearrange("p a b -> p (a b)") if len(shp) == 3 else tile_[:].rearrange("p a b c -> p (a b c)"))
                P.dma("sp", dd[:, :], src, reads=keys)
        act(Ltot[:], Ltot[:], AF.Exp, ["Ltot"], ["Ltot"])
        P.dma("sp", f_out[:, :], Ltot[:], reads=["Ltot"])
        P.dma("sp", s_out[:, :], S[:], reads=[("S", q_) for q_ in range(4)])
        P.finish()
        return nc, P

    groups = [[0]] + [list(range(1 + g * TGL, 1 + (g + 1) * TGL)) for g in range((NT - 1) // TGL)]
    group_chunks = [0, 1, 7, 2, 6, 8, 9, 13, 14, 15] + [3, 4, 5, 10, 11, 12, 16, 17, 18] + [19, 20]
    for _ in groups:
        sched.extend(group_chunks)
    for _ in range(3):
        ring_load()

    normed = set()

    def group(gi, tiles, g0, vm_first):
        nt = len(tiles)
        T = nt * 128
        tok0 = tiles[0] * 128
        nchunk = 2 * nt
        hk = ("hT", gi)
        P.stage = "norm"
        if gi not in normed:
            norm(gi, tok0, T)
            normed.add(gi)
        use_hn(gi)
        P.stage = "conv_glu"
        wa, wak = ring_get(0)
        wb, wbk = ring_get(1)
        for cg in range(4):
            pa, pak = pj()
            pb, pbk = pj()
            proj(pa, pak, wa, wak, cg * 128, T)
            proj(pb, pbk, wb, wbk, cg * 128, T)
            act(st[2][:, :T], pb[:, :T], AF.Sigmoid, [pbk], ["st2"])
            if g0:
                tt("dve", st[2][:, :T], st[2][:, :T], vrow[:, :T], ALU.mult, ["st2", "cstf"], ["st2"])
            tt("dve", uT[:, cg, 30:30 + T], pa[:, :T], st[2][:, :T], ALU.mult, [pak, "st2"], [("uT", cg)])
        ring_done()
        ring_done()
        P.stage = "hg_gates"
        wf, wfk = ring_get(7)
        for hb in range(4):
            psb, pk = pj()
            proj(psb, pk, wf, wfk, hb * 128, T)
            hg_gates(psb, pk, hb, T, nchunk, g0)
        ring_done()
        P.stage = "conv_taps"
        for cg in range(4):
            pc, pck = pj()
            for k in range(31):
                mm(pc[:, :T], diag[:, cg * 31 + k, :], uT[:, cg, k:k + T], k == 0, k == 30, ["diag", ("uT", cg), "uT"], [pck])
            yk = ak(0, cg * T, cg * T + T)
            act(yT[:, cg * T:(cg + 1) * T], pc[:, :T], AF.Identity, [pck, "prm"], yk, bias=cb[:, cg:cg + 1])
            act(bfs[:, 0, :T], pc[:, :T], AF.Square, [pck, "prm"], ["bfs0"], bias=cb[:, cg:cg + 1])
            cp("pool", bfs[:, 1, :T], yT[:, cg * T:(cg + 1) * T], yk, ["bfs1"])
            mm(ps[3][:, :T], ones, bfs[:, 1, :T], cg == 0, cg == 3, ["cstb", "bfs1"], ["ps3"])
            mm(ps[4][:, :T], ones, bfs[:, 0, :T], cg == 0, cg == 3, ["cstb", "bfs0"], ["ps4"])
        cp("pool", uT[:, :, 0:30], uT[:, :, T:T + 30], [("uT", c) for c in range(4)], ["uT"])
        act(st[0][:, :T], ps[3][:, :T], AF.Copy, ["ps3"], ["st0"], scale=1.0 / 512)
        tt("dve", st[1][:, :T], st[0][:, :T], st[0][:, :T], ALU.mult, ["st0"], ["st1"])
        stt("dve", st[1][:, :T], ps[4][:, :T], 1.0 / 512, st[1][:, :T], ALU.mult, ALU.subtract, ["ps4", "st1"], ["st1"])
        act(st[1][:, :T], st[1][:, :T], AF.Ln, ["st1"], ["st1"], bias=EPS)
        act(st[1][:, :T], st[1][:, :T], AF.Exp, ["st1"], ["st1"], scale=-0.5)
        P.stage = "conv_gate"
        wg, wgk = ring_get(2)
        for cg in range(4):
            pg, pgk = pj()
            proj(pg, pgk, wg, wgk, cg * 128, T)
            ysl = yT[:, cg * T:(cg + 1) * T]
            yk = ak(0, cg * T, cg * T + T)
            tt("dve", ysl, ysl, st[0][:, :T], ALU.subtract, yk + ["st0"], yk)
            tt("dve", ysl, ysl, st[1][:, :T], ALU.mult, yk + ["st1"], yk)
            act(ysl, ysl, AF.Silu, yk + ["prm"], yk, scale=lng[:, cg:cg + 1], bias=lnb[:, cg:cg + 1])
            act(st[2][:, :T], pg[:, :T], AF.Silu, [pgk], ["st2"])
            tt("dve", brY[0][:, cg, :T], ysl, st[2][:, :T], ALU.mult, yk + ["st2"], [("brY0", cg)])
        ring_done()
        P.stage = "hg_q"
        wq, wqk = ring_get(6)
        for hb in range(4):
            sl = slice(hb * T, (hb + 1) * T)
            psb, pk = pj()
            proj(psb, pk, wq, wqk, hb * 128, T)
            act(A[4][:, sl], A[2][:, sl], AF.Exp, ak(2, hb * T, hb * T + T), ak(4, hb * T, hb * T + T))
            act(A[0][:, sl], psb[:, :T], AF.Silu, [pk], ak(0, hb * T, hb * T + T))
            tt("dve", qA[:, hb, :T], A[0][:, sl], A[4][:, sl], ALU.mult, ak(0, hb * T, hb * T + T) + ak(4, hb * T, hb * T + T), [("qA", hb)])
        ring_done()
        P.stage = "hg_vk"
        wi, wik = ring_get(8)
        hg_vtok(wi, wik, nt, T)
        ring_done()
        hg_ktok(nt)
        wgt, wgtk = ring_get(9)
        for hb in range(4):
            psb, pk = pj()
            proj(psb, pk, wgt, wgtk, hb * 128, T)
            act(A[1][:, hb * T:(hb + 1) * T], psb[:, :T], AF.Silu, [pk], ak(1, hb * T, hb * T + T))
        ring_done()
        P.stage = "hg_core"
        for ti in range(nt):
            tsl = slice(ti * 128, (ti + 1) * 128)
            for hb in range(4):
                for half in range(2):
                    c0 = ti * 128 + half * 64
                    mm(ps[4][half * 64:half * 64 + 64, hb * 64:(hb + 1) * 64], kA[:, hb, c0:c0 + 64], qA[:, hb, c0:c0 + 64],
                       True, True, [("kA", hb), ("qA", hb)], ["ps4"])
            tt("dve", scm[:, ti, :, :], ps[4][:, 0:256].rearrange("p (h t) -> p h t", t=64),
               trim.unsqueeze(1).to_broadcast([128, 4, 64]), ALU.mult, ["ps4", "cstb"], [("scm", ti)])
            for half in range(2):
                cj = ti * 2 + half
                rows = slice(half * 64, half * 64 + 64)
                c0 = ti * 128 + half * 64
                for hb in range(4):
                    hs = slice(hb * 128, (hb + 1) * 128)
                    ts("dve", Sbf[:, hs], S[:, hs], E3[:, hb, 1, cj:cj + 1], None, ALU.mult, None, [("S", hb), ("E3", hb)], [("Sbf", hb)])
                for hb in range(4):
                    hs = slice(hb * 128, (hb + 1) * 128)
                    osl = ps[6][:, hb * 128 + half * 64: hb * 128 + half * 64 + 64]
                    mm(osl, vtok[rows, ti, hs], scm[rows, ti, hb, :], True, False, [("vtok", ti), ("scm", ti)], ["ps6"])
                    mm(osl, Sbf[:, hs], qA[:, hb, c0:c0 + 64], False, True, [("Sbf", hb), ("qA", hb)], ["ps6"])
                hg_state(ti, half)
            bff = bfs[:, :, :].rearrange("p a t -> p (a t)")[:, 0:512]
            a3k = ak(3, 0, 512)
            act(bff, ps[6][:, :], AF.Square, ["ps6"], ["bfs0", "bfs1"])
            mm(ps[3][:, :], ones, bff, True, True, ["cstb", "bfs0", "bfs1"], ["ps3"])
            act(A[3][:, 0:512], ps[3][:, :], AF.Ln, ["ps3"], a3k, scale=1.0 / 128, bias=EPS)
            act(A[3][:, 0:512], A[3][:, 0:512], AF.Exp, a3k, a3k, scale=-0.5)
            tt("dve", A[3][:, 0:512], ps[6][:, :], A[3][:, 0:512], ALU.mult, ["ps6"] + a3k, a3k)
            for hb in range(4):
                g0_, g1_ = hb * T + ti * 128, hb * T + (ti + 1) * 128
                stt("dve", brY[1][:, hb, tsl], A[3][:, hb * 128:(hb + 1) * 128], gng, A[1][:, g0_:g1_],
                    ALU.mult, ALU.mult, a3k + ["prm"] + ak(1, g0_, g1_), [("brY1", hb)])
        P.stage = "att_proj"
        wq2, wq2k = ring_get(13)
        for qb in range(4):
            psb, pk = pj()
            proj(psb, pk, wq2, wq2k, qb * 128, T)
            knorm(psb, pk, qn[:, qb, :T], [("qn", qb)], qg, T)
        ring_done()
        wc, wck = ring_get(14)
        wvc = wc[:, :].rearrange("p (k c) -> p k c", c=512)
        for kvh in range(2):
            psb, pk = pj()
            proj(psb, pk, wc, wck, kvh * 128, T)
            knorm(psb, pk, KT[:, kvh, 128:128 + T], [("KT", kvh)], kg, T)
        for ti in range(nt):
            psb, pk = pj()
            for kc in range(KC):
                mm(psb[:, :128], HN["hn"][:, kc, ti * 128:(ti + 1) * 128], wvc[:, kc, 256:384], kc == 0, kc == KC - 1, [wck, HN["hk"]], [pk])
            act(Vaug[:, 1 + ti, :, 0:64], psb[:, 0:128].rearrange("p (h d) -> p h d", d=64), AF.Copy, [pk], [("Vaug", 1 + ti)])
            if g0:
                ts("dve", Vaug[:, 1 + ti, :, 0:64], Vaug[:, 1 + ti, :, 0:64], vcol, None, ALU.mult, None, [("Vaug", 1 + ti), "cstf"], [("Vaug", 1 + ti)])
                cp("dve", Vaug[:, 1 + ti, :, 64:128], vcol.unsqueeze(1).to_broadcast([128, 2, 64]), [("Vaug", 1 + ti), "cstf"], [("Vaug", 1 + ti)])
            else:
                ms("dve", Vaug[:, 1 + ti, :, 64:128], 1.0, [("Vaug", 1 + ti)])
        ring_done()
        wgc, wgck = ring_get(15)
        for qb in range(4):
            psb, pk = pj()
            proj(psb, pk, wgc, wgck, qb * 128, T)
            act(A[1][:, qb * T:(qb + 1) * T], psb[:, :T], AF.Silu, [pk], ak(1, qb * T, qb * T + T))
        ring_done()
        P.stage = "att_core"
        for ti in range(nt):
            tsl = slice(ti * 128, (ti + 1) * 128)
            hist = slice(ti * 128, (ti + 1) * 128)
            cur = slice((ti + 1) * 128, (ti + 2) * 128)
            if g0:
                km, kmk, vm, vmk = KMz, "KMz", VMs, "VMs"
            elif vm_first and ti == 0:
                km, kmk, vm, vmk = KM, "KM", VM0, "VM0"
            else:
                km, kmk, vm, vmk = KM, "KM", VM, "VMall"
            for kvh in range(2):
                for g in range(4):
                    hq = kvh * 4 + g
                    qb = hq // 2
                    r = slice((hq % 2) * 64, (hq % 2) * 64 + 64)
                    mm(ps[5][:, g * 128:(g + 1) * 128], KT[r, kvh, hist], qn[r, qb, tsl], True, True, [("KT", kvh), "KThist", ("qn", qb)], ["ps5"])
                    mm(ps[6][:, g * 128:(g + 1) * 128], KT[r, kvh, cur], qn[r, qb, tsl], True, True, [("KT", kvh), ("qn", qb)], ["ps6"])
                    mm(ps[4][0:32, g * 128:(g + 1) * 128], km[r, kvh, :], qn[r, qb, tsl], True, True, [kmk, ("qn", qb)], ["ps4"])
                act(pA[:, :, :], ps[5][:, :].rearrange("p (g t) -> p g t", t=128), AF.Exp, ["ps5"], ["pA"], scale=0.125)
                act(pB[:, :, :], ps[6][:, :].rearrange("p (g t) -> p g t", t=128), AF.Exp, ["ps6"], ["pB"], scale=0.125)
                act(pM[:, :, :], ps[4][0:32, :].rearrange("p (g t) -> p g t", t=128), AF.Exp, ["ps4"], ["pM"], scale=0.125)
                for g in range(4):
                    hq = kvh * 4 + g
                    o0 = ps[3][:, g * 128:g * 128 + 64]
                    o1 = ps[3][:, g * 128 + 64:g * 128 + 128]
                    mm(o0, Vaug[:, ti, kvh, :], pA[:, g, 0:64], True, False, [("Vaug", ti), "pA"], ["ps3"])
                    mm(o0, Vaug[0:64, ti + 1, kvh, :], pB[0:64, g, 0:64], False, False, [("Vaug", ti + 1), "pB"], ["ps3"])
                    vmr = ["VM"] + [("VMd", q_) for q_ in range(8)] if vmk == "VMall" else [vmk]
                    mm(o0, vm[0:32, hq, :], pM[0:32, g, 0:64], False, True, vmr + ["pM"], ["ps3"])
                    mm(o1, Vaug[64:128, ti, kvh, :], pA[64:128, g, 64:128], True, False, [("Vaug", ti), "pA"], ["ps3"])
                    mm(o1, Vaug[:, ti + 1, kvh, :], pB[:, g, 64:128], False, False, [("Vaug", ti + 1), "pB"], ["ps3"])
                    mm(o1, vm[0:32, hq, :], pM[0:32, g, 64:128], False, True, vmr + ["pM"], ["ps3"])
                P.add("dve", lambda e: e.reciprocal(out=rden[64:128, :, :], in_=ps[3][64:128, :].rearrange("p (g t) -> p g t", t=128)),
                      ["ps3"], ["rden"])
                pv = ps[3][0:64, :].rearrange("p (g t) -> p g t", t=128)
                for par in range(2):
                    orow = slice(par * 64, par * 64 + 64)
                    qb0 = kvh * 2
                    tt("dve", ao[orow, :, :], pv[:, par::2, :], rden[64:128, par::2, :], ALU.mult, ["ps3", "rden"], [("ao", par)])
                    gk = ak(1, qb0 * T, (qb0 + 2) * T)
                    tt("dve", brY[2][orow, qb0:qb0 + 2, tsl], ao[orow, :, :],
                       A[1][orow, :].rearrange("p (q t) -> p q t", t=T)[:, qb0:qb0 + 2, tsl], ALU.mult,
                       [("ao", par)] + gk, [("brY2", qb0), ("brY2", qb0 + 1)])
        cp("pool", KT[:, :, 0:128], KT[:, :, T:T + 128], [("KT", 0), ("KT", 1)], ["KThist"])
        cp("pool", Vaug[:, 0, :, :], Vaug[:, nt, :, :], [("Vaug", nt)], [("Vaug", 0)])
        if gi + 1 < len(groups):
            P.stage = "norm"
            nt2 = groups[gi + 1]
            norm(gi + 1, nt2[0] * 128, len(nt2) * 128)
            normed.add(gi + 1)
        P.stage = "mix"
        wga = [ring_get(3), ring_get(4)]
        wco = ring_get(5)
        for br, (gch, och) in enumerate([((3, 4), 5), ((10, 11), 12), ((16, 17), 18)]):
            if br > 0:
                wga = [ring_get(gch[0]), ring_get(gch[1])]
                wco = ring_get(och)
            wov = wco[0][:, :].rearrange("p (k c) -> p k c", c=1024)
            for ob in range(8):
                pg, pgk = pj()
                wg_, wgk_ = wga[ob // 4]
                proj(pg, pgk, wg_, wgk_, (ob % 4) * 128, T)
                pz, pzk = pj()
                for kc4 in range(4):
                    mm(pz[:, :T], wov[:, kc4, ob * 128:(ob + 1) * 128], brY[br][:, kc4, :T], kc4 == 0, kc4 == 3,
                       [wco[1], ("brY%d" % br, kc4)], [pzk])
                act(st[2][:, :T], pg[:, :T], AF.Sigmoid, [pgk], ["st2"])
                msl = A[2 + (ob % 2)][:, (ob // 2) * T:(ob // 2) * T + T]
                mk = ak(2 + ob % 2, (ob // 2) * T, (ob // 2) * T + T)
                if br == 0:
                    tt("dve", msl, pz[:, :T], st[2][:, :T], ALU.mult, [pzk, "st2"], mk)
                elif br == 1:
                    tt("dve", st[3][:, :T], pz[:, :T], st[2][:, :T], ALU.mult, [pzk, "st2"], ["st3"])
                    tt("pool", msl, msl, st[3][:, :T], ALU.add, mk + ["st3"], mk)
                else:
                    tt("dve", st[3][:, :T], pz[:, :T], st[2][:, :T], ALU.mult, [pzk, "st2"], ["st3"])
                    tt("pool", mixb[:, ob, :T], msl, st[3][:, :T], ALU.add, mk + ["st3"], ["sqb"])
            ring_done()
            ring_done()
            ring_done()
        P.stage = "out"
        wo = [ring_get(19), ring_get(20)]
        for ob2 in range(8):
            psb, pk = pj()
            w_, wk_ = wo[ob2 // 4]
            wv = w_[:, :].rearrange("p (k c) -> p k c", c=512)
            for kc in range(KC):
                mm(psb[:, :T], wv[:, kc, (ob2 % 4) * 128:(ob2 % 4 + 1) * 128], mixb[:, kc, :T], kc == 0, kc == KC - 1, [wk_, "sqb"], [pk])
            tt("dve", hT[:, ob2, tok0:tok0 + T], hT[:, ob2, tok0:tok0 + T], psb[:, :T], ALU.add, [hk, pk], [hk])
        ring_done()
        ring_done()

    group(0, groups[0], True, False)
    if DBG:
        for i_ in range(3):
            dd = dout("d_brY%d" % i_, [128, 4 * TML])
            P.dma("pool", dd[:, :], brY[i_][:].rearrange("p a t -> p (a t)"), reads=[("brY%d" % i_, q_) for q_ in range(4)])
        dd = dout("d_uT", [128, 4 * (30 + TML)])
        P.dma("pool", dd[:, :], uT[:].rearrange("p a t -> p (a t)"), reads=["uT"] + [("uT", q_) for q_ in range(4)])
        dd = dout("d_mixb", [128, KC * TML])
        P.dma("pool", dd[:, :], sqb[:].rearrange("p a t -> p (a t)"), reads=["sqb"])
    f0 = cmask[:, 8:9]
    omf0 = cmask[:, 9:10]
    cp("dve", KM[:, :, :], KT[:, :, 96:128], ["KThist"], ["KM"])
    ms("dve", KM[:, :, 0:1], 0.0, ["KM"])
    cp("dve", VM[:, :, :], VMs[:, :, :], ["VMs"], ["VM"])
    for hq in range(8):
        P.dma("sp", VM[16:32, hq, :], Vaug[112:128, 0, hq // 4, :], reads=[("Vaug", 0), "VM"], writes=[("VMd", hq)])
    ms("dve", lbt[0:32, 56:57], 1.0, ["lbt2"])
    P.dma("sp", lbt[16:32, 56:57], cmask[16:32, 9:10], reads=["cmask", "lbt2"], writes=["lbt3"])
    ts("dve", VM0[:, :, :], VM[:, :, :], lbt[0:32, 56:57], None, ALU.mult, None, ["VM", "lbt2", "lbt3"] + [("VMd", q_) for q_ in range(8)], ["VM0"])
    EXK = ak(4, 0, 512)
    SKEYS = [("S", q_) for q_ in range(4)]
    P.dma("sp", exs[:, 0:120], uex_d.rearrange("p c t -> p (c t)"), writes=EXK)
    ts("dve", uT[:, :, 0:30], uT[:, :, 0:30], f0, None, ALU.mult, None, ["uT", "cmask"], ["uT"])
    stt("dve", uT[:, :, 0:30], exs[:, 0:120].rearrange("p (c t) -> p c t", t=30), omf0, uT[:, :, 0:30], ALU.mult, ALU.add,
        EXK + ["cmask", "uT"], ["uT"])
    P.dma("sp", exs[:, 128:384], kex_d.rearrange("p c t -> p (c t)"), writes=EXK)
    ts("dve", KT[:, :, 0:128], KT[:, :, 0:128], f0, None, ALU.mult, None, ["KThist", "cmask"], ["KThist"])
    stt("dve", KT[:, :, 0:128], exs[:, 128:384].rearrange("p (c t) -> p c t", t=128), omf0, KT[:, :, 0:128], ALU.mult, ALU.add,
        EXK + ["cmask", "KThist"], ["KThist"])
    P.dma("sp", exs[:, 384:512], vex_d[:, :], writes=EXK)
    ts("dve", Vaug[:, 0, :, 0:64], Vaug[:, 0, :, 0:64], f0, None, ALU.mult, None, [("Vaug", 0), "cmask"], [("Vaug", 0)])
    stt("dve", Vaug[:, 0, :, 0:64], exs[:, 384:512].rearrange("p (h d) -> p h d", d=64), omf0, Vaug[:, 0, :, 0:64], ALU.mult, ALU.add,
        EXK + ["cmask", ("Vaug", 0)], [("Vaug", 0)])
    ts("dve", Vaug[:, 0, :, 64:128], Vaug[:, 0, :, 64:128], f0, omf0, ALU.mult, ALU.add, [("Vaug", 0), "cmask"], [("Vaug", 0)])
    P.dma("sp", Fall[:, :, :], fall_d.rearrange("c p h -> p c h"), writes=["Fall"])
    for j in range(NCORES - 1):
        mj = cmask[:, j:j + 1]
        P.dma("sp", exs[:, :], sall_d[j], writes=EXK)
        ts("dve", lbt[:, 60:64], Fall[:, j, :], -1.0, None, ALU.add, None, ["Fall"], ["lbt4"])
        ts("dve", lbt[:, 60:64], lbt[:, 60:64], mj, onecol, ALU.mult, ALU.add, ["lbt4", "cmask", "cstf"], ["lbt4"])
        tt("dve", S[:, :].rearrange("p (h v) -> p h v", v=128), S[:, :].rearrange("p (h v) -> p h v", v=128),
           lbt[:, 60:64].unsqueeze(2).to_broadcast([128, 4, 128]), ALU.mult, SKEYS + ["lbt4"], SKEYS)
        stt("dve", S[:, :], exs[:, :], mj, S[:, :], ALU.mult, ALU.add, EXK + ["cmask"] + SKEYS, SKEYS)
    for gi in range(1, len(groups)):
        group(gi, groups[gi], False, gi == 1)
    for kc in range(KC):
        P.dma("sp", h_out[:, kc, :], hT[:, kc, :], reads=[("hT", g) for g in range(len(groups))])
    P.finish()
    return nc, P


def _chunk1024(W):
    return np.ascontiguousarray(W.reshape(8, 128, 512).transpose(1, 0, 2).reshape(128, CW))


def _chunk512(W):
    return np.ascontiguousarray(W.reshape(4, 128, 1024).transpose(1, 0, 2).reshape(128, CW))


def _layer_chunks(inp, l):
    w = inp["w_in"][l]
    ch = np.zeros((NCH, 128, CW), np.float32)
    def c(i, a, b):
        ch[i] = _chunk1024(w[:, a:b])
    c(0, 0, 512); c(1, 512, 1024); c(2, 1024, 1536)
    c(3, 4864, 5376); c(4, 5376, 5888)
    ch[5] = _chunk512(inp["w_conv_out"][l])
    c(6, 1536, 2048); c(7, 2048, 2560); c(8, 2560, 3072); c(9, 3072, 3584)
    c(10, 5888, 6400); c(11, 6400, 6912)
    ch[12] = _chunk512(inp["w_hg_out"][l])
    c(13, 3584, 4096)
    ckv = np.zeros((1024, 512), np.float32)
    ckv[:, 0:64] = w[:, 4096:4160]; ckv[:, 64:128] = w[:, 4096:4160]
    ckv[:, 128:192] = w[:, 4160:4224]; ckv[:, 192:256] = w[:, 4160:4224]
    ckv[:, 256:384] = w[:, 4224:4352]
    ch[14] = _chunk1024(ckv)
    c(15, 4352, 4864); c(16, 6912, 7424); c(17, 7424, 7936)
    ch[18] = _chunk512(inp["w_att_out"][l])
    ch[19] = _chunk1024(inp["w_out"][l][:, 0:512]); ch[20] = _chunk1024(inp["w_out"][l][:, 512:1024])
    return ch


def _layer_prm(inp, l):
    p = np.zeros((128, NPRM), np.float32)
    p[:, P_NG:P_NG + 8] = inp["norm_g"][l].reshape(8, 128).T
    hlb = inp["hg_lower_bounds"]
    p[:, P_LBRAW:P_LBRAW + 16] = hlb.reshape(4, 4, 128).transpose(2, 1, 0).reshape(128, 16)
    p[:, P_LSEL + l] = 1.0
    p[:, P_KG] = np.tile(inp["k_norm_g"][l], 2)
    p[:, P_QG] = np.tile(inp["q_norm_g"][l], 2)
    p[:, P_GNG] = inp["hg_norm_g"][l]
    p[:, P_CB:P_CB + 4] = inp["conv_b"][l].reshape(4, 128).T
    p[:, P_LNG:P_LNG + 4] = inp["conv_ln_g"][l].reshape(4, 128).T
    p[:, P_LNB:P_LNB + 4] = inp["conv_ln_b"][l].reshape(4, 128).T
    p[:, P_CW:P_CW + 124] = inp["conv_w"][l].reshape(31, 4, 128).transpose(2, 1, 0).reshape(128, 124)
    return p


def _consts():
    c = np.zeros((128, NCST), np.float32)
    c[:, C_ID:C_ID + 128] = np.eye(128)
    c[:, C_ONES:C_ONES + 128] = 1.0
    c[0:64, C_BONES:C_BONES + 64] = 1.0
    c[64:128, C_BONES + 64:C_BONES + 128] = 1.0
    s = np.arange(128)[:, None] % 64
    t = np.arange(64)[None, :]
    c[:, C_TRI:C_TRI + 64] = (s <= t)
    c[:, C_VROW + 112:C_VROW + 128] = 1.0
    c[112:128, C_VCOL] = 1.0
    return c


_CACHE = {}


def _prog(phase):
    if phase not in _CACHE:
        _CACHE[phase] = build(phase)[0]
    return _CACHE[phase]


def kernel(x, meta_tokens, norm_g, w_in, conv_w, conv_b, conv_ln_g, conv_ln_b, w_conv_out,
           hg_lower_bounds, hg_norm_g, w_hg_out, q_norm_g, k_norm_g, attn_sinks, w_att_out, w_out):
    inp = dict(x=x, meta_tokens=meta_tokens, norm_g=norm_g, w_in=w_in, conv_w=conv_w, conv_b=conv_b,
               conv_ln_g=conv_ln_g, conv_ln_b=conv_ln_b, w_conv_out=w_conv_out, hg_lower_bounds=hg_lower_bounds,
               hg_norm_g=hg_norm_g, w_hg_out=w_hg_out, q_norm_g=q_norm_g, k_norm_g=k_norm_g, attn_sinks=attn_sinks,
               w_att_out=w_att_out, w_out=w_out)
    inp = {k: np.asarray(v, np.float32) for k, v in inp.items()}
    xs = inp["x"][0]
    cst = _consts()
    hTs = []
    for c in range(NCORES):
        h = np.zeros((TOK, D), np.float32)
        h[112:128] = inp["meta_tokens"]
        h[128:] = xs[c * OWN:(c + 1) * OWN]
        hTs.append(np.ascontiguousarray(h.T.reshape(KC, 128, TOK).transpose(1, 0, 2)))
    cmasks = []
    for c in range(NCORES):
        m = np.zeros((128, 16), np.float32)
        m[:, 0:8] = (np.arange(8) < c)[None, :]
        m[:, 8] = 1.0 if c == 0 else 0.0
        m[:, 9] = 0.0 if c == 0 else 1.0
        cmasks.append(m)
    ncA = _prog("A")
    ncB = _prog("B")
    cores = list(range(NCORES))
    for l in range(DEPTH):
        ch = _layer_chunks(inp, l)
        prm = _layer_prm(inp, l)
        chA = np.ascontiguousarray(ch[[7, 8, 0, 1, 14]])
        inA = [dict(hT=hTs[c], wch=chA, prm=prm, cst=cst, cmask=cmasks[c]) for c in cores]
        rA = run_bass_kernel_spmd(ncA, inA, core_ids=cores).results
        Sall = np.ascontiguousarray(np.stack([rA[c]["Sloc"] for c in cores]))
        Fall = np.ascontiguousarray(np.stack([rA[c]["Floc"] for c in cores]))
        sk = np.zeros((32, 8), np.float32)
        sk[0] = inp["attn_sinks"][l]
        inB = []
        for c in cores:
            p = max(c - 1, 0)
            inB.append(dict(hT=hTs[c], wch=ch, prm=prm, cst=cst, cmask=cmasks[c], sinks=sk, Sall=Sall, Fall=Fall,
                            uex=rA[p]["utail"], kex=rA[p]["ktail"], vex=rA[p]["vtail"]))
        rB = run_bass_kernel_spmd(ncB, inB, core_ids=cores).results
        hTs = [rB[c]["hT_out"] for c in cores]
    out = np.zeros((1, SEQ, D), np.float32)
    for c in cores:
        hc = hTs[c].transpose(1, 0, 2).reshape(D, TOK).T
        out[0, c * OWN:(c + 1) * OWN] = hc[128:]
    return out
```

```python
import contextlib
import numpy as np
import concourse.bass as bass
import concourse.mybir as mybir
from concourse.bass_utils import run_bass_kernel_spmd

F32 = mybir.dt.float32
BF16 = mybir.dt.bfloat16
AF = mybir.ActivationFunctionType
ALU = mybir.AluOpType
AX = mybir.AxisListType

NCORES = 8
D = 1024
KC = 8
SEQ = 16384
OWN = SEQ // NCORES
NT = 1 + OWN // 128
TOK = NT * 128
TG = 2
TG_A = 4
TMAX = TG * 128
DEPTH = 4
EPS = 1e-6
FL = 1e-30
NCH = 21
CW = 4096
DBG = False
ENGS = ("pe", "act", "dve", "pool", "sp")

P_NG, P_LBRAW, P_LSEL, P_KG, P_QG, P_GNG, P_CB, P_LNG, P_LNB, P_CW = 0, 8, 24, 28, 29, 30, 31, 35, 39, 43
NPRM = 43 + 124
C_ID, C_ONES, C_BONES, C_TRI, C_VROW, C_VCOL = 0, 128, 256, 384, 448, 576
NCST = 577


class _Op:
    __slots__ = ("eng", "fn", "reads", "writes", "dma", "idx", "waits", "flag", "sem", "val", "stage")


class Prog:
    def __init__(self, nc):
        self.nc = nc
        self.ops = []
        self.stack = contextlib.ExitStack()
        self.n_dma_sems = {"sp": 16, "pool": 12}

    def sb(self, name, shape, dtype):
        return self.stack.enter_context(self.nc.sbuf_tensor(name, list(shape), dtype))

    def ps(self, name, shape, dtype=F32):
        return self.stack.enter_context(self.nc.psum_tensor(name, list(shape), dtype))

    def add(self, eng, fn, reads=(), writes=(), dma=False):
        op = _Op()
        op.eng, op.fn, op.reads, op.writes, op.dma = eng, fn, tuple(reads), tuple(writes), dma
        op.idx = len(self.ops)
        op.stage = getattr(self, "stage", "")
        op.flag = dma
        self.ops.append(op)
        return op

    def dma(self, eng, out, in_, reads=(), writes=()):
        return self.add(eng, lambda e: e.dma_start(out=out, in_=in_), reads, writes, dma=True)

    def finish(self):
        nc, ops = self.nc, self.ops
        last_w, rdrs, deps_all = {}, {}, []
        for op in ops:
            deps = set()
            raw = set()
            for k in op.reads:
                w = last_w.get(k)
                if w is not None:
                    deps.add(w)
                    raw.add(w)
            for k in op.writes:
                w = last_w.get(k)
                if w is not None:
                    deps.add(w)
                r = rdrs.get(k)
                if r:
                    deps.update(r)
            deps.discard(op.idx)
            need = []
            for d in deps:
                dop = ops[d]
                if dop.eng == op.eng and not dop.dma and not op.dma:
                    if op.eng == "pe" or d not in raw:
                        continue
                need.append(d)
                dop.flag = True
            deps_all.append(need)
            for k in op.reads:
                rdrs.setdefault(k, []).append(op.idx)
            for k in op.writes:
                last_w[k] = op.idx
                rdrs[k] = []
        sem_eng = {e: self.stack.enter_context(nc.semaphore("s_" + e)) for e in ENGS if e != "sp"}
        dma_sems = {q: [self.stack.enter_context(nc.semaphore("d_%s%d" % (q, i))) for i in range(n)]
                    for q, n in self.n_dma_sems.items()}
        cnt = {e: 0 for e in ENGS}
        dma_rr = {q: 0 for q in dma_sems}
        dma_val, dma_prev = {}, {}
        for op in ops:
            if op.dma:
                q = op.eng
                i = dma_rr[q] % len(dma_sems[q])
                dma_rr[q] += 1
                v = dma_val.get((q, i), 0) + 16
                dma_val[(q, i)] = v
                op.sem, op.val = dma_sems[q][i], v
                p = dma_prev.get((q, i))
                if p is not None:
                    deps_all[op.idx].append(p)
                dma_prev[(q, i)] = op.idx
            elif op.flag:
                cnt[op.eng] += 1
                op.sem, op.val = sem_eng[op.eng], cnt[op.eng]
            else:
                op.sem = op.val = None
        seen = {e: {} for e in ENGS}
        for op in ops:
            best = {}
            for d in deps_all[op.idx]:
                dop = ops[d]
                key = id(dop.sem)
                if key not in best or best[key][1] < dop.val:
                    best[key] = (dop.sem, dop.val)
            sn = seen[op.eng]
            waits = []
            for key, (sem, val) in best.items():
                if sn.get(key, 0) >= val:
                    continue
                sn[key] = val
                waits.append((sem, val))
            op.waits = waits
        per = {e: [o for o in ops if o.eng == e] for e in ENGS}
        final = [(dma_sems[q][i], v) for (q, i), v in dma_val.items()]
        self.stats = {e: len(per[e]) for e in ENGS}
        self.stats["waits"] = sum(len(o.waits) for o in ops)

        def emit(e, name):
            for op in per[name]:
                for sem, val in op.waits:
                    e.wait_ge(sem, val)
                ins = op.fn(e)
                if op.sem is not None:
                    ins.then_inc(op.sem, 16 if op.dma else 1)
            if name == "sp":
                for sem, val in final:
                    e.wait_ge(sem, val)

        with nc.Block() as block:
            @block.tensor
            def _(e):
                emit(e, "pe")

            @block.scalar
            def _(e):
                emit(e, "act")

            @block.vector
            def _(e):
                emit(e, "dve")

            @block.gpsimd
            def _(e):
                emit(e, "pool")

            @block.sync
            def _(e):
                emit(e, "sp")
        self.stack.close()


def build(phase):
    nc = bass.Bass("TRN2", target_bir_lowering=False)
    P = Prog(nc)
    TGL = TG_A if phase == "A" else TG
    TML = TGL * 128

    def din(name, shape, dt=F32):
        return nc.dram_tensor(name, list(shape), dt, kind="ExternalInput").ap()

    def dout(name, shape, dt=F32):
        return nc.dram_tensor(name, list(shape), dt, kind="ExternalOutput").ap()

    h_in = din("hT", [128, KC, TOK])
    nw = NCH if phase == "B" else 5
    w_in_d = din("wch", [nw, 128, CW])
    prm_d = din("prm", [128, NPRM])
    cst_d = din("cst", [128, NCST])
    cmask_d = din("cmask", [128, 16])
    wscr = nc.dram_tensor("wscr", [nw, 128, CW], BF16).ap()
    if phase == "B":
        sinks_d = din("sinks", [32, 8])
        sall_d = din("Sall", [NCORES, 128, 512])
        fall_d = din("Fall", [NCORES, 128, 4])
        uex_d = din("uex", [128, 4, 30])
        kex_d = din("kex", [128, 2, 128])
        vex_d = din("vex", [128, 128])
        h_out = dout("hT_out", [128, KC, TOK])
    else:
        s_out = dout("Sloc", [128, 512])
        f_out = dout("Floc", [128, 4])
        u_out = dout("utail", [128, 4, 30])
        k_out = dout("ktail", [128, 2, 128])
        v_out = dout("vtail", [128, 128])
    if phase == "A":
        cmap = {7: 0, 8: 1, 0: 2, 1: 3, 14: 4}
    else:
        cmap = {i: i for i in range(NCH)}

    T = TML
    hT = P.sb("hTs", [128, KC, TOK], F32)
    hnTs = [P.sb("hnT%d" % i, [128, KC, T], BF16) for i in range(2)]
    HN = {"hn": hnTs[0], "hk": ("hnT", 0)}
    sqb = P.sb("sqb", [128, KC, T], BF16)
    wr = [P.sb("wr%d" % i, [128, CW], BF16) for i in range(3)]
    prm = P.sb("prm_s", [128, NPRM], F32)
    cstf = P.sb("cstf", [128, NCST], F32)
    cstb = P.sb("cstb", [128, NCST], BF16)
    cmask = P.sb("cmask_s", [128, 16], F32)
    A = [P.sb("A%d" % i, [128, 4 * T], F32) for i in range(5)]
    st = [P.sb("st%d" % i, [128, T], F32) for i in range(4)]
    lbt = P.sb("lbt", [128, 64], F32)
    kA = P.sb("kA", [128, 4, T], BF16)
    kAtok = P.sb("kAtok", [128, TGL, 512], BF16)
    vtok = P.sb("vtok", [128, TGL, 512], BF16)
    S = P.sb("S", [128, 512], F32)
    Stmp = P.sb("Stmp", [128, 512], F32)
    E3 = P.sb("E3", [128, 4, 3, 2 * TGL], F32)
    D3 = P.sb("D3", [128, 4, 3, 2 * TGL], F32)
    Bend = P.sb("Bend", [128, 4, 2 * TGL + 1], F32)
    Bmid = P.sb("Bmid", [128, 4, 2 * TGL], F32)
    Ltot = P.sb("Ltot", [128, 4], F32)
    bfs = P.sb("bfs", [128, 2, T], BF16)
    if phase == "B":
        diag = P.sb("diag", [128, 124, 128], BF16)
        uT = P.sb("uT", [128, 4, 30 + T], BF16)
        yT = A[0]
        brY = [P.sb("brY%d" % i, [128, 4, T], BF16) for i in range(3)]
        qA = P.sb("qA", [128, 4, T], BF16)
        scm = P.sb("scm", [128, TGL, 4, 64], BF16)
        Sbf = P.sb("Sbf", [128, 512], BF16)
        qn = P.sb("qn", [128, 4, T], BF16)
        KT = P.sb("KT", [128, 2, 128 + T], BF16)
        Vaug = P.sb("Vaug", [128, TGL + 1, 2, 128], BF16)
        KM = P.sb("KM", [128, 2, 32], BF16)
        KMz = P.sb("KMz", [128, 2, 32], BF16)
        VM = P.sb("VM", [32, 8, 128], BF16)
        VM0 = P.sb("VM0", [32, 8, 128], BF16)
        VMs = P.sb("VMs", [32, 8, 128], BF16)
        sinks = P.sb("sinks_s", [32, 8], F32)
        pA = P.sb("pA", [128, 4, 128], BF16)
        pB = P.sb("pB", [128, 4, 128], BF16)
        pM = P.sb("pM", [32, 4, 128], BF16)
        rden = P.sb("rden", [128, 4, 128], F32)
        ao = P.sb("ao", [128, 2, 128], F32)
        exs = A[4][:, 0:512]
        Fall = P.sb("Fall_s", [128, NCORES, 4], F32)
        mixb = sqb
    else:
        utl = P.sb("utl", [128, 4, 128], F32)
    ps = [P.ps("ps%d" % i, [128, 512], F32) for i in range(7)]
    pst = P.ps("pst", [128, 512], BF16)
    pjc = [0]

    def pj():
        i = pjc[0] % 3
        pjc[0] += 1
        return ps[i], "ps%d" % i

    def ak(i, lo, hi):
        return [("A", i, u) for u in range(lo // 128, (hi + 127) // 128)]

    ident = cstb[:, C_ID:C_ID + 128]
    ones = cstb[:, C_ONES:C_ONES + 128]
    bones = cstb[:, C_BONES:C_BONES + 128]
    trim = cstb[:, C_TRI:C_TRI + 64]
    vrow = cstf[:, C_VROW:C_VROW + 128]
    vcol = cstf[:, C_VCOL:C_VCOL + 1]
    onecol = cstf[:, C_ONES:C_ONES + 1]

    def act(out, in_, func, reads, writes, scale=1.0, bias=0.0):
        P.add("act", lambda e: e.activation(out=out, in_=in_, func=func, bias=bias, scale=scale), reads, writes)

    def tt(eng, out, in0, in1, op, reads, writes):
        P.add(eng, lambda e: e.tensor_tensor(out=out, in0=in0, in1=in1, op=op), reads, writes)

    def ts(eng, out, in0, s1, s2, op0, op1, reads, writes):
        if op1 is None:
            P.add(eng, lambda e: e.tensor_scalar(out=out, in0=in0, scalar1=s1, scalar2=None, op0=op0), reads, writes)
        else:
            P.add(eng, lambda e: e.tensor_scalar(out=out, in0=in0, scalar1=s1, scalar2=s2, op0=op0, op1=op1), reads, writes)

    def stt(eng, out, in0, scalar, in1, op0, op1, reads, writes):
        P.add(eng, lambda e: e.scalar_tensor_tensor(out=out, in0=in0, scalar=scalar, in1=in1, op0=op0, op1=op1), reads, writes)

    def cp(eng, out, in_, reads, writes):
        P.add(eng, lambda e: e.tensor_copy(out=out, in_=in_), reads, writes)

    def mm(out, lhsT, rhs, start, stop, reads, writes):
        P.add("pe", lambda e: e.matmul(out, lhsT=lhsT, rhs=rhs, start=start, stop=stop), reads, writes)

    def tr(out, in_, reads, writes):
        P.add("pe", lambda e: e.transpose(out, in_, ident), reads, writes)

    def ms(eng, ap, val, writes):
        P.add(eng, lambda e: e.memset(ap, val), (), writes)

    P.dma("sp", prm[:], prm_d[:, :], writes=["prm"])
    P.dma("sp", cstf[:], cst_d[:, :], writes=["cstf"])
    P.dma("pool", cstb[:], cst_d[:, :], writes=["cstb"])
    P.dma("sp", cmask[:], cmask_d[:, :], writes=["cmask"])
    for kc in range(KC):
        P.dma("sp", hT[:, kc, :], h_in[:, kc, :], writes=[("hTl", kc)])
    if phase == "A":
        order = [7, 8, 0, 1, 14]
    else:
        order = list(range(NCH))
    for ci in order:
        P.dma("pool", wscr[cmap[ci]], w_in_d[cmap[ci]], writes=[("wscr", ci)])

    sched = []
    ring = {"next_load": 0, "next_use": 0}

    def ring_load():
        i = ring["next_load"]
        if i < len(sched):
            ci = sched[i]
            slot = i % 3
            P.dma("sp", wr[slot][:, :], wscr[cmap[ci]], reads=[("wscr", ci)], writes=[("wr", slot)])
            ring["next_load"] += 1

    def ring_get(ci):
        i = ring["next_use"]
        assert sched[i] == ci, (sched[i], ci, i)
        ring["next_use"] += 1
        slot = i % 3
        return wr[slot], ("wr", slot)

    def ring_done():
        ring_load()

    ng = prm[:, P_NG:P_NG + 8]
    lbraw = prm[:, P_LBRAW:P_LBRAW + 16].rearrange("p (h l) -> p h l", l=4)
    lsel = prm[:, P_LSEL:P_LSEL + 4]
    kg = prm[:, P_KG:P_KG + 1]
    qg = prm[:, P_QG:P_QG + 1]
    gng = prm[:, P_GNG:P_GNG + 1]
    cb = prm[:, P_CB:P_CB + 4]
    lng = prm[:, P_LNG:P_LNG + 4]
    lnb = prm[:, P_LNB:P_LNB + 4]
    cwv = prm[:, P_CW:P_CW + 124].rearrange("p (c k) -> p c k", k=31)
    mx = lbt[:, 0:4]
    ex = lbt[:, 4:20].rearrange("p (h l) -> p h l", l=4)
    sm = lbt[:, 20:24]
    lball = lbt[:, 24:40].rearrange("p (h l) -> p h l", l=4)
    lb = lbt[:, 40:44]
    oml = lbt[:, 44:48]
    flb = lbt[:, 48:52]
    negone = lbt[:, 52:53]
    P.add("dve", lambda e: e.tensor_reduce(out=mx, in_=lbraw, axis=AX.X, op=ALU.max), ["prm"], ["lbt"])
    tt("dve", ex, lbraw, mx.unsqueeze(2).to_broadcast([128, 4, 4]), ALU.subtract, ["prm", "lbt"], ["lbt"])
    act(ex, ex, AF.Exp, ["lbt"], ["lbt"])
    P.add("dve", lambda e: e.tensor_reduce(out=sm, in_=ex, axis=AX.X, op=ALU.add), ["lbt"], ["lbt"])
    P.add("dve", lambda e: e.reciprocal(out=sm, in_=sm), ["lbt"], ["lbt"])
    tt("dve", ex, ex, sm.unsqueeze(2).to_broadcast([128, 4, 4]), ALU.mult, ["lbt"], ["lbt"])
    ms("dve", lball[:, :, 0:1], 0.0, ["lbt"])
    for l in range(1, 4):
        tt("dve", lball[:, :, l:l + 1], lball[:, :, l - 1:l], ex[:, :, l:l + 1], ALU.add, ["lbt"], ["lbt"])
    ts("dve", lball, lball, 0.0, 1.0, ALU.max, ALU.min, ["lbt"], ["lbt"])
    tt("dve", lball, lball, lsel.unsqueeze(1).to_broadcast([128, 4, 4]), ALU.mult, ["lbt", "prm"], ["lbt"])
    P.add("dve", lambda e: e.tensor_reduce(out=lb, in_=lball, axis=AX.X, op=ALU.add), ["lbt"], ["lbt"])
    ts("dve", oml, lb, -1.0, 1.0, ALU.mult, ALU.add, ["lbt"], ["lbt"])
    ts("dve", flb, lb, -1.0, FL, ALU.mult, ALU.add, ["lbt"], ["lbt"])
    ms("dve", negone, -1.0, ["lbt"])

    ms("dve", S[:], 0.0, [("S", q_) for q_ in range(4)])
    ms("dve", Ltot[:], 0.0, ["Ltot"])
    ms("dve", Bend[:], 0.0, [("Bend", q_) for q_ in range(4)])

    if phase == "B":
        P.dma("sp", sinks[:], sinks_d[:, :], writes=["sinks"])
        for cg in range(4):
            for k in range(31):
                ts("pool", diag[:, cg * 31 + k, :], ident, cwv[:, cg, k:k + 1], None, ALU.mult, None,
                   ["cstb", "prm"], ["diag"])
        ms("pool", uT[:], 0.0, ["uT"] + [("uT", q_) for q_ in range(4)])
        ms("pool", KT[:], 0.0, ["KThist", ("KT", 0), ("KT", 1)])
        ms("pool", Vaug[:], 0.0, [("Vaug", q_) for q_ in range(TGL + 1)])
        ms("pool", KMz[:], 0.0, ["KMz"])
        ms("dve", VMs[:], 0.0, ["VMs"])
        act(sinks[0:1, :], sinks[0:1, :], AF.Exp, ["sinks"], ["sinks"])
        cp("dve", VMs[0:1, :, 64:128], sinks[0:1, :].unsqueeze(2).to_broadcast([1, 8, 64]), ["sinks", "VMs"], ["VMs"])

    def norm(gi, tok0, T):
        hk = ("hT", gi)
        hn, hnk = hnTs[gi % 2], ("hnT", gi % 2)
        act(sqb[:, :, :T], hT[:, :, tok0:tok0 + T], AF.Square, [hk] + [("hTl", k_) for k_ in range(KC)], ["sqb"])
        for kc in range(KC):
            mm(ps[3][:, :T], ones, sqb[:, kc, :T], kc == 0, kc == KC - 1, ["cstb", "sqb"], ["ps3"])
        act(st[0][:, :T], ps[3][:, :T], AF.Ln, ["ps3"], ["st0"], scale=1.0 / D, bias=EPS)
        act(st[1][:, :T], st[0][:, :T], AF.Exp, ["st0"], ["st1"], scale=-0.5)
        for kc in range(KC):
            stt("dve", hn[:, kc, :T], hT[:, kc, tok0:tok0 + T], ng[:, kc:kc + 1], st[1][:, :T], ALU.mult, ALU.mult,
                [hk, "prm", "st1"], [hnk])

    def use_hn(gi):
        HN["hn"], HN["hk"] = hnTs[gi % 2], ("hnT", gi % 2)

    def proj(psb, pk, wbuf, wk, c0, T, width=128, ncol=512):
        wv = wbuf[:, :].rearrange("p (k c) -> p k c", c=ncol)
        for kc in range(KC):
            mm(psb[:width, :T], wv[:, kc, c0:c0 + width], HN["hn"][:, kc, :T], kc == 0, kc == KC - 1, [wk, HN["hk"]], [pk])

    def hg_gates(psz, pzk, hb, T, nchunk, g0):
        sl = slice(hb * T, (hb + 1) * T)
        X1, X2, X3, X4 = A[0][:, sl], A[1][:, sl], A[2][:, sl], A[3][:, sl]
        k1, k2, k3, k4 = ak(0, hb * T, hb * T + T), ak(1, hb * T, hb * T + T), ak(2, hb * T, hb * T + T), ak(3, hb * T, hb * T + T)
        act(X1, psz[:, :T], AF.Sigmoid, [pzk], k1)
        ts("dve", X1, X1, oml[:, hb:hb + 1], flb[:, hb:hb + 1], ALU.mult, ALU.max, k1 + ["lbt"], k1)
        act(X2, X1, AF.Ln, k1 + ["lbt"], k2, bias=lb[:, hb:hb + 1])
        ts("dve", X1, X1, negone, oml[:, hb:hb + 1], ALU.mult, ALU.add, k1 + ["lbt"], k1)
        if g0:
            tt("dve", X2, X2, vrow[:, :T], ALU.mult, k2 + ["cstf"], k2)
            tt("dve", X1, X1, vrow[:, :T], ALU.mult, k1 + ["cstf"], k1)
        P.add("dve", lambda e: e.tensor_tensor_scan(out=X3, data0=X2, data1=X2, initial=0.0, op0=ALU.add, op1=ALU.bypass),
              k2, k3)
        B3 = X3.rearrange("p (c t) -> p c t", t=64)
        cp("dve", Bmid[:, hb, :nchunk].unsqueeze(2), B3[:, :, 31:32], k3, [("Bmid", hb)])
        cp("dve", Bend[:, hb, 1:1 + nchunk].unsqueeze(2), B3[:, :, 63:64], k3, [("Bend", hb)])
        tt("dve", D3[:, hb, 0, :nchunk], Bend[:, hb, 1:1 + nchunk], Bend[:, hb, 0:nchunk], ALU.subtract, [("Bend", hb)], [("D3", hb)])
        tt("dve", D3[:, hb, 1, :nchunk], Bmid[:, hb, :nchunk], Bend[:, hb, 0:nchunk], ALU.subtract, [("Bend", hb), ("Bmid", hb)], [("D3", hb)])
        tt("dve", D3[:, hb, 2, :nchunk], Bend[:, hb, 1:1 + nchunk], Bmid[:, hb, :nchunk], ALU.subtract, [("Bend", hb), ("Bmid", hb)], [("D3", hb)])
        act(E3[:, hb, :, :nchunk], D3[:, hb, :, :nchunk], AF.Exp, [("D3", hb)], [("E3", hb)])
        tt("dve", Ltot[:, hb:hb + 1], Ltot[:, hb:hb + 1], Bend[:, hb, nchunk:nchunk + 1], ALU.add, [("Bend", hb), "Ltot"], ["Ltot"])
        tt("dve", B3, B3, Bmid[:, hb, :nchunk].unsqueeze(2).to_broadcast([128, nchunk, 64]), ALU.subtract, k3 + [("Bmid", hb)], k3)
        act(X4, X3, AF.Exp, k3, k4, scale=-1.0)
        tt("dve", kA[:, hb, :T], X1, X4, ALU.mult, k1 + k4, [("kA", hb)])

    def hg_vtok(wbuf, wk, nt, T):
        wv = wbuf[:, :].rearrange("p (k c) -> p k c", c=512)
        for ti in range(nt):
            psb, pk = pj()
            for kc in range(KC):
                mm(psb[:, :], HN["hn"][:, kc, ti * 128:(ti + 1) * 128], wv[:, kc, :], kc == 0, kc == KC - 1, [wk, HN["hk"]], [pk])
            act(vtok[:, ti, :], psb[:, :], AF.Copy, [pk], [("vtok", ti)])

    def hg_ktok(nt):
        for ti in range(nt):
            for hb in range(4):
                tr(pst[:, hb * 128:(hb + 1) * 128], kA[:, hb, ti * 128:(ti + 1) * 128], [("kA", hb), "cstb"], ["pst"])
            cp("dve", kAtok[:, ti, :], pst[:, :], ["pst"], [("kAtok", ti)])

    def hg_state(ti, half):
        cj = ti * 2 + half
        rows = slice(half * 64, half * 64 + 64)
        for hb in range(4):
            mm(ps[5][:, hb * 128:(hb + 1) * 128], kAtok[rows, ti, hb * 128:(hb + 1) * 128],
               vtok[rows, ti, hb * 128:(hb + 1) * 128], True, True, [("kAtok", ti), ("vtok", ti)], ["ps5"])
        for hb in range(4):
            hs = slice(hb * 128, (hb + 1) * 128)
            ts("dve", Stmp[:, hs], ps[5][:, hs], E3[:, hb, 2, cj:cj + 1], None, ALU.mult, None, ["ps5", ("E3", hb)], [("Stmp", hb)])
            stt("dve", S[:, hs], S[:, hs], E3[:, hb, 0, cj:cj + 1], Stmp[:, hs], ALU.mult, ALU.add,
                [("S", hb), ("Stmp", hb), ("E3", hb)], [("S", hb)])

    def knorm(psb, pk, out_ap, out_keys, gcol, T):
        act(bfs[:, 0, :T], psb[:, :T], AF.Square, [pk], ["bfs0"])
        mm(ps[3][:, :T], bones, bfs[:, 0, :T], True, True, ["cstb", "bfs0"], ["ps3"])
        act(st[2][:, :T], ps[3][:, :T], AF.Ln, ["ps3"], ["st2"], scale=1.0 / 64, bias=EPS)
        act(st[3][:, :T], st[2][:, :T], AF.Exp, ["st2"], ["st3"], scale=-0.5)
        stt("dve", out_ap, psb[:, :T], gcol, st[3][:, :T], ALU.mult, ALU.mult, [pk, "prm", "st3"], out_keys)

    if phase == "A":
        groups = [list(range(1 + g * TGL, 1 + (g + 1) * TGL)) for g in range((NT - 1) // TGL)]
        for gi in range(len(groups)):
            sched.extend([7, 8])
        sched.extend([0, 1, 14])
        for _ in range(3):
            ring_load()
        for gi, tiles in enumerate(groups):
            nt = len(tiles)
            T = nt * 128
            tok0 = tiles[0] * 128
            nchunk = 2 * nt
            if gi == 0:
                norm(gi, tok0, T)
            use_hn(gi)
            wf, wfk = ring_get(7)
            for hb in range(4):
                psb, pk = pj()
                proj(psb, pk, wf, wfk, hb * 128, T)
                hg_gates(psb, pk, hb, T, nchunk, False)
            ring_done()
            wi, wik = ring_get(8)
            hg_vtok(wi, wik, nt, T)
            ring_done()
            hg_ktok(nt)
            if gi + 1 < len(groups):
                nt2 = groups[gi + 1]
                norm(gi + 1, nt2[0] * 128, len(nt2) * 128)
            for ti in range(nt):
                for half in range(2):
                    hg_state(ti, half)
            if gi == len(groups) - 1:
                lt = (nt - 1) * 128
                wa, wak = ring_get(0)
                wb, wbk = ring_get(1)
                wva = wa[:, :].rearrange("p (k c) -> p k c", c=512)
                wvb = wb[:, :].rearrange("p (k c) -> p k c", c=512)
                for cg in range(4):
                    pa, pak = pj()
                    pb, pbk = pj()
                    for kc in range(KC):
                        mm(pa[:, :128], wva[:, kc, cg * 128:(cg + 1) * 128], HN["hn"][:, kc, lt:lt + 128], kc == 0, kc == KC - 1, [wak, HN["hk"]], [pak])
                    for kc in range(KC):
                        mm(pb[:, :128], wvb[:, kc, cg * 128:(cg + 1) * 128], HN["hn"][:, kc, lt:lt + 128], kc == 0, kc == KC - 1, [wbk, HN["hk"]], [pbk])
                    act(st[2][:, :128], pb[:, :128], AF.Sigmoid, [pbk], ["st2"])
                    tt("dve", utl[:, cg, :], pa[:, :128], st[2][:, :128], ALU.mult, [pak, "st2"], ["utl"])
                ring_done()
                ring_done()
                P.dma("sp", u_out[:, :, :], utl[:, :, 98:128], reads=["utl"])
                wc, wck = ring_get(14)
                wvc = wc[:, :].rearrange("p (k c) -> p k c", c=512)
                for kvh in range(2):
                    psb, pk = pj()
                    for kc in range(KC):
                        mm(psb[:, :128], wvc[:, kc, kvh * 128:(kvh + 1) * 128], HN["hn"][:, kc, lt:lt + 128], kc == 0, kc == KC - 1, [wck, HN["hk"]], [pk])
                    knorm(psb, pk, A[4][:, kvh * 128:(kvh + 1) * 128], ak(4, kvh * 128, kvh * 128 + 128), kg, 128)
                    P.dma("sp", k_out[:, kvh, :], A[4][:, kvh * 128:(kvh + 1) * 128], reads=ak(4, kvh * 128, kvh * 128 + 128))
                psb, pk = pj()
                for kc in range(KC):
                    mm(psb[:, :128], HN["hn"][:, kc, lt:lt + 128], wvc[:, kc, 256:384], kc == 0, kc == KC - 1, [wck, HN["hk"]], [pk])
                act(A[4][:, 256:384], psb[:, :128], AF.Copy, [pk], ak(4, 256, 384))
                P.dma("sp", v_out[:, :], A[4][:, 256:384], reads=ak(4, 256, 384))
                ring_done()
        if DBG:
            for nm, tile_, keys in [("d_lbt", lbt, ["lbt"]), ("d_Ltot", Ltot, ["Ltot"]), ("d_Bend", Bend, [("Bend", h_) for h_ in range(4)]),
                                    ("d_D3", D3, [("D3", h_) for h_ in range(4)]), ("d_E3", E3, [("E3", h_) for h_ in range(4)]),
                                    ("d_A0", A[0], ak(0, 0, 4 * TML)), ("d_A1", A[1], ak(1, 0, 4 * TML)), ("d_A2", A[2], ak(2, 0, 4 * TML)),
                                    ("d_A3", A[3], ak(3, 0, 4 * TML))]:
                shp = list(tile_.shape)
                dd = dout(nm, [shp[0], int(np.prod(shp[1:]))])
                src = tile_[:] if len(shp) == 2 else (tile_[:].rearrange("p a b -> p (a b)") if len(shp) == 3 else tile_[:].rearrange("p a b c -> p (a b c)"))
                P.dma("sp", dd[:, :], src, reads=keys)
        act(Ltot[:], Ltot[:], AF.Exp, ["Ltot"], ["Ltot"])
        P.dma("sp", f_out[:, :], Ltot[:], reads=["Ltot"])
        P.dma("sp", s_out[:, :], S[:], reads=[("S", q_) for q_ in range(4)])
        P.finish()
        return nc, P

    groups = [[0]] + [list(range(1 + g * TGL, 1 + (g + 1) * TGL)) for g in range((NT - 1) // TGL)]
    group_chunks = [0, 1, 7, 2, 6, 8, 9, 13, 14, 15] + [3, 4, 5, 10, 11, 12, 16, 17, 18] + [19, 20]
    for _ in groups:
        sched.extend(group_chunks)
    for _ in range(3):
        ring_load()

    normed = set()

    def group(gi, tiles, g0, vm_first):
        nt = len(tiles)
        T = nt * 128
        tok0 = tiles[0] * 128
        nchunk = 2 * nt
        hk = ("hT", gi)
        P.stage = "norm"
        if gi not in normed:
            norm(gi, tok0, T)
            normed.add(gi)
        use_hn(gi)
        P.stage = "conv_glu"
        wa, wak = ring_get(0)
        wb, wbk = ring_get(1)
        for cg in range(4):
            pa, pak = pj()
            pb, pbk = pj()
            proj(pa, pak, wa, wak, cg * 128, T)
            proj(pb, pbk, wb, wbk, cg * 128, T)
            sgl, sgk = A[1][:, cg * T:(cg + 1) * T], ak(1, cg * T, cg * T + T)
            act(sgl, pb[:, :T], AF.Sigmoid, [pbk], sgk)
            if g0:
                tt("dve", sgl, sgl, vrow[:, :T], ALU.mult, sgk + ["cstf"], sgk)
            tt("dve", uT[:, cg, 30:30 + T], pa[:, :T], sgl, ALU.mult, [pak] + sgk, [("uT", cg)])
        ring_done()
        ring_done()
        P.stage = "hg_gates"
        wf, wfk = ring_get(7)
        for hb in range(4):
            psb, pk = pj()
            proj(psb, pk, wf, wfk, hb * 128, T)
            hg_gates(psb, pk, hb, T, nchunk, g0)
        ring_done()
        P.stage = "conv_taps"
        for cg in range(4):
            pc, pck = pj()
            for k in range(31):
                mm(pc[:, :T], diag[:, cg * 31 + k, :], uT[:, cg, k:k + T], k == 0, k == 30, ["diag", ("uT", cg), "uT"], [pck])
            yk = ak(0, cg * T, cg * T + T)
            act(yT[:, cg * T:(cg + 1) * T], pc[:, :T], AF.Identity, [pck, "prm"], yk, bias=cb[:, cg:cg + 1])
            act(sqb[:, 2 * (cg % 2) + 1, :T], pc[:, :T], AF.Square, [pck, "prm"], [("sqbc", cg % 2, 1)], bias=cb[:, cg:cg + 1])
            yb_, ysq_ = sqb[:, 2 * (cg % 2), :T], sqb[:, 2 * (cg % 2) + 1, :T]
            cp("pool", yb_, yT[:, cg * T:(cg + 1) * T], yk, [("sqbc", cg % 2, 0)])
            if cg > 0:
                pcg = cg - 1
                mm(ps[3][:, :T], ones, sqb[:, 2 * (pcg % 2), :T], pcg == 0, False, ["cstb", ("sqbc", pcg % 2, 0)], ["ps3"])
                mm(ps[4][:, :T], ones, sqb[:, 2 * (pcg % 2) + 1, :T], pcg == 0, False, ["cstb", ("sqbc", pcg % 2, 1)], ["ps4"])
            if cg == 3:
                mm(ps[3][:, :T], ones, sqb[:, 2 * (cg % 2), :T], False, True, ["cstb", ("sqbc", cg % 2, 0)], ["ps3"])
                mm(ps[4][:, :T], ones, sqb[:, 2 * (cg % 2) + 1, :T], False, True, ["cstb", ("sqbc", cg % 2, 1)], ["ps4"])
        cp("pool", uT[:, :, 0:30], uT[:, :, T:T + 30], [("uT", c) for c in range(4)], ["uT"])
        act(st[0][:, :T], ps[3][:, :T], AF.Copy, ["ps3"], ["st0"], scale=1.0 / 512)
        tt("dve", st[1][:, :T], st[0][:, :T], st[0][:, :T], ALU.mult, ["st0"], ["st1"])
        stt("dve", st[1][:, :T], ps[4][:, :T], 1.0 / 512, st[1][:, :T], ALU.mult, ALU.subtract, ["ps4", "st1"], ["st1"])
        act(st[1][:, :T], st[1][:, :T], AF.Ln, ["st1"], ["st1"], bias=EPS)
        act(st[1][:, :T], st[1][:, :T], AF.Exp, ["st1"], ["st1"], scale=-0.5)
        P.stage = "conv_gate"
        wg, wgk = ring_get(2)
        for cg in range(4):
            pg, pgk = pj()
            proj(pg, pgk, wg, wgk, cg * 128, T)
            ysl = yT[:, cg * T:(cg + 1) * T]
            yk = ak(0, cg * T, cg * T + T)
            tt("dve", ysl, ysl, st[0][:, :T], ALU.subtract, yk + ["st0"], yk)
            tt("dve", ysl, ysl, st[1][:, :T], ALU.mult, yk + ["st1"], yk)
            act(ysl, ysl, AF.Silu, yk + ["prm"], yk, scale=lng[:, cg:cg + 1], bias=lnb[:, cg:cg + 1])
            sgl, sgk = A[3][:, cg * T:(cg + 1) * T], ak(3, cg * T, cg * T + T)
            act(sgl, pg[:, :T], AF.Silu, [pgk], sgk)
            tt("dve", brY[0][:, cg, :T], ysl, sgl, ALU.mult, yk + sgk, [("brY0", cg)])
        ring_done()
        P.stage = "hg_q"
        wq, wqk = ring_get(6)
        for hb in range(4):
            sl = slice(hb * T, (hb + 1) * T)
            psb, pk = pj()
            proj(psb, pk, wq, wqk, hb * 128, T)
            act(A[4][:, sl], A[2][:, sl], AF.Exp, ak(2, hb * T, hb * T + T), ak(4, hb * T, hb * T + T))
            act(A[0][:, sl], psb[:, :T], AF.Silu, [pk], ak(0, hb * T, hb * T + T))
            tt("dve", qA[:, hb, :T], A[0][:, sl], A[4][:, sl], ALU.mult, ak(0, hb * T, hb * T + T) + ak(4, hb * T, hb * T + T), [("qA", hb)])
        ring_done()
        P.stage = "hg_vk"
        wi, wik = ring_get(8)
        hg_vtok(wi, wik, nt, T)
        ring_done()
        hg_ktok(nt)
        wgt, wgtk = ring_get(9)
        for hb in range(4):
            psb, pk = pj()
            proj(psb, pk, wgt, wgtk, hb * 128, T)
            act(A[1][:, hb * T:(hb + 1) * T], psb[:, :T], AF.Silu, [pk], ak(1, hb * T, hb * T + T))
        ring_done()
        P.stage = "hg_core"
        for ti in range(nt):
            tsl = slice(ti * 128, (ti + 1) * 128)
            for hb in range(4):
                for half in range(2):
                    c0 = ti * 128 + half * 64
                    mm(ps[4][half * 64:half * 64 + 64, hb * 64:(hb + 1) * 64], kA[:, hb, c0:c0 + 64], qA[:, hb, c0:c0 + 64],
                       True, True, [("kA", hb), ("qA", hb)], ["ps4"])
            tt("dve", scm[:, ti, :, :], ps[4][:, 0:256].rearrange("p (h t) -> p h t", t=64),
               trim.unsqueeze(1).to_broadcast([128, 4, 64]), ALU.mult, ["ps4", "cstb"], [("scm", ti)])
            for half in range(2):
                cj = ti * 2 + half
                rows = slice(half * 64, half * 64 + 64)
                c0 = ti * 128 + half * 64
                for hb in range(4):
                    hs = slice(hb * 128, (hb + 1) * 128)
                    ts("dve", Sbf[:, hs], S[:, hs], E3[:, hb, 1, cj:cj + 1], None, ALU.mult, None, [("S", hb), ("E3", hb)], [("Sbf", hb)])
                for hb in range(4):
                    hs = slice(hb * 128, (hb + 1) * 128)
                    osl = ps[6][:, hb * 128 + half * 64: hb * 128 + half * 64 + 64]
                    mm(osl, vtok[rows, ti, hs], scm[rows, ti, hb, :], True, False, [("vtok", ti), ("scm", ti)], ["ps6"])
                    mm(osl, Sbf[:, hs], qA[:, hb, c0:c0 + 64], False, True, [("Sbf", hb), ("qA", hb)], ["ps6"])
                hg_state(ti, half)
            bff = bfs[:, :, :].rearrange("p a t -> p (a t)")[:, 0:512]
            a3k = ak(3, 0, 512)
            act(bff, ps[6][:, :], AF.Square, ["ps6"], ["bfs0", "bfs1"])
            mm(ps[3][:, :], ones, bff, True, True, ["cstb", "bfs0", "bfs1"], ["ps3"])
            act(A[3][:, 0:512], ps[3][:, :], AF.Ln, ["ps3"], a3k, scale=1.0 / 128, bias=EPS)
            act(A[3][:, 0:512], A[3][:, 0:512], AF.Exp, a3k, a3k, scale=-0.5)
            tt("dve", A[3][:, 0:512], ps[6][:, :], A[3][:, 0:512], ALU.mult, ["ps6"] + a3k, a3k)
            for hb in range(4):
                g0_, g1_ = hb * T + ti * 128, hb * T + (ti + 1) * 128
                stt("dve", brY[1][:, hb, tsl], A[3][:, hb * 128:(hb + 1) * 128], gng, A[1][:, g0_:g1_],
                    ALU.mult, ALU.mult, a3k + ["prm"] + ak(1, g0_, g1_), [("brY1", hb)])
        P.stage = "att_proj"
        wq2, wq2k = ring_get(13)
        for qb in range(4):
            psb, pk = pj()
            proj(psb, pk, wq2, wq2k, qb * 128, T)
            knorm(psb, pk, qn[:, qb, :T], [("qn", qb)], qg, T)
        ring_done()
        wc, wck = ring_get(14)
        wvc = wc[:, :].rearrange("p (k c) -> p k c", c=512)
        for kvh in range(2):
            psb, pk = pj()
            proj(psb, pk, wc, wck, kvh * 128, T)
            knorm(psb, pk, KT[:, kvh, 128:128 + T], [("KT", kvh)], kg, T)
        for ti in range(nt):
            psb, pk = pj()
            for kc in range(KC):
                mm(psb[:, :128], HN["hn"][:, kc, ti * 128:(ti + 1) * 128], wvc[:, kc, 256:384], kc == 0, kc == KC - 1, [wck, HN["hk"]], [pk])
            act(Vaug[:, 1 + ti, :, 0:64], psb[:, 0:128].rearrange("p (h d) -> p h d", d=64), AF.Copy, [pk], [("Vaug", 1 + ti)])
            if g0:
                ts("dve", Vaug[:, 1 + ti, :, 0:64], Vaug[:, 1 + ti, :, 0:64], vcol, None, ALU.mult, None, [("Vaug", 1 + ti), "cstf"], [("Vaug", 1 + ti)])
                cp("dve", Vaug[:, 1 + ti, :, 64:128], vcol.unsqueeze(1).to_broadcast([128, 2, 64]), [("Vaug", 1 + ti), "cstf"], [("Vaug", 1 + ti)])
            else:
                ms("dve", Vaug[:, 1 + ti, :, 64:128], 1.0, [("Vaug", 1 + ti)])
        ring_done()
        wgc, wgck = ring_get(15)
        for qb in range(4):
            psb, pk = pj()
            proj(psb, pk, wgc, wgck, qb * 128, T)
            act(A[1][:, qb * T:(qb + 1) * T], psb[:, :T], AF.Silu, [pk], ak(1, qb * T, qb * T + T))
        ring_done()
        P.stage = "att_core"
        for ti in range(nt):
            tsl = slice(ti * 128, (ti + 1) * 128)
            hist = slice(ti * 128, (ti + 1) * 128)
            cur = slice((ti + 1) * 128, (ti + 2) * 128)
            if g0:
                km, kmk, vm, vmk = KMz, "KMz", VMs, "VMs"
            elif vm_first and ti == 0:
                km, kmk, vm, vmk = KM, "KM", VM0, "VM0"
            else:
                km, kmk, vm, vmk = KM, "KM", VM, "VMall"
            for kvh in range(2):
                for g in range(4):
                    hq = kvh * 4 + g
                    qb = hq // 2
                    r = slice((hq % 2) * 64, (hq % 2) * 64 + 64)
                    mm(ps[5][:, g * 128:(g + 1) * 128], KT[r, kvh, hist], qn[r, qb, tsl], True, True, [("KT", kvh), "KThist", ("qn", qb)], ["ps5"])
                    mm(ps[6][:, g * 128:(g + 1) * 128], KT[r, kvh, cur], qn[r, qb, tsl], True, True, [("KT", kvh), ("qn", qb)], ["ps6"])
                    mm(ps[4][0:32, g * 128:(g + 1) * 128], km[r, kvh, :], qn[r, qb, tsl], True, True, [kmk, ("qn", qb)], ["ps4"])
                act(pA[:, :, :], ps[5][:, :].rearrange("p (g t) -> p g t", t=128), AF.Exp, ["ps5"], ["pA"], scale=0.125)
                act(pB[:, :, :], ps[6][:, :].rearrange("p (g t) -> p g t", t=128), AF.Exp, ["ps6"], ["pB"], scale=0.125)
                act(pM[:, :, :], ps[4][0:32, :].rearrange("p (g t) -> p g t", t=128), AF.Exp, ["ps4"], ["pM"], scale=0.125)
                for g in range(4):
                    hq = kvh * 4 + g
                    o0 = ps[3][:, g * 128:g * 128 + 64]
                    o1 = ps[3][:, g * 128 + 64:g * 128 + 128]
                    mm(o0, Vaug[:, ti, kvh, :], pA[:, g, 0:64], True, False, [("Vaug", ti), "pA"], ["ps3"])
                    mm(o0, Vaug[0:64, ti + 1, kvh, :], pB[0:64, g, 0:64], False, False, [("Vaug", ti + 1), "pB"], ["ps3"])
                    vmr = ["VM"] + [("VMd", q_) for q_ in range(8)] if vmk == "VMall" else [vmk]
                    mm(o0, vm[0:32, hq, :], pM[0:32, g, 0:64], False, True, vmr + ["pM"], ["ps3"])
                    mm(o1, Vaug[64:128, ti, kvh, :], pA[64:128, g, 64:128], True, False, [("Vaug", ti), "pA"], ["ps3"])
                    mm(o1, Vaug[:, ti + 1, kvh, :], pB[:, g, 64:128], False, False, [("Vaug", ti + 1), "pB"], ["ps3"])
                    mm(o1, vm[0:32, hq, :], pM[0:32, g, 64:128], False, True, vmr + ["pM"], ["ps3"])
                P.add("dve", lambda e: e.reciprocal(out=rden[64:128, :, :], in_=ps[3][64:128, :].rearrange("p (g t) -> p g t", t=128)),
                      ["ps3"], ["rden"])
                pv = ps[3][0:64, :].rearrange("p (g t) -> p g t", t=128)
                for par in range(2):
                    orow = slice(par * 64, par * 64 + 64)
                    qb0 = kvh * 2
                    tt("dve", ao[orow, :, :], pv[:, par::2, :], rden[64:128, par::2, :], ALU.mult, ["ps3", "rden"], [("ao", par)])
                    gk = ak(1, qb0 * T, (qb0 + 2) * T)
                    tt("dve", brY[2][orow, qb0:qb0 + 2, tsl], ao[orow, :, :],
                       A[1][orow, :].rearrange("p (q t) -> p q t", t=T)[:, qb0:qb0 + 2, tsl], ALU.mult,
                       [("ao", par)] + gk, [("brY2", qb0), ("brY2", qb0 + 1)])
        cp("pool", KT[:, :, 0:128], KT[:, :, T:T + 128], [("KT", 0), ("KT", 1)], ["KThist"])
        cp("pool", Vaug[:, 0, :, :], Vaug[:, nt, :, :], [("Vaug", nt)], [("Vaug", 0)])
        if gi + 1 < len(groups):
            P.stage = "norm"
            nt2 = groups[gi + 1]
            norm(gi + 1, nt2[0] * 128, len(nt2) * 128)
            normed.add(gi + 1)
        P.stage = "mix"
        wga = [ring_get(3), ring_get(4)]
        wco = ring_get(5)
        for br, (gch, och) in enumerate([((3, 4), 5), ((10, 11), 12), ((16, 17), 18)]):
            if br > 0:
                wga = [ring_get(gch[0]), ring_get(gch[1])]
                wco = ring_get(och)
            wov = wco[0][:, :].rearrange("p (k c) -> p k c", c=1024)
            for ob in range(8):
                pg, pgk = pj()
                wg_, wgk_ = wga[ob // 4]
                proj(pg, pgk, wg_, wgk_, (ob % 4) * 128, T)
                pz, pzk = pj()
                for kc4 in range(4):
                    mm(pz[:, :T], wov[:, kc4, ob * 128:(ob + 1) * 128], brY[br][:, kc4, :T], kc4 == 0, kc4 == 3,
                       [wco[1], ("brY%d" % br, kc4)], [pzk])
                gsl, gsk = A[0][:, (ob % 4) * T:(ob % 4) * T + T], ak(0, (ob % 4) * T, (ob % 4) * T + T)
                prl, prk = A[1][:, (ob % 4) * T:(ob % 4) * T + T], ak(1, (ob % 4) * T, (ob % 4) * T + T)
                act(gsl, pg[:, :T], AF.Sigmoid, [pgk], gsk)
                msl = A[2 + (ob % 2)][:, (ob // 2) * T:(ob // 2) * T + T]
                mk = ak(2 + ob % 2, (ob // 2) * T, (ob // 2) * T + T)
                if br == 0:
                    tt("dve", msl, pz[:, :T], gsl, ALU.mult, [pzk] + gsk, mk)
                elif br == 1:
                    tt("dve", prl, pz[:, :T], gsl, ALU.mult, [pzk] + gsk, prk)
                    tt("pool", msl, msl, prl, ALU.add, mk + prk, mk)
                else:
                    tt("dve", prl, pz[:, :T], gsl, ALU.mult, [pzk] + gsk, prk)
                    tt("pool", mixb[:, ob, :T], msl, prl, ALU.add, mk + prk, ["sqb"])
                if ob == 3:
                    ring_done()
            ring_done()
            ring_done()
        P.stage = "out"
        wo = [ring_get(19), ring_get(20)]
        for ob2 in range(8):
            psb, pk = pj()
            w_, wk_ = wo[ob2 // 4]
            wv = w_[:, :].rearrange("p (k c) -> p k c", c=512)
            for kc in range(KC):
                mm(psb[:, :T], wv[:, kc, (ob2 % 4) * 128:(ob2 % 4 + 1) * 128], mixb[:, kc, :T], kc == 0, kc == KC - 1, [wk_, "sqb"], [pk])
            tt("dve", hT[:, ob2, tok0:tok0 + T], hT[:, ob2, tok0:tok0 + T], psb[:, :T], ALU.add, [hk, pk], [hk])
            if ob2 == 3 or ob2 == 7:
                ring_done()

    group(0, groups[0], True, False)
    if DBG:
        for i_ in range(3):
            dd = dout("d_brY%d" % i_, [128, 4 * TML])
            P.dma("pool", dd[:, :], brY[i_][:].rearrange("p a t -> p (a t)"), reads=[("brY%d" % i_, q_) for q_ in range(4)])
        dd = dout("d_uT", [128, 4 * (30 + TML)])
        P.dma("pool", dd[:, :], uT[:].rearrange("p a t -> p (a t)"), reads=["uT"] + [("uT", q_) for q_ in range(4)])
        dd = dout("d_mixb", [128, KC * TML])
        P.dma("pool", dd[:, :], sqb[:].rearrange("p a t -> p (a t)"), reads=["sqb"])
    f0 = cmask[:, 8:9]
    omf0 = cmask[:, 9:10]
    cp("dve", KM[:, :, :], KT[:, :, 96:128], ["KThist"], ["KM"])
    ms("dve", KM[:, :, 0:1], 0.0, ["KM"])
    cp("dve", VM[:, :, :], VMs[:, :, :], ["VMs"], ["VM"])
    for hq in range(8):
        P.dma("sp", VM[16:32, hq, :], Vaug[112:128, 0, hq // 4, :], reads=[("Vaug", 0), "VM"], writes=[("VMd", hq)])
    ms("dve", lbt[0:32, 56:57], 1.0, ["lbt2"])
    P.dma("sp", lbt[16:32, 56:57], cmask[16:32, 9:10], reads=["cmask", "lbt2"], writes=["lbt3"])
    ts("dve", VM0[:, :, :], VM[:, :, :], lbt[0:32, 56:57], None, ALU.mult, None, ["VM", "lbt2", "lbt3"] + [("VMd", q_) for q_ in range(8)], ["VM0"])
    EXK = ak(4, 0, 512)
    SKEYS = [("S", q_) for q_ in range(4)]
    P.dma("sp", exs[:, 0:120], uex_d.rearrange("p c t -> p (c t)"), writes=EXK)
    ts("dve", uT[:, :, 0:30], uT[:, :, 0:30], f0, None, ALU.mult, None, ["uT", "cmask"], ["uT"])
    stt("dve", uT[:, :, 0:30], exs[:, 0:120].rearrange("p (c t) -> p c t", t=30), omf0, uT[:, :, 0:30], ALU.mult, ALU.add,
        EXK + ["cmask", "uT"], ["uT"])
    P.dma("sp", exs[:, 128:384], kex_d.rearrange("p c t -> p (c t)"), writes=EXK)
    ts("dve", KT[:, :, 0:128], KT[:, :, 0:128], f0, None, ALU.mult, None, ["KThist", "cmask"], ["KThist"])
    stt("dve", KT[:, :, 0:128], exs[:, 128:384].rearrange("p (c t) -> p c t", t=128), omf0, KT[:, :, 0:128], ALU.mult, ALU.add,
        EXK + ["cmask", "KThist"], ["KThist"])
    P.dma("sp", exs[:, 384:512], vex_d[:, :], writes=EXK)
    ts("dve", Vaug[:, 0, :, 0:64], Vaug[:, 0, :, 0:64], f0, None, ALU.mult, None, [("Vaug", 0), "cmask"], [("Vaug", 0)])
    stt("dve", Vaug[:, 0, :, 0:64], exs[:, 384:512].rearrange("p (h d) -> p h d", d=64), omf0, Vaug[:, 0, :, 0:64], ALU.mult, ALU.add,
        EXK + ["cmask", ("Vaug", 0)], [("Vaug", 0)])
    ts("dve", Vaug[:, 0, :, 64:128], Vaug[:, 0, :, 64:128], f0, omf0, ALU.mult, ALU.add, [("Vaug", 0), "cmask"], [("Vaug", 0)])
    P.dma("sp", Fall[:, :, :], fall_d.rearrange("c p h -> p c h"), writes=["Fall"])
    for j in range(NCORES - 1):
        mj = cmask[:, j:j + 1]
        P.dma("sp", exs[:, :], sall_d[j], writes=EXK)
        ts("dve", lbt[:, 60:64], Fall[:, j, :], -1.0, None, ALU.add, None, ["Fall"], ["lbt4"])
        ts("dve", lbt[:, 60:64], lbt[:, 60:64], mj, onecol, ALU.mult, ALU.add, ["lbt4", "cmask", "cstf"], ["lbt4"])
        tt("dve", S[:, :].rearrange("p (h v) -> p h v", v=128), S[:, :].rearrange("p (h v) -> p h v", v=128),
           lbt[:, 60:64].unsqueeze(2).to_broadcast([128, 4, 128]), ALU.mult, SKEYS + ["lbt4"], SKEYS)
        stt("dve", S[:, :], exs[:, :], mj, S[:, :], ALU.mult, ALU.add, EXK + ["cmask"] + SKEYS, SKEYS)
    for gi in range(1, len(groups)):
        group(gi, groups[gi], False, gi == 1)
    for kc in range(KC):
        P.dma("sp", h_out[:, kc, :], hT[:, kc, :], reads=[("hT", g) for g in range(len(groups))])
    P.finish()
    return nc, P


def _chunk1024(W):
    return np.ascontiguousarray(W.reshape(8, 128, 512).transpose(1, 0, 2).reshape(128, CW))


def _chunk512(W):
    return np.ascontiguousarray(W.reshape(4, 128, 1024).transpose(1, 0, 2).reshape(128, CW))


def _layer_chunks(inp, l):
    w = inp["w_in"][l]
    ch = np.zeros((NCH, 128, CW), np.float32)
    def c(i, a, b):
        ch[i] = _chunk1024(w[:, a:b])
    c(0, 0, 512); c(1, 512, 1024); c(2, 1024, 1536)
    c(3, 4864, 5376); c(4, 5376, 5888)
    ch[5] = _chunk512(inp["w_conv_out"][l])
    c(6, 1536, 2048); c(7, 2048, 2560); c(8, 2560, 3072); c(9, 3072, 3584)
    c(10, 5888, 6400); c(11, 6400, 6912)
    ch[12] = _chunk512(inp["w_hg_out"][l])
    c(13, 3584, 4096)
    ckv = np.zeros((1024, 512), np.float32)
    ckv[:, 0:64] = w[:, 4096:4160]; ckv[:, 64:128] = w[:, 4096:4160]
    ckv[:, 128:192] = w[:, 4160:4224]; ckv[:, 192:256] = w[:, 4160:4224]
    ckv[:, 256:384] = w[:, 4224:4352]
    ch[14] = _chunk1024(ckv)
    c(15, 4352, 4864); c(16, 6912, 7424); c(17, 7424, 7936)
    ch[18] = _chunk512(inp["w_att_out"][l])
    ch[19] = _chunk1024(inp["w_out"][l][:, 0:512]); ch[20] = _chunk1024(inp["w_out"][l][:, 512:1024])
    return ch


def _layer_prm(inp, l):
    p = np.zeros((128, NPRM), np.float32)
    p[:, P_NG:P_NG + 8] = inp["norm_g"][l].reshape(8, 128).T
    hlb = inp["hg_lower_bounds"]
    p[:, P_LBRAW:P_LBRAW + 16] = hlb.reshape(4, 4, 128).transpose(2, 1, 0).reshape(128, 16)
    p[:, P_LSEL + l] = 1.0
    p[:, P_KG] = np.tile(inp["k_norm_g"][l], 2)
    p[:, P_QG] = np.tile(inp["q_norm_g"][l], 2)
    p[:, P_GNG] = inp["hg_norm_g"][l]
    p[:, P_CB:P_CB + 4] = inp["conv_b"][l].reshape(4, 128).T
    p[:, P_LNG:P_LNG + 4] = inp["conv_ln_g"][l].reshape(4, 128).T
    p[:, P_LNB:P_LNB + 4] = inp["conv_ln_b"][l].reshape(4, 128).T
    p[:, P_CW:P_CW + 124] = inp["conv_w"][l].reshape(31, 4, 128).transpose(2, 1, 0).reshape(128, 124)
    return p


def _consts():
    c = np.zeros((128, NCST), np.float32)
    c[:, C_ID:C_ID + 128] = np.eye(128)
    c[:, C_ONES:C_ONES + 128] = 1.0
    c[0:64, C_BONES:C_BONES + 64] = 1.0
    c[64:128, C_BONES + 64:C_BONES + 128] = 1.0
    s = np.arange(128)[:, None] % 64
    t = np.arange(64)[None, :]
    c[:, C_TRI:C_TRI + 64] = (s <= t)
    c[:, C_VROW + 112:C_VROW + 128] = 1.0
    c[112:128, C_VCOL] = 1.0
    return c


_CACHE = {}


def _prog(phase):
    if phase not in _CACHE:
        _CACHE[phase] = build(phase)[0]
    return _CACHE[phase]


def kernel(x, meta_tokens, norm_g, w_in, conv_w, conv_b, conv_ln_g, conv_ln_b, w_conv_out,
           hg_lower_bounds, hg_norm_g, w_hg_out, q_norm_g, k_norm_g, attn_sinks, w_att_out, w_out):
    inp = dict(x=x, meta_tokens=meta_tokens, norm_g=norm_g, w_in=w_in, conv_w=conv_w, conv_b=conv_b,
               conv_ln_g=conv_ln_g, conv_ln_b=conv_ln_b, w_conv_out=w_conv_out, hg_lower_bounds=hg_lower_bounds,
               hg_norm_g=hg_norm_g, w_hg_out=w_hg_out, q_norm_g=q_norm_g, k_norm_g=k_norm_g, attn_sinks=attn_sinks,
               w_att_out=w_att_out, w_out=w_out)
    inp = {k: np.asarray(v, np.float32) for k, v in inp.items()}
    xs = inp["x"][0]
    cst = _consts()
    hTs = []
    for c in range(NCORES):
        h = np.zeros((TOK, D), np.float32)
        h[112:128] = inp["meta_tokens"]
        h[128:] = xs[c * OWN:(c + 1) * OWN]
        hTs.append(np.ascontiguousarray(h.T.reshape(KC, 128, TOK).transpose(1, 0, 2)))
    cmasks = []
    for c in range(NCORES):
        m = np.zeros((128, 16), np.float32)
        m[:, 0:8] = (np.arange(8) < c)[None, :]
        m[:, 8] = 1.0 if c == 0 else 0.0
        m[:, 9] = 0.0 if c == 0 else 1.0
        cmasks.append(m)
    ncA = _prog("A")
    ncB = _prog("B")
    cores = list(range(NCORES))
    for l in range(DEPTH):
        ch = _layer_chunks(inp, l)
        prm = _layer_prm(inp, l)
        chA = np.ascontiguousarray(ch[[7, 8, 0, 1, 14]])
        inA = [dict(hT=hTs[c], wch=chA, prm=prm, cst=cst, cmask=cmasks[c]) for c in cores]
        rA = run_bass_kernel_spmd(ncA, inA, core_ids=cores).results
        Sall = np.ascontiguousarray(np.stack([rA[c]["Sloc"] for c in cores]))
        Fall = np.ascontiguousarray(np.stack([rA[c]["Floc"] for c in cores]))
        sk = np.zeros((32, 8), np.float32)
        sk[0] = inp["attn_sinks"][l]
        inB = []
        for c in cores:
            p = max(c - 1, 0)
            inB.append(dict(hT=hTs[c], wch=ch, prm=prm, cst=cst, cmask=cmasks[c], sinks=sk, Sall=Sall, Fall=Fall,
                            uex=rA[p]["utail"], kex=rA[p]["ktail"], vex=rA[p]["vtail"]))
        rB = run_bass_kernel_spmd(ncB, inB, core_ids=cores).results
        hTs = [rB[c]["hT_out"] for c in cores]
    out = np.zeros((1, SEQ, D), np.float32)
    for c in cores:
        hc = hTs[c].transpose(1, 0, 2).reshape(D, TOK).T
        out[0, c * OWN:(c + 1) * OWN] = hc[128:]
    return out
```
